# Optimizing a Trainium2 kernel written in Bass

```python
import jax, jax.numpy as jnp
from jax import lax
import numpy as np

D_MODEL = 1024
BATCH = 4
SEQ = 4096
DEPTH = 2
DEC_BATCH = 32
DEC_SEQ = 1
PAST_LEN = 16384
PAGE_SIZE = 128

N_EVEN = (DEPTH + 1) // 2
N_ODD = DEPTH // 2
HEAD_DIM = 64
D_ATTN = D_MODEL // 2
N_HEADS_A = D_ATTN // HEAD_DIM
D_CONV = D_MODEL - D_ATTN
CONV_WIDTH = 31
D_IN_EVEN = 3 * D_ATTN + N_HEADS_A + 2 * D_CONV
D_SGU = D_MODEL
N_SGU_GROUPS = 8
SGU_GROUP = D_SGU // N_SGU_GROUPS
CHUNK = 128
D_FF = -(-8 * D_MODEL // (3 * 256)) * 256
BLOCK_Q = 128
RMS_EPS = 1e-6
LN_EPS = 1e-5
FORGET_BIAS_INIT = 4.0

kernel_name = "fox_conformer_gmlp_hybrid_step"


def rmsnorm(x, g):
    xf = x.astype(jnp.float32)
    y = xf * lax.rsqrt(jnp.mean(xf * xf, axis=-1, keepdims=True) + RMS_EPS)
    return (y * g.astype(jnp.float32)).astype(x.dtype)


def layernorm(x, g, b):
    xf = x.astype(jnp.float32)
    mu = jnp.mean(xf, axis=-1, keepdims=True)
    var = jnp.mean(jnp.square(xf - mu), axis=-1, keepdims=True)
    y = (xf - mu) * lax.rsqrt(var + LN_EPS) * g.astype(jnp.float32) + b.astype(jnp.float32)
    return y.astype(x.dtype)


def swiglu_ffn(h, w_gate, w_up, w_down):
    return (jax.nn.silu(h @ w_gate) * (h @ w_up)) @ w_down


def even_project(h, w_in, b_f):
    b, t, _ = h.shape
    p = h @ w_in
    q = p[..., :D_ATTN].reshape(b, t, N_HEADS_A, HEAD_DIM)
    k = p[..., D_ATTN:2 * D_ATTN].reshape(b, t, N_HEADS_A, HEAD_DIM)
    v = p[..., 2 * D_ATTN:3 * D_ATTN].reshape(b, t, N_HEADS_A, HEAD_DIM)
    o = 3 * D_ATTN
    logf = jax.nn.log_sigmoid(p[..., o:o + N_HEADS_A].astype(jnp.float32) + b_f.astype(jnp.float32))
    o = o + N_HEADS_A
    glu = p[..., o:o + D_CONV] * jax.nn.sigmoid(p[..., o + D_CONV:])
    return q, k, v, logf, glu


def fox_logits(q, k, c_q, c_k):
    s = jnp.einsum('bqhd,bkhd->bhqk', q, k, preferred_element_type=jnp.float32) * (HEAD_DIM ** -0.5)
    decay = jnp.swapaxes(c_q, 1, 2)[..., :, None] - jnp.swapaxes(c_k, 1, 2)[..., None, :]
    return s + decay


def fox_prompt(q, k, v, logf):
    c = jnp.cumsum(logf, axis=1)
    pos = jnp.arange(q.shape[1])
    outs = []
    for s0 in range(0, q.shape[1], BLOCK_Q):
        e = s0 + BLOCK_Q
        s = fox_logits(q[:, s0:e], k[:, :e], c[:, s0:e], c[:, :e])
        s = jnp.where(pos[:e][None, :] <= pos[s0:e][:, None], s, -jnp.inf)
        p = jax.nn.softmax(s, axis=-1).astype(v.dtype)
        outs.append(jnp.einsum('bhqk,bkhd->bqhd', p, v[:, :e]))
    return jnp.concatenate(outs, axis=1)


def fox_sample(q, k_new, v_new, logf_new, k_past, v_past, logf_past):
    c_past = jnp.cumsum(logf_past, axis=1)
    c_new = c_past[:, -1:] + jnp.cumsum(logf_new, axis=1)
    s_past = fox_logits(q, k_past, c_new, c_past)
    t = q.shape[1]
    causal = jnp.tril(jnp.ones((t, t), dtype=bool))
    s_new = jnp.where(causal, fox_logits(q, k_new, c_new, c_new), -jnp.inf)
    p = jax.nn.softmax(jnp.concatenate([s_past, s_new], axis=-1), axis=-1).astype(v_new.dtype)
    n_past = k_past.shape[1]
    return (jnp.einsum('bhqk,bkhd->bqhd', p[..., :n_past], v_past)
            + jnp.einsum('bhqk,bkhd->bqhd', p[..., n_past:], v_new))


def conformer_conv(glu_ext, conv_w, conv_b, ln_g, ln_b):
    y = lax.conv_general_dilated(glu_ext, conv_w[:, None, :].astype(glu_ext.dtype), window_strides=(1,),
                                 padding='VALID', dimension_numbers=('NWC', 'WIO', 'NWC'),
                                 feature_group_count=D_CONV)
    return jax.nn.silu(layernorm(y + conv_b, ln_g, ln_b))


def chunk_sgu_mixer(h, w_in, ln_g, ln_b, w_s, b_s, w_out):
    z = jax.nn.gelu(h @ w_in, approximate=False)
    u, v = z[..., :D_SGU], z[..., D_SGU:]
    vn = layernorm(v, ln_g, ln_b)
    b, t, _ = vn.shape
    pad = (-t) % CHUNK
    vp = jnp.pad(vn, ((0, 0), (0, pad), (0, 0))).reshape(b, (t + pad) // CHUNK, CHUNK, N_SGU_GROUPS, SGU_GROUP)
    mask = jnp.tril(jnp.ones((CHUNK, CHUNK), dtype=bool))
    w_c = jnp.where(mask, w_s, 0).astype(vp.dtype)
    mixed = jnp.einsum('gts,bnsgc->bntgc', w_c, vp) + jnp.swapaxes(b_s, 0, 1)[:, :, None]
    mixed = mixed.reshape(b, t + pad, D_SGU)[:, :t]
    return (u * mixed) @ w_out, vn


def setup_inputs(seed: int = 0) -> dict:
    key = jax.random.key(seed)
    ks = jax.random.split(key, 32)
    nrm = jax.random.normal
    n_pages = PAST_LEN // PAGE_SIZE
    n_phys = (5 * DEC_BATCH * n_pages + 3) // 4
    page_table = jax.random.permutation(ks[0], n_phys)[:DEC_BATCH * n_pages].reshape(DEC_BATCH, n_pages).astype(jnp.int32)
    row_scale = lax.rsqrt(jnp.arange(1, CHUNK + 1, dtype=jnp.float32))[:, None]
    return {
        'x_prompt': nrm(ks[1], (BATCH, SEQ, D_MODEL), jnp.float32),
        'x_sample': nrm(ks[2], (DEC_BATCH, DEC_SEQ, D_MODEL), jnp.float32),
        'cache_k': nrm(ks[3], (N_EVEN, n_phys, PAGE_SIZE, N_HEADS_A, HEAD_DIM), jnp.float32),
        'cache_v': nrm(ks[4], (N_EVEN, n_phys, PAGE_SIZE, N_HEADS_A, HEAD_DIM), jnp.float32),
        'cache_logf': jax.nn.log_sigmoid(FORGET_BIAS_INIT + nrm(ks[5], (N_EVEN, n_phys, PAGE_SIZE, N_HEADS_A), jnp.float32)),
        'page_table': page_table,
        'state_conv': 0.5 * nrm(ks[6], (N_EVEN, DEC_BATCH, CONV_WIDTH - 1, D_CONV), jnp.float32),
        'norm_mix': 1.0 + 0.02 * nrm(ks[7], (DEPTH, D_MODEL), jnp.float32),
        'norm_ffn': 1.0 + 0.02 * nrm(ks[8], (DEPTH, D_MODEL), jnp.float32),
        'norm_final': 1.0 + 0.02 * nrm(ks[9], (D_MODEL,), jnp.float32),
        'w_in_even': nrm(ks[10], (N_EVEN, D_MODEL, D_IN_EVEN), jnp.float32) * D_MODEL ** -0.5,
        'b_forget': FORGET_BIAS_INIT + 0.1 * nrm(ks[11], (N_EVEN, N_HEADS_A), jnp.float32),
        'conv_w': nrm(ks[12], (N_EVEN, CONV_WIDTH, D_CONV), jnp.float32) * CONV_WIDTH ** -0.5,
        'conv_b': 0.02 * nrm(ks[13], (N_EVEN, D_CONV), jnp.float32),
        'conv_ln_g': 1.0 + 0.02 * nrm(ks[14], (N_EVEN, D_CONV), jnp.float32),
        'conv_ln_b': 0.02 * nrm(ks[15], (N_EVEN, D_CONV), jnp.float32),
        'w_out_even': nrm(ks[16], (N_EVEN, D_ATTN + D_CONV, D_MODEL), jnp.float32) * (D_ATTN + D_CONV) ** -0.5,
        'w_in_odd': nrm(ks[17], (N_ODD, D_MODEL, 2 * D_SGU), jnp.float32) * D_MODEL ** -0.5,
        'sgu_ln_g': 1.0 + 0.02 * nrm(ks[18], (N_ODD, D_SGU), jnp.float32),
        'sgu_ln_b': 0.02 * nrm(ks[19], (N_ODD, D_SGU), jnp.float32),
        'sgu_w': nrm(ks[20], (N_ODD, N_SGU_GROUPS, CHUNK, CHUNK), jnp.float32) * row_scale,
        'sgu_b': 1.0 + 0.02 * nrm(ks[21], (N_ODD, N_SGU_GROUPS, CHUNK), jnp.float32),
        'w_out_odd': nrm(ks[22], (N_ODD, D_SGU, D_MODEL), jnp.float32) * D_SGU ** -0.5,
        'w_gate': nrm(ks[23], (DEPTH, D_MODEL, D_FF), jnp.float32) * D_MODEL ** -0.5,
        'w_up': nrm(ks[24], (DEPTH, D_MODEL, D_FF), jnp.float32) * D_MODEL ** -0.5,
        'w_down': nrm(ks[25], (DEPTH, D_FF, D_MODEL), jnp.float32) * D_FF ** -0.5,
    }


def reference(x_prompt, x_sample, cache_k, cache_v, cache_logf, page_table, state_conv,
              norm_mix, norm_ffn, norm_final, w_in_even, b_forget, conv_w, conv_b, conv_ln_g, conv_ln_b,
              w_out_even, w_in_odd, sgu_ln_g, sgu_ln_b, sgu_w, sgu_b, w_out_odd, w_gate, w_up, w_down):
    xp, xs = x_prompt, x_sample
    db = x_sample.shape[0]
    n_past = page_table.shape[1] * PAGE_SIZE
    kp_l, vp_l, fp_l, cp_l = [], [], [], []
    ks_l, vs_l, fs_l, cs_l = [], [], [], []
    sgu_l = []
    for l in range(DEPTH):
        hp = rmsnorm(xp, norm_mix[l])
        hs = rmsnorm(xs, norm_mix[l])
        if l % 2 == 0:
            i = l // 2
            q, k, v, logf, glu = even_project(hp, w_in_even[i], b_forget[i])
            attn = fox_prompt(q, k, v, logf).reshape(xp.shape[0], xp.shape[1], D_ATTN)
            conv = conformer_conv(jnp.pad(glu, ((0, 0), (CONV_WIDTH - 1, 0), (0, 0))),
                                  conv_w[i], conv_b[i], conv_ln_g[i], conv_ln_b[i])
            mp = jnp.concatenate([attn, conv], axis=-1) @ w_out_even[i]
            kp_l.append(k); vp_l.append(v); fp_l.append(logf); cp_l.append(glu[:, -(CONV_WIDTH - 1):])
            q2, k2, v2, logf2, glu2 = even_project(hs, w_in_even[i], b_forget[i])
            k_past = cache_k[i, page_table].reshape(db, n_past, N_HEADS_A, HEAD_DIM)
            v_past = cache_v[i, page_table].reshape(db, n_past, N_HEADS_A, HEAD_DIM)
            f_past = cache_logf[i, page_table].reshape(db, n_past, N_HEADS_A).astype(jnp.float32)
            attn2 = fox_sample(q2, k2, v2, logf2, k_past, v_past, f_past).reshape(db, xs.shape[1], D_ATTN)
            glu_ext = jnp.concatenate([state_conv[i].astype(glu2.dtype), glu2], axis=1)
            conv2 = conformer_conv(glu_ext, conv_w[i], conv_b[i], conv_ln_g[i], conv_ln_b[i])
            ms = jnp.concatenate([attn2, conv2], axis=-1) @ w_out_even[i]
            ks_l.append(k2); vs_l.append(v2); fs_l.append(logf2); cs_l.append(glu_ext[:, -(CONV_WIDTH - 1):])
        else:
            j = l // 2
            mp, _ = chunk_sgu_mixer(hp, w_in_odd[j], sgu_ln_g[j], sgu_ln_b[j], sgu_w[j], sgu_b[j], w_out_odd[j])
            ms, vn_s = chunk_sgu_mixer(hs, w_in_odd[j], sgu_ln_g[j], sgu_ln_b[j], sgu_w[j], sgu_b[j], w_out_odd[j])
            sgu_l.append(vn_s)
        xp = xp + mp
        xs = xs + ms
        xp = xp + swiglu_ffn(rmsnorm(xp, norm_ffn[l]), w_gate[l], w_up[l], w_down[l])
        xs = xs + swiglu_ffn(rmsnorm(xs, norm_ffn[l]), w_gate[l], w_up[l], w_down[l])
    y_prompt = rmsnorm(xp, norm_final)
    y_sample = rmsnorm(xs, norm_final)
    return (y_prompt, y_sample,
            jnp.stack(kp_l), jnp.stack(vp_l), jnp.stack(fp_l), jnp.stack(cp_l),
            jnp.stack(ks_l), jnp.stack(vs_l), jnp.stack(fs_l), jnp.stack(cs_l),
            jnp.stack(sgu_l))
```

```python
import contextlib
import numpy as np
import concourse.bass as bass
import concourse.mybir as mybir
from concourse.bass_utils import run_bass_kernel_spmd

F32 = mybir.dt.float32
BF16 = mybir.dt.bfloat16
I32 = mybir.dt.int32
AF = mybir.ActivationFunctionType
ALU = mybir.AluOpType

ENGS = ("pe", "act", "dve", "pool", "sp")
NDMA = 48

D = 1024
NOWN = 2048
NB = 16
DIN = 2568
DFF = 2816
QCH = [(0, 6), (6, 12), (12, 17), (17, 22)]
RMS_EPS = 1e-6
LN_EPS = 1e-5
NEG = -30000.0


class Buf:
    __slots__ = ("name", "w", "r")

    def __init__(self, name):
        self.name = name
        self.w = None
        self.r = []


class Op:
    __slots__ = ("eng", "fn", "deps", "dma", "signal", "count", "waits")

    def __init__(self, eng, fn, dma=None):
        self.eng = eng
        self.fn = fn
        self.deps = set()
        self.dma = dma
        self.signal = False
        self.count = 0
        self.waits = []


class Sched:
    def __init__(self, nc):
        self.nc = nc
        self.ops = {e: [] for e in ENGS}
        self.dma_uses = [0] * NDMA
        self.dma_next = 0
        self.pending = {e: set() for e in ENGS}

    def barrier(self):
        deps = set()
        for f in ENGS:
            if self.ops[f]:
                k = len(self.ops[f]) - 1
                while k >= 0 and self.ops[f][k].dma is not None:
                    k -= 1
                if k >= 0:
                    deps.add((f, k))
        for s in range(NDMA):
            if self.dma_uses[s] > 0:
                deps.add(("dma", s, self.dma_uses[s]))
        for e in ENGS:
            self.pending[e] |= deps

    def _track(self, key, op, R, W):
        for b in R:
            if b.w is not None:
                op.deps.add(b.w)
        for b in W:
            if b.w is not None:
                op.deps.add(b.w)
            for k in b.r:
                op.deps.add(k)
        for b in R:
            b.r.append(key)
        for b in W:
            b.w = key
            b.r = []

    def op(self, eng, fn, R=(), W=()):
        o = Op(eng, fn)
        key = (eng, len(self.ops[eng]))
        self._track(key, o, R, W)
        o.deps |= self.pending[eng]
        self.pending[eng] = set()
        o.deps.discard(key)
        self.ops[eng].append(o)
        return o

    def dma(self, eng, fn, R=(), W=()):
        slot = self.dma_next
        self.dma_next = (self.dma_next + 1) % NDMA
        self.dma_uses[slot] += 1
        use = self.dma_uses[slot]
        o = Op(eng, fn, dma=(slot, use))
        key = ("dma", slot, use)
        self._track(key, o, R, W)
        o.deps |= self.pending[eng]
        self.pending[eng] = set()
        o.deps.discard(key)
        if use > 1:
            o.deps.add(("dma", slot, use - 1))
        self.ops[eng].append(o)
        return o

    def emit(self, sems, dsems, block):
        for e in ENGS:
            for o in self.ops[e]:
                for d in o.deps:
                    if d[0] != "dma":
                        if d[0] == "pe" and e == "pe":
                            continue
                        self.ops[d[0]][d[1]].signal = True
        for e in ENGS:
            c = 0
            for o in self.ops[e]:
                if o.signal and o.dma is None:
                    c += 1
                o.count = c
        for e in ENGS:
            known = {f: -1 for f in ENGS}
            kd = {}
            for o in self.ops[e]:
                need = {}
                for d in o.deps:
                    if d[0] == "dma":
                        _, slot, use = d
                        if kd.get(slot, 0) < use:
                            kd[slot] = use
                            o.waits.append((dsems[slot], 16 * use))
                    else:
                        f, k = d
                        if f == "pe" and e == "pe":
                            continue
                        if k > known[f]:
                            need[f] = max(need.get(f, -1), k)
                for f, k in need.items():
                    known[f] = k
                    o.waits.append((sems[f], self.ops[f][k].count))
        final = [(dsems[s], 16 * self.dma_uses[s]) for s in range(NDMA) if self.dma_uses[s] > 0]

        def run(e, handle):
            for o in self.ops[e]:
                for (s, v) in o.waits:
                    handle.wait_ge(s, v)
                ins = o.fn(handle)
                if o.dma is not None:
                    ins.then_inc(dsems[o.dma[0]], 16)
                elif o.signal:
                    ins.then_inc(sems[e], 1)
            if e == "sp":
                for (s, v) in final:
                    handle.wait_ge(s, v)

        @block.tensor
        def _(eng):
            run("pe", eng)

        @block.scalar
        def _(eng):
            run("act", eng)

        @block.vector
        def _(eng):
            run("dve", eng)

        @block.gpsimd
        def _(eng):
            run("pool", eng)

        @block.sync
        def _(eng):
            run("sp", eng)


class Tl:
    __slots__ = ("ap", "b")

    def __init__(self, ap, b):
        self.ap = ap
        self.b = b


def _dsize(dt):
    return 2 if dt == BF16 else 4


def build(kind="main", nphase=4):
    nc = bass.Bass("TRN2", target_bir_lowering=False)

    def din(name, shape, dt=F32):
        return nc.dram_tensor(name, list(shape), dt, kind="ExternalInput").ap()

    def dout(name, shape, dt=F32):
        return nc.dram_tensor(name, list(shape), dt, kind="ExternalOutput").ap()

    MAIN = (kind == "main")
    gains_d = din("gains", [128, 4, 8])
    xs_d = din("xs", [128, D])
    if MAIN:
        xc_d = din("xc", [NOWN, D])
        xo_d = din("xo", [NOWN, D])
        ctxb_d = din("ctxb", [128, 1])
        gfin_d = din("gfin", [128, D])
        wie_d = din("w_in_even", [D, DIN])
        bf_d = din("b_forget", [128, 8])
        cw_d = din("conv_w", [128, 4, 31])
        cv_d = din("conv_vec", [128, 3, 4])
        woe_d = din("w_out_even", [D, D])
        wio_d = din("w_in_odd", [D, 2 * D])
        sln_d = din("sgu_ln", [128, 2, D])
        sw_d = din("sgu_w", [8, 128, 128])
        sb_d = din("sgu_b", [128, 8, 128])
        sw0_d = din("sgu_w0", [128, 8])
        sb0_d = din("sgu_b0", [128, 8])
        woo_d = din("w_out_odd", [D, D])
        wg_d = din("w_gate", [2, D, DFF])
        wu_d = din("w_up", [2, D, DFF])
        wd_d = din("w_down", [2, DFF, D])
        st_d = din("state_c", [8, 30 * 512])
        cwb_d = din("conv_wb", [8, 31, 512])
        cvb_d = din("conv_vb", [8, 3, 512])
        attn2_d = din("attn2", [8, 512])
        css_d = dout("css_out", [8, 30, 512])
        y_d = dout("y", [NOWN + 128, D])
        ko_d = dout("k_out", [NOWN + 128, 512])
        vo_d = dout("v_out", [NOWN + 128, 512])
        lf_d = dout("lf_out", [NOWN + 128, 8])
        cs_d = dout("cs_out", [32, 512])
        sv_d = dout("sv_out", [8, D])
        xres_d = nc.dram_tensor("xres", [NOWN + 128, D], F32, kind="Internal").ap()
    else:
        wsm_d = din("w_samp", [D, 196])
        bfc_d = din("bf_c", [128, 1])
        ptT_d = din("ptT", [128, 32], I32)
        kc_d = [din("kc0", [5120, 8192])]
        vc_d = [din("vc0", [5120, 8192])]
        lc_d = [din("lc0", [5120, 128])]
        oa_d = dout("o_attn", [32, 64])

    S = Sched(nc)
    with contextlib.ExitStack() as st:
        AW = 53100
        arena = st.enter_context(nc.sbuf_tensor("arena", [128, AW], F32))
        top = [0]
        limit = [AW]

        def alloc(name, shape, dt=F32, nb=None):
            n = int(np.prod(shape[1:]))
            words = (n * _dsize(dt) + 3) // 4
            off = top[0]
            top[0] += words
            assert top[0] <= limit[0], (name, top[0], limit[0])
            ap = arena[0:shape[0], off:off + words]
            if dt != F32:
                ap = ap.bitcast(dt)
            if n * _dsize(dt) != words * 4:
                ap = ap[:, 0:n]
            if len(shape) == 3:
                ap = ap.rearrange("p (a b) -> p a b", a=shape[1])
            elif len(shape) == 4:
                ap = ap.rearrange("p (a b c) -> p a b c", a=shape[1], b=shape[2])
            return Tl(ap, Buf(name))

        banks = []
        for i in range(8):
            t = st.enter_context(nc.psum_tensor(f"bank{i}", [128, 512], F32))
            banks.append(Tl(t[:], Buf(f"bank{i}")))
        sems = {e: st.enter_context(nc.semaphore("s_" + e)) for e in ENGS}
        dsems = [st.enter_context(nc.semaphore(f"d{i}")) for i in range(NDMA)]
        block = st.enter_context(nc.Block())

        rr = [0]

        def nbank(lo=0, hi=6):
            b = banks[lo + rr[0] % (hi - lo)]
            rr[0] += 1
            return b

        ident_f = alloc("ident_f", [128, 128])
        ident_b = alloc("ident_b", [128, 128], BF16)
        tri_f = alloc("tri_f", [128, 128])
        tri_b = alloc("tri_b", [128, 128], BF16)
        ones_f = alloc("ones_f", [128, 128])
        o512_f = alloc("o512_f", [128, 128])
        gains = alloc("gains", [128, 4, 8])
        ctxb = alloc("ctxb", [128, 1])
        bfb = alloc("bfb", [128, 8])
        epsr = alloc("epsr", [128, 1])
        epsl = alloc("epsl", [128, 1])
        one1 = alloc("one1", [128, 1])

        S.op("pool", lambda e: e.memset(ident_f.ap, 0.0), W=[ident_f.b])
        S.op("pool", lambda e: e.affine_select(out=ident_f.ap, in_=ident_f.ap, pattern=[[-1, 128]], compare_op=ALU.not_equal,
                                               fill=1.0, base=0, channel_multiplier=1), R=[ident_f.b], W=[ident_f.b])
        S.op("pool", lambda e: e.tensor_copy(out=ident_b.ap, in_=ident_f.ap), R=[ident_f.b], W=[ident_b.b])
        S.op("pool", lambda e: e.memset(tri_f.ap, 1.0), W=[tri_f.b])
        S.op("pool", lambda e: e.affine_select(out=tri_f.ap, in_=tri_f.ap, pattern=[[1, 128]], compare_op=ALU.is_ge,
                                               fill=0.0, base=0, channel_multiplier=-1), R=[tri_f.b], W=[tri_f.b])
        S.op("pool", lambda e: e.tensor_copy(out=tri_b.ap, in_=tri_f.ap), R=[tri_f.b], W=[tri_b.b])
        S.op("pool", lambda e: e.memset(ones_f.ap, 1.0), W=[ones_f.b])
        S.op("pool", lambda e: e.memset(o512_f.ap, 1.0 / 512), W=[o512_f.b])
        S.op("pool", lambda e: e.memset(epsr.ap, RMS_EPS), W=[epsr.b])
        S.op("pool", lambda e: e.memset(epsl.ap, LN_EPS), W=[epsl.b])
        S.op("pool", lambda e: e.memset(one1.ap, 1.0), W=[one1.b])
        S.dma("sp", lambda e: e.dma_start(out=gains.ap, in_=gains_d), W=[gains.b])
        if MAIN:
            S.dma("sp", lambda e: e.dma_start(out=ctxb.ap, in_=ctxb_d), W=[ctxb.b])
            S.dma("sp", lambda e: e.dma_start(out=bfb.ap, in_=bf_d), W=[bfb.b])

        _save = top[0]
        top[0] = AW - 900
        conv2s = alloc("conv2s", [128, 512])
        o_s = alloc("o_s", [128, 256])
        uf = alloc("uf", [128, 128])
        top[0] = _save
        limit[0] = AW - 900
        S.op("pool", lambda e: e.tensor_scalar(out=uf.ap, in0=tri_f.ap, scalar1=-1.0, scalar2=1.0, op0=ALU.mult, op1=ALU.add), R=[tri_f.b], W=[uf.b])
        base_top = top[0]

        ncnt = [0]

        def norm_T(xt, gi, hT_ap, hT_bufs, scr):
            hn, ss, rstd = scr[ncnt[0] % 2]
            ncnt[0] += 1
            S.op("act", lambda e: e.activation(out=hn.ap, in_=xt.ap, func=AF.Square, accum_out=ss.ap), R=[xt.b], W=[hn.b, ss.b])
            S.op("act", lambda e: e.activation(out=rstd.ap, in_=ss.ap, func=AF.Sqrt, scale=1.0 / D, bias=epsr.ap), R=[ss.b, epsr.b], W=[rstd.b])
            S.op("dve", lambda e: e.reciprocal(out=rstd.ap, in_=rstd.ap), R=[rstd.b], W=[rstd.b])
            S.op("act", lambda e: e.activation(out=hn.ap, in_=xt.ap, func=AF.Copy, scale=rstd.ap), R=[xt.b, rstd.b], W=[hn.b])
            bk = nbank()
            pv = bk.ap.bitcast(BF16).rearrange("p (a b) -> p a b", a=8)
            for c in range(8):
                S.op("pe", lambda e, c=c: e.transpose(out=pv[:, c, :], in_=hn.ap[:, c * 128:(c + 1) * 128], identity=ident_b.ap),
                     R=[hn.b, ident_b.b], W=[bk.b])
            S.op("dve", lambda e: e.tensor_tensor(out=hT_ap, in0=pv, in1=gains.ap[:, gi, :].unsqueeze(2).to_broadcast([128, 8, 128]), op=ALU.mult),
                 R=[gains.b], W=[bk.b] + list(hT_bufs))

        def norm_scratch():
            return [(alloc(f"hn{i}", [128, D], BF16), alloc(f"ss{i}", [128, 1]), alloc(f"rstd{i}", [128, 1])) for i in range(2)]

        def load_w(dst, src_ap, engine="pool"):
            S.dma(engine, lambda e: e.dma_start(out=dst.ap, in_=src_ap), W=[dst.b])

        def phase_S0():
            top[0] = base_top
            Wi = alloc("Wi", [128, 8, DIN], BF16)
            WoS = alloc("Wo", [128, 8, D], BF16)
            glu2 = alloc("glu2", [128, 512])
            keep_top = top[0]
            xs_t = alloc("xs_t", [128, D])
            hT = alloc("hT_s", [128, 8, 128], BF16)
            stg = alloc("stg", [128, 512])
            sig = alloc("sig", [128, 512])
            lz = alloc("lzs", [128, 8])
            scr = norm_scratch()
            wv = wie_d.rearrange("(c p) n -> p c n", p=128)
            for (a, b_) in [(0, 1284), (1284, 2568)]:
                S.dma("pool", lambda e, a=a, b_=b_: e.dma_start(out=Wi.ap[:, :, a:b_], in_=wv[:, :, a:b_]), W=[Wi.b])
            load_w(WoS, woe_d.rearrange("(c p) n -> p c n", p=128))
            S.dma("sp", lambda e: e.dma_start(out=xs_t.ap, in_=xs_d), W=[xs_t.b])
            norm_T(xs_t, 0, hT.ap, [hT.b], scr)

            def proj(Wt, c0, n):
                bk = nbank(0, 8)
                for c in range(8):
                    S.op("pe", lambda e, c=c: e.matmul(bk.ap[:, 0:n], lhsT=hT.ap[:, c, :], rhs=Wt.ap[:, c, c0:c0 + n], start=(c == 0), stop=(c == 7)), R=[hT.b, Wt.b], W=[bk.b])
                return bk

            def logsig(bank, src_ap, n, bias_t, out_t):
                S.op("dve", lambda e: e.tensor_tensor(out=lz.ap[:, 0:n], in0=src_ap, in1=bias_t.ap[:, 0:n], op=ALU.add), R=[bias_t.b], W=[lz.b, bank.b])
                S.op("act", lambda e: e.activation(out=lz.ap[:, 0:n], in_=lz.ap[:, 0:n], func=AF.Exp, scale=-1.0), R=[], W=[lz.b])
                S.op("act", lambda e: e.activation(out=lz.ap[:, 0:n], in_=lz.ap[:, 0:n], func=AF.Ln, bias=one1.ap), R=[one1.b], W=[lz.b])
                S.op("dve", lambda e: e.tensor_scalar(out=out_t.ap[:, 0:n] if out_t.ap.shape[0] == 128 else out_t.ap, in0=lz.ap[0:out_t.ap.shape[0], 0:n], scalar1=-1.0, scalar2=None, op0=ALU.mult),
                     R=[lz.b], W=[out_t.b])

            bk = proj(Wi, 512, 512)
            S.op("dve", lambda e, bk=bk: e.tensor_copy(out=stg.ap, in_=bk.ap), R=[], W=[bk.b, stg.b])
            S.dma("sp", lambda e: e.dma_start(out=ko_d[NOWN:NOWN + 128, :], in_=stg.ap), R=[stg.b])
            bk = proj(Wi, 1024, 512)
            S.op("dve", lambda e, bk=bk: e.tensor_copy(out=stg.ap, in_=bk.ap), R=[], W=[bk.b, stg.b])
            S.dma("sp", lambda e: e.dma_start(out=vo_d[NOWN:NOWN + 128, :], in_=stg.ap), R=[stg.b])
            bk = proj(Wi, 1536, 8)
            lfo = alloc("lfo_s", [128, 8])
            logsig(bk, bk.ap[:, 0:8], 8, bfb, lfo)
            S.dma("sp", lambda e: e.dma_start(out=lf_d[NOWN:NOWN + 128, :], in_=lfo.ap), R=[lfo.b])
            bkg = proj(Wi, 2056, 512)
            S.op("act", lambda e: e.activation(out=sig.ap, in_=bkg.ap, func=AF.Sigmoid), R=[], W=[bkg.b, sig.b])
            bkv = proj(Wi, 1544, 512)
            S.op("dve", lambda e: e.tensor_tensor(out=glu2.ap, in0=bkv.ap, in1=sig.ap, op=ALU.mult), R=[sig.b], W=[bkv.b, glu2.b])

            S.barrier()
            top[0] = keep_top
            st = alloc("st", [8, 30, 512])
            wb = alloc("wb", [8, 31, 512])
            cvb = alloc("cvb", [8, 3, 512])
            acc = alloc("acc_s", [8, 512])
            tmp = alloc("tmp_s", [8, 512])
            bst = alloc("bst_s", [8, 6])
            mv = alloc("mv_s", [8, 2])
            rs1 = alloc("rs1_s", [8, 1])
            S.dma("sp", lambda e: e.dma_start(out=st.ap, in_=st_d.rearrange("p (j c) -> p j c", j=30)), W=[st.b])
            S.dma("sp", lambda e: e.dma_start(out=wb.ap, in_=cwb_d), W=[wb.b])
            S.dma("sp", lambda e: e.dma_start(out=cvb.ap, in_=cvb_d), W=[cvb.b])
            S.dma("sp", lambda e: e.dma_start(out=css_d[:, 0:29, :], in_=st.ap[:, 1:30, :]), R=[st.b])
            S.dma("sp", lambda e: e.dma_start(out=css_d[:, 29, :], in_=glu2.ap[0:8, :]), R=[glu2.b])
            S.op("dve", lambda e: e.tensor_tensor(out=wb.ap[:, 0:30, :], in0=st.ap, in1=wb.ap[:, 0:30, :], op=ALU.mult), R=[st.b], W=[wb.b])
            S.op("dve", lambda e: e.tensor_reduce(out=acc.ap, in_=wb.ap[:, 0:30, :].rearrange("p j c -> p c j"), axis=mybir.AxisListType.X, op=ALU.add), R=[wb.b], W=[acc.b])
            S.op("dve", lambda e: e.tensor_tensor(out=tmp.ap, in0=glu2.ap[0:8, :], in1=wb.ap[:, 30, :], op=ALU.mult), R=[glu2.b, wb.b], W=[tmp.b])
            S.op("dve", lambda e: e.tensor_tensor(out=acc.ap, in0=acc.ap, in1=tmp.ap, op=ALU.add), R=[tmp.b], W=[acc.b])
            S.op("dve", lambda e: e.tensor_tensor(out=acc.ap, in0=acc.ap, in1=cvb.ap[:, 0, :], op=ALU.add), R=[cvb.b], W=[acc.b])
            S.op("dve", lambda e: e.bn_stats(out=bst.ap, in_=acc.ap), R=[acc.b], W=[bst.b])
            S.op("dve", lambda e: e.bn_aggr(out=mv.ap, in_=bst.ap), R=[bst.b], W=[mv.b])
            S.op("act", lambda e: e.activation(out=rs1.ap, in_=mv.ap[:, 1:2], func=AF.Sqrt, bias=epsl.ap[0:8, :]), R=[mv.b, epsl.b], W=[rs1.b])
            S.op("dve", lambda e: e.reciprocal(out=rs1.ap, in_=rs1.ap), R=[], W=[rs1.b])
            S.op("dve", lambda e: e.tensor_scalar(out=acc.ap, in0=acc.ap, scalar1=mv.ap[:, 0:1], scalar2=rs1.ap, op0=ALU.subtract, op1=ALU.mult), R=[mv.b, rs1.b], W=[acc.b])
            S.op("dve", lambda e: e.tensor_tensor(out=acc.ap, in0=acc.ap, in1=cvb.ap[:, 1, :], op=ALU.mult), R=[cvb.b], W=[acc.b])
            S.op("dve", lambda e: e.tensor_tensor(out=acc.ap, in0=acc.ap, in1=cvb.ap[:, 2, :], op=ALU.add), R=[cvb.b], W=[acc.b])
            S.op("pool", lambda e: e.memset(conv2s.ap, 0.0), W=[conv2s.b])
            S.op("act", lambda e: e.activation(out=conv2s.ap[0:8, :], in_=acc.ap, func=AF.Silu), R=[acc.b], W=[conv2s.b])


        def sample_attention(NS, NH, q4, k4, v4, lf4, ptT, kcs, vcs, lcs, o_out):
            W65 = NH * 65
            Kt = [alloc(f"Kt{i}", [128, 128, 64]) for i in range(2)]
            Vt = [alloc(f"Vt{i}", [128, 128, 64]) for i in range(2)]
            pvb = [alloc(f"pvb{i}", [128, 8192], BF16) for i in range(2)]
            Ft = [alloc(f"Ft{i}", [128, 128]) for i in range(2)]
            Pf = alloc("Pf", [128, 128])
            lg = alloc("lg", [128, 128])
            pt = alloc("pt_s", [128, 128])
            bj = alloc("bj", [128, 1])
            Rr = alloc("Rr", [128, NS * NH])
            qrep = alloc("qrep", [128, NS, W65])
            qd = alloc("qd", [NS, NS, W65])
            sel = alloc("sel", [128, NS, NS], BF16)
            self_ = alloc("self", [128, NS, NS])
            osum = alloc("osum", [NS, NH, 64])
            dn = alloc("dn", [NS, NS, NH])
            den = alloc("den", [NS, NH])
            sn = alloc("sn", [NS, NH])
            pn = alloc("pn", [NS, NH])
            tq = alloc("tq", [NS, NH * 64])
            idn = ident_f.ap[0:NS, 0:NS]
            S.op("dve", lambda e: e.tensor_tensor(out=qd.ap[:, :, 0:NH * 64], in0=q4.ap.unsqueeze(1).to_broadcast([NS, NS, NH * 64]), in1=idn.unsqueeze(2).to_broadcast([NS, NS, NH * 64]), op=ALU.mult),
                 R=[q4.b, ident_f.b], W=[qd.b])
            S.op("dve", lambda e: e.tensor_tensor(out=qd.ap[:, :, NH * 64:W65], in0=lf4.ap.unsqueeze(1).to_broadcast([NS, NS, NH]), in1=idn.unsqueeze(2).to_broadcast([NS, NS, NH]), op=ALU.mult),
                 R=[lf4.b, ident_f.b], W=[qd.b])
            qdf = qd.ap.rearrange("p a b -> p (a b)")
            qrf = qrep.ap.rearrange("p a b -> p (a b)")
            tot = NS * W65
            assert tot % 5 == 0 and tot // 5 <= 512
            stp = tot // 5
            for i0_ in range(0, tot, stp):
                def rep(i0_):
                    bk = nbank(4, 8)
                    S.op("pe", lambda e: e.matmul(bk.ap[:, 0:stp], lhsT=ones_f.ap[0:NS, :], rhs=qdf[:, i0_:i0_ + stp], start=True, stop=True), R=[ones_f.b, qd.b], W=[bk.b])
                    S.op("act", lambda e: e.activation(out=qrf[:, i0_:i0_ + stp], in_=bk.ap[:, 0:stp], func=AF.Copy), R=[], W=[bk.b, qrep.b])
                rep(i0_)
            S.op("pool", lambda e: e.memset(self_.ap, 0.0), W=[self_.b])
            S.op("pool", lambda e: e.affine_select(out=self_.ap, in_=self_.ap, pattern=[[1, NS], [-1, NS]], compare_op=ALU.not_equal, fill=1.0, base=0, channel_multiplier=0),
                 R=[], W=[self_.b])
            S.op("pool", lambda e: e.tensor_copy(out=sel.ap, in_=self_.ap), R=[self_.b], W=[sel.b])
            S.op("pool", lambda e: e.memset(Rr.ap, 0.0), W=[Rr.b])
            obank = [banks[i] for i in range(NH)]
            cnt = [0]
            for b in range(NS):
                for hh in range(NH):
                    def one(b, hh):
                        i = cnt[0] % 2
                        cnt[0] += 1
                        col = b * NH + hh
                        K_, V_, F_, pv_ = Kt[i], Vt[i], Ft[i], pvb[i]
                        K2 = K_.ap.rearrange("p s d -> p (s d)")
                        V2 = V_.ap.rearrange("p s d -> p (s d)")
                        S.dma("pool", lambda e: e.indirect_dma_start(out=K2, out_offset=None, in_=kcs[hh], in_offset=bass.IndirectOffsetOnAxis(ap=ptT.ap[:, b:b + 1], axis=0)),
                              R=[ptT.b], W=[K_.b])
                        S.dma("pool", lambda e: e.indirect_dma_start(out=V2, out_offset=None, in_=vcs[hh], in_offset=bass.IndirectOffsetOnAxis(ap=ptT.ap[:, b:b + 1], axis=0)),
                              R=[ptT.b], W=[V_.b])
                        S.dma("pool", lambda e: e.indirect_dma_start(out=F_.ap, out_offset=None, in_=lcs[hh], in_offset=bass.IndirectOffsetOnAxis(ap=ptT.ap[:, b:b + 1], axis=0)),
                              R=[ptT.b], W=[F_.b])
                        S.op("pool", lambda e: e.tensor_tensor(out=K_.ap, in0=K_.ap, in1=qrep.ap[:, b, hh * 64:(hh + 1) * 64].unsqueeze(1).to_broadcast([128, 128, 64]), op=ALU.mult),
                             R=[qrep.b], W=[K_.b])
                        S.op("dve", lambda e: e.tensor_reduce(out=lg.ap, in_=K_.ap, axis=mybir.AxisListType.X, op=ALU.add), R=[K_.b], W=[lg.b])
                        S.op("dve", lambda e: e.tensor_tensor_scan(out=Pf.ap, data0=ones_f.ap, data1=F_.ap, initial=0.0, op0=ALU.mult, op1=ALU.add), R=[ones_f.b, F_.b], W=[Pf.b])
                        gb = nbank(4, 8)
                        S.op("pe", lambda e: e.matmul(gb.ap[:, 0:1], lhsT=uf.ap, rhs=Pf.ap[:, 127:128], start=True, stop=True), R=[uf.b, Pf.b], W=[gb.b])
                        S.op("dve", lambda e: e.scalar_tensor_tensor(out=bj.ap, in0=gb.ap[:, 0:1], scalar=Pf.ap[:, 127:128], in1=qrep.ap[:, b, NH * 64 + hh:NH * 64 + hh + 1], op0=ALU.add, op1=ALU.add),
                             R=[Pf.b, qrep.b], W=[gb.b, bj.b])
                        S.op("dve", lambda e: e.tensor_tensor(out=lg.ap, in0=lg.ap, in1=Pf.ap, op=ALU.subtract), R=[Pf.b], W=[lg.b])
                        S.op("act", lambda e: e.activation(out=pt.ap, in_=lg.ap, func=AF.Exp, bias=bj.ap, scale=1.0, accum_out=Rr.ap[:, col:col + 1]), R=[lg.b, bj.b], W=[pt.b, Rr.b])
                        S.op("dve", lambda e: e.tensor_tensor(out=pv_.ap.rearrange("p (s d) -> p s d", s=128), in0=V_.ap, in1=pt.ap.unsqueeze(2).to_broadcast([128, 128, 64]), op=ALU.mult),
                             R=[V_.b, pt.b], W=[pv_.b])
                        ob = obank[hh]
                        for ck in range(16):
                            S.op("pe", lambda e, ck=ck: e.matmul(ob.ap[0:NS, :], lhsT=sel.ap[:, b, :], rhs=pv_.ap[:, ck * 512:(ck + 1) * 512], start=(b == 0 and ck == 0), stop=False, skip_group_check=True),
                                 R=[sel.b, pv_.b], W=[ob.b])
                    one(b, hh)
            for hh in range(NH):
                S.op("dve", lambda e, hh=hh: e.tensor_reduce(out=osum.ap[:, hh, :], in_=obank[hh].ap[0:NS, :].rearrange("p (s d) -> p d s", s=8), axis=mybir.AxisListType.X, op=ALU.add),
                     R=[], W=[obank[hh].b, osum.b])
            db = nbank(4, 8)
            S.op("pe", lambda e: e.matmul(db.ap[0:NS, 0:NS * NH], lhsT=ones_f.ap[:, 0:NS], rhs=Rr.ap, start=True, stop=True), R=[ones_f.b, Rr.b], W=[db.b])
            S.op("dve", lambda e: e.tensor_tensor(out=dn.ap, in0=db.ap[0:NS, 0:NS * NH].rearrange("p (a b) -> p a b", a=NS), in1=idn.unsqueeze(2).to_broadcast([NS, NS, NH]), op=ALU.mult),
                 R=[ident_f.b], W=[db.b, dn.b])
            S.op("dve", lambda e: e.tensor_reduce(out=den.ap, in_=dn.ap.rearrange("p a b -> p b a"), axis=mybir.AxisListType.X, op=ALU.add), R=[dn.b], W=[den.b])
            S.op("dve", lambda e: e.tensor_tensor(out=tq.ap, in0=q4.ap, in1=k4.ap, op=ALU.mult), R=[q4.b, k4.b], W=[tq.b])
            S.op("dve", lambda e: e.tensor_reduce(out=sn.ap, in_=tq.ap.rearrange("p (h d) -> p h d", h=NH), axis=mybir.AxisListType.X, op=ALU.add), R=[tq.b], W=[sn.b])
            S.op("act", lambda e: e.activation(out=pn.ap, in_=sn.ap, func=AF.Exp), R=[sn.b], W=[pn.b])
            S.op("dve", lambda e: e.tensor_tensor(out=den.ap, in0=den.ap, in1=pn.ap, op=ALU.add), R=[pn.b], W=[den.b])
            S.op("dve", lambda e: e.reciprocal(out=den.ap, in_=den.ap), R=[], W=[den.b])
            S.op("dve", lambda e: e.tensor_tensor(out=tq.ap.rearrange("p (h d) -> p h d", h=NH), in0=v4.ap.rearrange("p (h d) -> p h d", h=NH), in1=pn.ap.unsqueeze(2).to_broadcast([NS, NH, 64]), op=ALU.mult),
                 R=[v4.b, pn.b], W=[tq.b])
            S.op("dve", lambda e: e.tensor_tensor(out=osum.ap, in0=osum.ap, in1=tq.ap.rearrange("p (h d) -> p h d", h=NH), op=ALU.add), R=[tq.b], W=[osum.b])
            S.op("dve", lambda e: e.tensor_tensor(out=o_out.ap.rearrange("p (h d) -> p h d", h=NH), in0=osum.ap, in1=den.ap.unsqueeze(2).to_broadcast([NS, NH, 64]), op=ALU.mult),
                 R=[den.b], W=[osum.b, o_out.b])

        def phase_A():
            Wi = alloc("Wi", [128, 8, DIN], BF16)
            Wo = alloc("Wo", [128, 8, D], BF16)
            kT = alloc("kT", [128, 4, 4096], BF16)
            V = alloc("V", [128, 32, 8, 65], BF16)
            CN = alloc("CN", [128, 32, 8])
            BI = alloc("BI", [128, 32, 8])
            carry = alloc("carry", [128, 8])
            cref = alloc("cref", [128, 8])
            cw = alloc("cw", [128, 4, 31])
            cv = alloc("cv", [128, 3, 4])
            hTs = alloc("hTs", [128, 8, 512], BF16)
            hb2 = Buf("hTs2")
            qT = alloc("qT", [128, 4, 512], BF16)
            glu = alloc("glu", [128, 4, 542])
            acc = alloc("acc", [128, 4, 512])
            ysq = alloc("ysq", [128, 512])
            msb = alloc("msb", [128, 512])
            rsd = alloc("rsd", [128, 512])
            convT = alloc("convT", [128, 4, 512], BF16)
            Pt = [alloc(f"P{i}", [128, 512], BF16) for i in range(3)]
            xin = [alloc(f"xin{i}", [128, D]) for i in range(2)]
            xr = [alloc(f"xr{i}", [128, D]) for i in range(2)]
            kst = alloc("kst", [128, 512])
            vst = alloc("vst", [128, 512])
            lz = alloc("lz", [128, 8])
            lfo = alloc("lfo", [128, 8])
            rd = alloc("rd", [128, 4, 1])
            cst = alloc("cst", [32, 512])
            scr = norm_scratch()
            attn_tok = Tl(hTs.ap.rearrange("p a b -> p (a b)")[:, 0:2048].rearrange("p (a b) -> p a b", a=4), hTs.b)
            attnT = Tl(hTs.ap.rearrange("p a b -> p (a b)")[:, 2048:4096].rearrange("p (a b) -> p a b", a=4), hb2)

            S.dma("sp", lambda e: e.dma_start(out=cw.ap, in_=cw_d), W=[cw.b])
            S.dma("sp", lambda e: e.dma_start(out=cv.ap, in_=cv_d), W=[cv.b])
            S.op("pool", lambda e: e.memset(V.ap, 1.0), W=[V.b])
            S.op("pool", lambda e: e.memset(carry.ap, 0.0), W=[carry.b])
            S.op("pool", lambda e: e.memset(glu.ap, 0.0), W=[glu.b])

            xcnt = [0]

            def proj_fm(col0, evac):
                bk = nbank()
                for c in range(8):
                    S.op("pe", lambda e, c=c: e.matmul(bk.ap, lhsT=Wi.ap[:, c, col0:col0 + 128], rhs=hTs.ap[:, c, :], start=(c == 0), stop=(c == 7)),
                         R=[Wi.b, hTs.b, hb2], W=[bk.b])
                evac(bk)

            def superblock(sbi, own):
                gsb = sbi if not own else 4 + sbi
                src = xo_d if own else xc_d
                for b in range(4):
                    xt = xin[xcnt[0] % 2]
                    xcnt[0] += 1
                    r0 = (sbi * 4 + b) * 128
                    S.dma("sp", lambda e, xt=xt, r0=r0: e.dma_start(out=xt.ap, in_=src[r0:r0 + 128, :]), W=[xt.b])
                    norm_T(xt, 0, hTs.ap[:, :, b * 128:(b + 1) * 128], [hTs.b, hb2], scr)
                for p in range(4):
                    def ev(bk, p=p):
                        S.op("act", lambda e: e.activation(out=kT.ap[:, p, gsb * 512:(gsb + 1) * 512], in_=bk.ap, func=AF.Copy), R=[], W=[bk.b, kT.b])
                    proj_fm(512 + p * 128, ev)
                if own:
                    for p in range(4):
                        def ev(bk, p=p):
                            S.op("act", lambda e: e.activation(out=qT.ap[:, p, :], in_=bk.ap, func=AF.Copy, scale=0.125), R=[], W=[bk.b, qT.b])
                        proj_fm(p * 128, ev)
                if own or sbi == 3:
                    S.op("pool", lambda e: e.tensor_copy(out=glu.ap[:, :, 0:30], in_=glu.ap[:, :, 512:542]), R=[glu.b], W=[glu.b])
                    for ch in range(4):
                        def ev_g(bk, ch=ch):
                            S.op("act", lambda e: e.activation(out=glu.ap[:, ch, 30:542], in_=bk.ap, func=AF.Sigmoid), R=[], W=[bk.b, glu.b])
                        proj_fm(1544 + 512 + ch * 128, ev_g)

                        def ev_v(bk, ch=ch):
                            S.op("dve", lambda e: e.tensor_tensor(out=glu.ap[:, ch, 30:542], in0=bk.ap, in1=glu.ap[:, ch, 30:542], op=ALU.mult), R=[], W=[bk.b, glu.b])
                        proj_fm(1544 + ch * 128, ev_v)
                for b in range(4):
                    jb = gsb * 4 + b
                    r0 = (sbi * 4 + b) * 128
                    bk = nbank()
                    for c in range(8):
                        S.op("pe", lambda e, c=c, bk=bk, b=b: e.matmul(bk.ap, lhsT=hTs.ap[:, c, b * 128:(b + 1) * 128], rhs=Wi.ap[:, c, 1024:1536], start=(c == 0), stop=(c == 7)),
                             R=[Wi.b, hTs.b, hb2], W=[bk.b])
                    S.op("act", lambda e, bk=bk, jb=jb: e.activation(out=V.ap[:, jb, :, 0:64], in_=bk.ap.rearrange("p (h d) -> p h d", h=8), func=AF.Copy), R=[], W=[bk.b, V.b])
                    if own:
                        S.op("dve", lambda e, bk=bk: e.tensor_copy(out=vst.ap, in_=bk.ap), R=[], W=[bk.b, vst.b])
                        S.dma("sp", lambda e, r0=r0: e.dma_start(out=vo_d[r0:r0 + 128, :], in_=vst.ap), R=[vst.b])
                        bk2 = nbank()
                        for c in range(8):
                            S.op("pe", lambda e, c=c, bk2=bk2, b=b: e.matmul(bk2.ap, lhsT=hTs.ap[:, c, b * 128:(b + 1) * 128], rhs=Wi.ap[:, c, 512:1024], start=(c == 0), stop=(c == 7)),
                                 R=[Wi.b, hTs.b, hb2], W=[bk2.b])
                        S.op("dve", lambda e, bk2=bk2: e.tensor_copy(out=kst.ap, in_=bk2.ap), R=[], W=[bk2.b, kst.b])
                        S.dma("sp", lambda e, r0=r0: e.dma_start(out=ko_d[r0:r0 + 128, :], in_=kst.ap), R=[kst.b])
                    bk3 = nbank()
                    for c in range(8):
                        S.op("pe", lambda e, c=c, bk3=bk3, b=b: e.matmul(bk3.ap[:, 0:8], lhsT=hTs.ap[:, c, b * 128:(b + 1) * 128], rhs=Wi.ap[:, c, 1536:1544], start=(c == 0), stop=(c == 7)),
                             R=[Wi.b, hTs.b, hb2], W=[bk3.b])
                    S.op("dve", lambda e, bk3=bk3: e.tensor_tensor(out=lz.ap, in0=bk3.ap[:, 0:8], in1=bfb.ap, op=ALU.add), R=[bfb.b], W=[bk3.b, lz.b])
                    S.op("act", lambda e: e.activation(out=lz.ap, in_=lz.ap, func=AF.Exp, scale=-1.0), R=[], W=[lz.b])
                    S.op("act", lambda e: e.activation(out=lz.ap, in_=lz.ap, func=AF.Ln, bias=one1.ap), R=[one1.b], W=[lz.b])
                    if own:
                        S.op("dve", lambda e: e.tensor_scalar(out=lfo.ap, in0=lz.ap, scalar1=-1.0, scalar2=None, op0=ALU.mult), R=[lz.b], W=[lfo.b])
                        S.dma("sp", lambda e, r0=r0: e.dma_start(out=lf_d[r0:r0 + 128, :], in_=lfo.ap), R=[lfo.b])
                    if own and b == 0:
                        S.op("dve", lambda e: e.tensor_copy(out=cref.ap, in_=carry.ap), R=[carry.b], W=[cref.b])
                    bk4 = nbank()
                    S.op("pe", lambda e, bk4=bk4: e.matmul(bk4.ap[:, 0:8], lhsT=tri_f.ap, rhs=lz.ap, start=True, stop=True), R=[tri_f.b, lz.b], W=[bk4.b])
                    S.op("pe", lambda e, bk4=bk4: e.matmul(bk4.ap[:, 8:16], lhsT=ones_f.ap, rhs=lz.ap, start=False, stop=True, skip_group_check=True), R=[ones_f.b, lz.b], W=[bk4.b])
                    S.op("dve", lambda e, bk4=bk4, jb=jb: e.tensor_tensor(out=CN.ap[:, jb, :], in0=bk4.ap[:, 0:8], in1=carry.ap, op=ALU.add), R=[carry.b], W=[bk4.b, CN.b])
                    S.op("dve", lambda e, bk4=bk4: e.tensor_tensor(out=carry.ap, in0=bk4.ap[:, 8:16], in1=carry.ap, op=ALU.add), R=[], W=[bk4.b, carry.b])
                if not own:
                    return
                nj = gsb * 4 + 4
                S.op("dve", lambda e: e.tensor_tensor(out=BI.ap[:, 0:nj, :], in0=CN.ap[:, 0:nj, :], in1=cref.ap.unsqueeze(1).to_broadcast([128, nj, 8]), op=ALU.subtract),
                     R=[CN.b, cref.b], W=[BI.b])
                S.op("dve", lambda e: e.tensor_scalar(out=BI.ap[:, 0:16, :], in0=BI.ap[:, 0:16, :], scalar1=ctxb.ap, scalar2=None, op0=ALU.add), R=[ctxb.b], W=[BI.b])

                def conv_ops():
                    for ch in range(4):
                        yield lambda ch=ch: S.op("dve", lambda e: e.tensor_scalar(out=acc.ap[:, ch, :], in0=glu.ap[:, ch, 0:512], scalar1=cw.ap[:, ch, 0:1], scalar2=cv.ap[:, 0, ch:ch + 1], op0=ALU.mult, op1=ALU.add),
                                                 R=[glu.b, cw.b, cv.b], W=[acc.b])
                        for j in range(1, 31):
                            yield lambda ch=ch, j=j: S.op("dve", lambda e: e.scalar_tensor_tensor(out=acc.ap[:, ch, :], in0=glu.ap[:, ch, j:j + 512], scalar=cw.ap[:, ch, j:j + 1], in1=acc.ap[:, ch, :], op0=ALU.mult, op1=ALU.add),
                                                          R=[glu.b, cw.b], W=[acc.b])
                cgen = conv_ops()

                def conv_some(n):
                    for _ in range(n):
                        f = next(cgen, None)
                        if f is None:
                            return
                        f()

                f0 = gsb * 4
                pcnt = [0]
                def head(h):
                    p, r0 = h // 2, (h % 2) * 64
                    ob = banks[6 + h % 2]
                    ov = ob.ap[:, 0:260].rearrange("p (a b) -> p a b", a=4)
                    jobs = []
                    for j in range(f0 + 4):
                        jobs.append((j, max(0, j - f0)))

                    def do_S(j, jj):
                        sb_ = nbank(0, 3)
                        pt = Pt[pcnt[0] % 3]
                        pcnt[0] += 1
                        c0 = jj * 128
                        S.op("pe", lambda e: e.matmul(sb_.ap[:, c0:512], lhsT=kT.ap[r0:r0 + 64, p, j * 128:(j + 1) * 128], rhs=qT.ap[r0:r0 + 64, p, c0:512], start=True, stop=True),
                             R=[kT.b, qT.b], W=[sb_.b])
                        S.op("act", lambda e: e.activation(out=pt.ap[:, c0:512], in_=sb_.ap[:, c0:512], func=AF.Exp, bias=BI.ap[:, j, h:h + 1], scale=1.0), R=[BI.b], W=[sb_.b, pt.b])
                        if j >= f0:
                            S.op("pool", lambda e: e.tensor_tensor(out=pt.ap[:, c0:c0 + 128], in0=pt.ap[:, c0:c0 + 128], in1=tri_b.ap, op=ALU.mult), R=[tri_b.b], W=[pt.b])
                        return pt

                    def do_PV(j, jj, pt, first):
                        def one(qb):
                            st_ = first and qb == 0
                            S.op("pe", lambda e: e.matmul(ov[:, qb, :], lhsT=pt.ap[:, qb * 128:(qb + 1) * 128], rhs=V.ap[:, j, h, :], start=st_, stop=False, skip_group_check=True),
                                 R=[pt.b, V.b], W=[ob.b])
                        for qb in range(jj, 4):
                            one(qb)

                    pend = []
                    for (j, jj) in jobs:
                        pt = do_S(j, jj)
                        pend.append((j, jj, pt))
                        if len(pend) > 2:
                            a = pend.pop(0)
                            do_PV(a[0], a[1], a[2], a[0] == 0)
                    for a in pend:
                        do_PV(a[0], a[1], a[2], a[0] == 0)
                    S.op("dve", lambda e: e.reciprocal(out=rd.ap, in_=ov[:, :, 64:65]), R=[], W=[ob.b, rd.b])
                    S.op("dve", lambda e: e.tensor_tensor(out=attn_tok.ap[:, :, h * 64:(h + 1) * 64], in0=ov[:, :, 0:64], in1=rd.ap.to_broadcast([128, 4, 64]), op=ALU.mult),
                         R=[rd.b], W=[ob.b, attn_tok.b])

                for h in range(8):
                    head(h)
                    conv_some(16)
                conv_some(1000)

                if sbi == 3:
                    bkc = nbank()
                    for ch in range(4):
                        S.op("pe", lambda e, ch=ch: e.transpose(out=bkc.ap[0:32, ch * 128:(ch + 1) * 128], in_=glu.ap[:, ch, 510:542], identity=ident_f.ap),
                             R=[glu.b, ident_f.b], W=[bkc.b])
                    S.op("dve", lambda e: e.tensor_copy(out=cst.ap, in_=bkc.ap[0:32, :]), R=[], W=[bkc.b, cst.b])
                    S.dma("sp", lambda e: e.dma_start(out=cs_d, in_=cst.ap), R=[cst.b])

                bm, be = nbank(), nbank()
                for ch in range(4):
                    S.op("pe", lambda e, ch=ch: e.matmul(bm.ap, lhsT=o512_f.ap, rhs=acc.ap[:, ch, :], start=(ch == 0), stop=(ch == 3)), R=[o512_f.b, acc.b], W=[bm.b])
                for ch in range(4):
                    S.op("act", lambda e, ch=ch: e.activation(out=ysq.ap, in_=acc.ap[:, ch, :], func=AF.Square), R=[acc.b], W=[ysq.b])
                    S.op("pe", lambda e, ch=ch: e.matmul(be.ap, lhsT=o512_f.ap, rhs=ysq.ap, start=(ch == 0), stop=(ch == 3)), R=[o512_f.b, ysq.b], W=[be.b])
                S.op("act", lambda e: e.activation(out=msb.ap, in_=bm.ap, func=AF.Copy), R=[], W=[bm.b, msb.b])
                S.op("dve", lambda e: e.tensor_tensor(out=rsd.ap, in0=msb.ap, in1=msb.ap, op=ALU.mult), R=[msb.b], W=[rsd.b])
                S.op("dve", lambda e: e.tensor_tensor(out=rsd.ap, in0=be.ap, in1=rsd.ap, op=ALU.subtract), R=[], W=[be.b, rsd.b])
                S.op("act", lambda e: e.activation(out=rsd.ap, in_=rsd.ap, func=AF.Sqrt, bias=epsl.ap), R=[epsl.b], W=[rsd.b])
                S.op("dve", lambda e: e.reciprocal(out=rsd.ap, in_=rsd.ap), R=[], W=[rsd.b])
                for ch in range(4):
                    S.op("dve", lambda e, ch=ch: e.tensor_tensor(out=acc.ap[:, ch, :], in0=acc.ap[:, ch, :], in1=msb.ap, op=ALU.subtract), R=[msb.b], W=[acc.b])
                    S.op("dve", lambda e, ch=ch: e.tensor_tensor(out=acc.ap[:, ch, :], in0=acc.ap[:, ch, :], in1=rsd.ap, op=ALU.mult), R=[rsd.b], W=[acc.b])
                    S.op("act", lambda e, ch=ch: e.activation(out=convT.ap[:, ch, :], in_=acc.ap[:, ch, :], func=AF.Silu, scale=cv.ap[:, 1, ch:ch + 1], bias=cv.ap[:, 2, ch:ch + 1]),
                         R=[acc.b, cv.b], W=[convT.b])

                def tr_attn(qb):
                    bk = nbank()
                    pv = bk.ap.bitcast(BF16).rearrange("p (a b) -> p a b", a=8)
                    for cc in range(4):
                        S.op("pe", lambda e, cc=cc: e.transpose(out=pv[:, cc, :], in_=attn_tok.ap[:, qb, cc * 128:(cc + 1) * 128], identity=ident_b.ap),
                             R=[attn_tok.b, ident_b.b], W=[bk.b])
                    S.op("act", lambda e: e.activation(out=attnT.ap[:, :, qb * 128:(qb + 1) * 128], in_=pv[:, 0:4, :], func=AF.Copy), R=[], W=[bk.b, attnT.b])
                for qb in range(4):
                    tr_attn(qb)
                for qb in range(4):
                    r0 = (sbi * 4 + qb) * 128
                    xt = xr[qb % 2]
                    S.dma("sp", lambda e, xt=xt, r0=r0: e.dma_start(out=xt.ap, in_=xo_d[r0:r0 + 128, :]), W=[xt.b])
                    for hf in range(2):
                        bk = nbank()
                        for c in range(8):
                            lhs = attnT.ap[:, c, qb * 128:(qb + 1) * 128] if c < 4 else convT.ap[:, c - 4, qb * 128:(qb + 1) * 128]
                            S.op("pe", lambda e, c=c, lhs=lhs, bk=bk, hf=hf: e.matmul(bk.ap, lhsT=lhs, rhs=Wo.ap[:, c, hf * 512:(hf + 1) * 512], start=(c == 0), stop=(c == 7)),
                                 R=[attnT.b, convT.b, Wo.b], W=[bk.b])
                        S.op("dve", lambda e, bk=bk, xt=xt, hf=hf: e.tensor_tensor(out=xt.ap[:, hf * 512:(hf + 1) * 512], in0=bk.ap, in1=xt.ap[:, hf * 512:(hf + 1) * 512], op=ALU.add),
                             R=[], W=[bk.b, xt.b])
                    S.dma("sp", lambda e, xt=xt, r0=r0: e.dma_start(out=xres_d[r0:r0 + 128, :], in_=xt.ap), R=[xt.b])

            for sbi in range(4):
                superblock(sbi, False)
            for sbi in range(4):
                superblock(sbi, True)

            xa = alloc("xa", [8, 2, 256])
            mix = alloc("mix", [128, D], BF16)
            S.dma("sp", lambda e: e.dma_start(out=xa.ap, in_=attn2_d.rearrange("b (par d) -> b par d", par=2)), W=[xa.b])
            S.op("pool", lambda e: e.memset(mix.ap, 0.0), W=[mix.b])
            S.op("act", lambda e: e.activation(out=mix.ap[0:8, 0:512], in_=xa.ap.rearrange("p a b -> p (a b)"), func=AF.Copy), R=[xa.b], W=[mix.b])
            S.op("act", lambda e: e.activation(out=mix.ap[0:8, 512:1024], in_=conv2s.ap[0:8, :], func=AF.Copy), R=[conv2s.b], W=[mix.b])
            bk = nbank()
            pv = bk.ap.bitcast(BF16).rearrange("p (a b) -> p a b", a=8)
            for c in range(8):
                S.op("pe", lambda e, c=c: e.transpose(out=pv[:, c, :], in_=mix.ap[:, c * 128:(c + 1) * 128], identity=ident_b.ap), R=[mix.b, ident_b.b], W=[bk.b])
            S.op("act", lambda e: e.activation(out=hTs.ap[:, :, 0:128], in_=pv, func=AF.Copy), R=[], W=[bk.b, hTs.b, hb2])
            xt = xr[0]
            S.dma("sp", lambda e: e.dma_start(out=xt.ap, in_=xs_d), W=[xt.b])
            for hf in range(2):
                def s1h(hf):
                    bk2 = nbank()
                    for c in range(8):
                        S.op("pe", lambda e, c=c: e.matmul(bk2.ap, lhsT=hTs.ap[:, c, 0:128], rhs=Wo.ap[:, c, hf * 512:(hf + 1) * 512], start=(c == 0), stop=(c == 7)),
                             R=[hTs.b, hb2, Wo.b], W=[bk2.b])
                    S.op("dve", lambda e: e.tensor_tensor(out=xt.ap[:, hf * 512:(hf + 1) * 512], in0=bk2.ap, in1=xt.ap[:, hf * 512:(hf + 1) * 512], op=ALU.add), R=[], W=[bk2.b, xt.b])
                s1h(hf)
            S.dma("sp", lambda e: e.dma_start(out=xres_d[NOWN:NOWN + 128, :], in_=xt.ap), R=[xt.b])

        SBS = [(0, 4), (4, 4), (8, 4), (12, 4), (16, 1)]

        def setup_resident():
            top[0] = base_top
            xs_ = alloc("xres_sb", [128, 17, D])
            xb = [Buf(f"x{i}") for i in range(17)]
            hTa = alloc("hT_all", [128, 8, 17 * 128], BF16)
            hb = [Buf(f"hTa{i}") for i in range(5)]
            return xs_, xb, hTa, hb

        def phase_ffn(l, from_dram, final, res):
            xs_, xb, hTa, hb = res
            mark = top[0]
            gi = 1 + 2 * l
            Wg = [alloc(f"Wg{i}", [128, 8, 768], BF16) for i in range(2)]
            Wu = [alloc(f"Wu{i}", [128, 8, 768], BF16) for i in range(2)]
            Wd = [alloc(f"Wd{i}", [128, 6, D], BF16) for i in range(2)]
            actT = [alloc(f"actT{i}", [128, 6, 512], BF16) for i in range(2)]
            sg = [alloc(f"sg{i}", [128, 512]) for i in range(2)]
            scr = norm_scratch()
            if final:
                gfin = alloc("gfin", [128, D])
                yst = [alloc("yst0", [128, D])] * 2
                S.dma("sp", lambda e: e.dma_start(out=gfin.ap, in_=gfin_d), W=[gfin.b])
            wgv = wg_d[l].rearrange("(c p) n -> p c n", p=128)
            wuv = wu_d[l].rearrange("(c p) n -> p c n", p=128)
            cnt = [0]

            def load_pass(q):
                c0, c1 = QCH[q]
                n = c1 - c0
                i = q % 2
                S.dma("pool", lambda e: e.dma_start(out=Wg[i].ap[:, :, 0:n * 128], in_=wgv[:, :, c0 * 128:c1 * 128]), W=[Wg[i].b])
                S.dma("pool", lambda e: e.dma_start(out=Wu[i].ap[:, :, 0:n * 128], in_=wuv[:, :, c0 * 128:c1 * 128]), W=[Wu[i].b])
                S.dma("pool", lambda e: e.dma_start(out=Wd[i].ap[:, 0:n, :], in_=wd_d[l][c0 * 128:c1 * 128, :].rearrange("(c p) n -> p c n", p=128)), W=[Wd[i].b])

            def norms(sbi):
                b0, nb_ = SBS[sbi]
                for b in range(b0, b0 + nb_):
                    def one(b):
                        xt = Tl(xs_.ap[:, b, :], xb[b])
                        if from_dram:
                            S.dma("sp", lambda e: e.dma_start(out=xt.ap, in_=xres_d[b * 128:(b + 1) * 128, :]), W=[xt.b])
                        norm_T(xt, gi, hTa.ap[:, :, b * 128:(b + 1) * 128], [hb[sbi]], scr)
                    one(b)

            def gate_up(q, sbi):
                b0, nb_ = SBS[sbi]
                t0, nt = b0 * 128, nb_ * 128
                n = QCH[q][1] - QCH[q][0]
                i = q % 2
                a = actT[cnt[0] % 2]

                def chunk(ci):
                    bg, bu = nbank(0, 8), nbank(0, 8)
                    for c in range(8):
                        S.op("pe", lambda e, c=c: e.matmul(bg.ap[:, 0:nt], lhsT=Wg[i].ap[:, c, ci * 128:(ci + 1) * 128], rhs=hTa.ap[:, c, t0:t0 + nt], start=(c == 0), stop=(c == 7)),
                             R=[Wg[i].b, hb[sbi]], W=[bg.b])
                    for c in range(8):
                        S.op("pe", lambda e, c=c: e.matmul(bu.ap[:, 0:nt], lhsT=Wu[i].ap[:, c, ci * 128:(ci + 1) * 128], rhs=hTa.ap[:, c, t0:t0 + nt], start=(c == 0), stop=(c == 7)),
                             R=[Wu[i].b, hb[sbi]], W=[bu.b])
                    s_ = sg[ci % 2]
                    S.op("act", lambda e: e.activation(out=s_.ap[:, 0:nt], in_=bg.ap[:, 0:nt], func=AF.Silu), R=[], W=[bg.b, s_.b])
                    S.op("dve", lambda e: e.tensor_tensor(out=a.ap[:, ci, 0:nt], in0=bu.ap[:, 0:nt], in1=s_.ap[:, 0:nt], op=ALU.mult), R=[s_.b], W=[bu.b, a.b])
                for ci in range(n):
                    chunk(ci)
                cnt[0] += 1
                return a

            def down(q, sbi, a):
                b0, nb_ = SBS[sbi]
                n = QCH[q][1] - QCH[q][0]
                i = q % 2

                def blk(bl):
                    b = b0 + bl
                    for hf in range(2):
                        def half(hf):
                            bk = nbank(0, 8)
                            for ci in range(n):
                                S.op("pe", lambda e, ci=ci: e.matmul(bk.ap, lhsT=a.ap[:, ci, bl * 128:(bl + 1) * 128], rhs=Wd[i].ap[:, ci, hf * 512:(hf + 1) * 512], start=(ci == 0), stop=(ci == n - 1)),
                                     R=[a.b, Wd[i].b], W=[bk.b])
                            S.op("dve", lambda e: e.tensor_tensor(out=xs_.ap[:, b, hf * 512:(hf + 1) * 512], in0=bk.ap, in1=xs_.ap[:, b, hf * 512:(hf + 1) * 512], op=ALU.add),
                                 R=[], W=[bk.b, xb[b]])
                        half(hf)
                    if final and q == 3:
                        sq, ss, rstd = scr[b % 2]
                        yt = yst[b % 2]
                        xap = xs_.ap[:, b, :]
                        S.op("act", lambda e: e.activation(out=sq.ap, in_=xap, func=AF.Square, accum_out=ss.ap), R=[xb[b]], W=[sq.b, ss.b])
                        S.op("act", lambda e: e.activation(out=rstd.ap, in_=ss.ap, func=AF.Sqrt, scale=1.0 / D, bias=epsr.ap), R=[ss.b, epsr.b], W=[rstd.b])
                        S.op("dve", lambda e: e.reciprocal(out=rstd.ap, in_=rstd.ap), R=[rstd.b], W=[rstd.b])
                        S.op("act", lambda e: e.activation(out=yt.ap, in_=xap, func=AF.Copy, scale=rstd.ap), R=[xb[b], rstd.b], W=[yt.b])
                        S.op("dve", lambda e: e.tensor_tensor(out=yt.ap, in0=yt.ap, in1=gfin.ap, op=ALU.mult), R=[gfin.b], W=[yt.b])
                        S.dma("sp", lambda e: e.dma_start(out=y_d[b * 128:(b + 1) * 128, :], in_=yt.ap), R=[yt.b])
                for bl in range(nb_):
                    blk(bl)

            load_pass(0)
            for q in range(4):
                if q + 1 < 4:
                    load_pass(q + 1)
                prev = None
                if q == 0:
                    norms(0)
                for sbi in range(5):
                    a = gate_up(q, sbi)
                    if q == 0 and sbi + 1 < 5:
                        norms(sbi + 1)
                    if prev is not None:
                        down(q, prev[0], prev[1])
                    prev = (sbi, a)
                down(q, prev[0], prev[1])
            top[0] = mark

        def phase_sgu(res):
            xs_, xb, hTa, hb = res
            mark = top[0]
            Wi1 = alloc("Wi1", [128, 8, 2 * D], BF16)
            Wo1 = alloc("Wo1", [128, 8, D], BF16)
            swt = alloc("swt", [128, 8, 128])
            WcT = alloc("WcT", [128, 8, 128], BF16)
            WcTs = alloc("WcTs", [128, 8, 128], BF16)
            BS = alloc("BS", [128, 8, 128])
            sw0 = alloc("sw0", [128, 8])
            sb0 = alloc("sb0", [128, 8])
            sln = alloc("sln", [128, 2, D])
            hTs = Tl(hTa.ap[:, :, 0:512], Buf("hTs1"))
            uT = Tl(hTa.ap[:, :, 512:1024], Buf("uT"))
            vts = [alloc(f"vt{i}", [128, D]) for i in range(2)]
            vnfs = [alloc(f"vnf{i}", [128, D]) for i in range(2)]
            vns = [alloc(f"vn{i}", [128, D], BF16) for i in range(2)]
            gated = alloc("gated", [128, 8, 128], BF16)
            tmp = alloc("tmpm", [128, 4, 128])
            bsts = [alloc(f"bst{i}", [128, 2, 6]) for i in range(2)]
            mvs = [alloc(f"mv{i}", [128, 2]) for i in range(2)]
            rs1s = [alloc(f"rs1{i}", [128, 1]) for i in range(2)]
            bcnt = [0]
            scr = norm_scratch()
            wv = wio_d.rearrange("(c p) n -> p c n", p=128)
            for a_ in range(0, 2048, 1024):
                S.dma("pool", lambda e, a_=a_: e.dma_start(out=Wi1.ap[:, :, a_:a_ + 1024], in_=wv[:, :, a_:a_ + 1024]), W=[Wi1.b])
            load_w(Wo1, woo_d.rearrange("(c p) n -> p c n", p=128))
            S.dma("sp", lambda e: e.dma_start(out=swt.ap, in_=sw_d.rearrange("g t s -> t g s")), W=[swt.b])
            S.dma("sp", lambda e: e.dma_start(out=BS.ap, in_=sb_d), W=[BS.b])
            S.dma("sp", lambda e: e.dma_start(out=sw0.ap, in_=sw0_d), W=[sw0.b])
            S.dma("sp", lambda e: e.dma_start(out=sb0.ap, in_=sb0_d), W=[sb0.b])
            S.dma("sp", lambda e: e.dma_start(out=sln.ap, in_=sln_d), W=[sln.b])
            S.op("pool", lambda e: e.affine_select(out=swt.ap, in_=swt.ap, pattern=[[0, 8], [-1, 128]], compare_op=ALU.is_ge, fill=0.0, base=0, channel_multiplier=1),
                 R=[], W=[swt.b])
            for g in range(8):
                def one(g):
                    bk = nbank(0, 8)
                    S.op("pe", lambda e: e.transpose(out=bk.ap[:, 0:128], in_=swt.ap[:, g, :], identity=ident_f.ap), R=[swt.b, ident_f.b], W=[bk.b])
                    S.op("act", lambda e: e.activation(out=WcT.ap[:, g, :], in_=bk.ap[:, 0:128], func=AF.Copy), R=[], W=[bk.b, WcT.b])
                    S.op("dve", lambda e: e.tensor_scalar(out=WcTs.ap[:, g, :], in0=ident_f.ap, scalar1=sw0.ap[:, g:g + 1], scalar2=None, op0=ALU.mult), R=[ident_f.b, sw0.b], W=[WcTs.b])
                one(g)

            def superblock(sbi):
                b0, nb_ = SBS[sbi]
                nt = nb_ * 128
                samp = (sbi == 4)
                W_ = WcTs if samp else WcT
                for bl in range(nb_):
                    def nb1(bl):
                        b = b0 + bl
                        norm_T(Tl(xs_.ap[:, b, :], xb[b]), 2, hTs.ap[:, :, bl * 128:(bl + 1) * 128], [hTs.b], scr)
                    nb1(bl)
                for ch in range(8):
                    def uch(ch):
                        bk = nbank(0, 8)
                        for c in range(8):
                            S.op("pe", lambda e, c=c: e.matmul(bk.ap[:, 0:nt], lhsT=Wi1.ap[:, c, ch * 128:(ch + 1) * 128], rhs=hTs.ap[:, c, 0:nt], start=(c == 0), stop=(c == 7)),
                                 R=[Wi1.b, hTs.b], W=[bk.b])
                        S.op("act", lambda e: e.activation(out=uT.ap[:, ch, 0:nt], in_=bk.ap[:, 0:nt], func=AF.Gelu), R=[], W=[bk.b, uT.b])
                    uch(ch)

                def stage1(bl):
                    i = bcnt[0] % 2
                    bcnt[0] += 1
                    vt, vnf, vn, bst, mv, rs1 = vts[i], vnfs[i], vns[i], bsts[i], mvs[i], rs1s[i]
                    for hf in range(2):
                        def vh(hf):
                            bk = nbank(0, 8)
                            for c in range(8):
                                S.op("pe", lambda e, c=c: e.matmul(bk.ap, lhsT=hTs.ap[:, c, bl * 128:(bl + 1) * 128], rhs=Wi1.ap[:, c, D + hf * 512:D + (hf + 1) * 512], start=(c == 0), stop=(c == 7)),
                                     R=[Wi1.b, hTs.b], W=[bk.b])
                            S.op("act", lambda e: e.activation(out=vt.ap[:, hf * 512:(hf + 1) * 512], in_=bk.ap, func=AF.Gelu), R=[], W=[bk.b, vt.b])
                            S.op("dve", lambda e: e.bn_stats(out=bst.ap[:, hf, :], in_=vt.ap[:, hf * 512:(hf + 1) * 512]), R=[vt.b], W=[bst.b])
                        vh(hf)
                    S.op("dve", lambda e: e.bn_aggr(out=mv.ap, in_=bst.ap), R=[bst.b], W=[mv.b])
                    S.op("act", lambda e: e.activation(out=rs1.ap, in_=mv.ap[:, 1:2], func=AF.Sqrt, bias=epsl.ap), R=[mv.b, epsl.b], W=[rs1.b])
                    S.op("dve", lambda e: e.reciprocal(out=rs1.ap, in_=rs1.ap), R=[], W=[rs1.b])
                    S.op("dve", lambda e: e.tensor_scalar(out=vnf.ap, in0=vt.ap, scalar1=mv.ap[:, 0:1], scalar2=rs1.ap, op0=ALU.subtract, op1=ALU.mult), R=[vt.b, mv.b, rs1.b], W=[vnf.b])
                    S.op("dve", lambda e: e.tensor_tensor(out=vnf.ap, in0=vnf.ap, in1=sln.ap[:, 0, :], op=ALU.mult), R=[sln.b], W=[vnf.b])
                    S.op("dve", lambda e: e.tensor_tensor(out=vnf.ap, in0=vnf.ap, in1=sln.ap[:, 1, :], op=ALU.add), R=[sln.b], W=[vnf.b])
                    S.op("act", lambda e: e.activation(out=vn.ap, in_=vnf.ap, func=AF.Copy), R=[vnf.b], W=[vn.b])
                    if samp:
                        S.dma("sp", lambda e: e.dma_start(out=sv_d, in_=vnf.ap[0:8, :]), R=[vnf.b])
                    return vn

                def stage2(bl, vn):
                    b = b0 + bl
                    for hh in range(2):
                        def mix(hh):
                            bk = nbank(0, 8)
                            bv = bk.ap.rearrange("p (a b) -> p a b", a=4)
                            for g4 in range(4):
                                g = hh * 4 + g4
                                S.op("pe", lambda e, g=g, g4=g4: e.matmul(bv[:, g4, :], lhsT=vn.ap[:, g * 128:(g + 1) * 128], rhs=W_.ap[:, g, :], start=(g4 == 0), stop=False, skip_group_check=True),
                                     R=[vn.b, W_.b], W=[bk.b])
                            if samp:
                                bias_ap = sb0.ap[:, hh * 4:hh * 4 + 4].unsqueeze(2).to_broadcast([128, 4, 128])
                                bias_b = sb0.b
                            else:
                                bias_ap = BS.ap[:, hh * 4:hh * 4 + 4, :]
                                bias_b = BS.b
                            S.op("dve", lambda e: e.tensor_tensor(out=tmp.ap, in0=bv, in1=bias_ap, op=ALU.add), R=[bias_b], W=[bk.b, tmp.b])
                            S.op("dve", lambda e: e.tensor_tensor(out=gated.ap[:, hh * 4:hh * 4 + 4, :], in0=tmp.ap, in1=uT.ap[:, hh * 4:hh * 4 + 4, bl * 128:(bl + 1) * 128], op=ALU.mult),
                                 R=[tmp.b, uT.b], W=[gated.b])
                        mix(hh)
                    for hf in range(2):
                        def oh(hf):
                            bk = nbank(0, 8)
                            for c in range(8):
                                S.op("pe", lambda e, c=c: e.matmul(bk.ap, lhsT=gated.ap[:, c, :], rhs=Wo1.ap[:, c, hf * 512:(hf + 1) * 512], start=(c == 0), stop=(c == 7)),
                                     R=[gated.b, Wo1.b], W=[bk.b])
                            S.op("dve", lambda e: e.tensor_tensor(out=xs_.ap[:, b, hf * 512:(hf + 1) * 512], in0=bk.ap, in1=xs_.ap[:, b, hf * 512:(hf + 1) * 512], op=ALU.add),
                                 R=[], W=[bk.b, xb[b]])
                        oh(hf)

                vn_next = stage1(0)
                for bl in range(nb_):
                    vn_cur = vn_next
                    if bl + 1 < nb_:
                        vn_next = stage1(bl + 1)
                    stage2(bl, vn_cur)

            for sbi in range(5):
                superblock(sbi)
            top[0] = mark

        def phase_attn():
            top[0] = base_top
            q1 = alloc("q1", [32, 64])
            k1 = alloc("k1", [32, 64])
            v1 = alloc("v1", [32, 64])
            lf1 = alloc("lf1", [32, 1])
            ptT = alloc("ptT", [128, 32], I32)
            oo = alloc("oo", [32, 64])
            keep = top[0]
            Wc = alloc("Wc", [128, 8, 196], BF16)
            xs_t = alloc("xs_t", [128, D])
            hT = alloc("hT_s", [128, 8, 128], BF16)
            lz = alloc("lzs", [128, 1])
            bfc = alloc("bfc", [128, 1])
            scr = norm_scratch()
            S.dma("pool", lambda e: e.dma_start(out=Wc.ap, in_=wsm_d.rearrange("(c p) n -> p c n", p=128)), W=[Wc.b])
            S.dma("sp", lambda e: e.dma_start(out=xs_t.ap, in_=xs_d), W=[xs_t.b])
            S.dma("sp", lambda e: e.dma_start(out=bfc.ap, in_=bfc_d), W=[bfc.b])
            S.dma("sp", lambda e: e.dma_start(out=ptT.ap, in_=ptT_d), W=[ptT.b])
            norm_T(xs_t, 0, hT.ap, [hT.b], scr)
            bk = nbank(0, 8)
            for c in range(8):
                S.op("pe", lambda e, c=c: e.matmul(bk.ap[:, 0:196], lhsT=hT.ap[:, c, :], rhs=Wc.ap[:, c, :], start=(c == 0), stop=(c == 7)), R=[hT.b, Wc.b], W=[bk.b])
            S.op("act", lambda e: e.activation(out=q1.ap, in_=bk.ap[0:32, 0:64], func=AF.Copy, scale=0.125), R=[], W=[bk.b, q1.b])
            S.op("act", lambda e: e.activation(out=k1.ap, in_=bk.ap[0:32, 64:128], func=AF.Copy), R=[], W=[bk.b, k1.b])
            S.op("act", lambda e: e.activation(out=v1.ap, in_=bk.ap[0:32, 128:192], func=AF.Copy), R=[], W=[bk.b, v1.b])
            S.op("dve", lambda e: e.tensor_tensor(out=lz.ap, in0=bk.ap[:, 192:193], in1=bfc.ap, op=ALU.add), R=[bfc.b], W=[lz.b, bk.b])
            S.op("act", lambda e: e.activation(out=lz.ap, in_=lz.ap, func=AF.Exp, scale=-1.0), R=[], W=[lz.b])
            S.op("act", lambda e: e.activation(out=lz.ap, in_=lz.ap, func=AF.Ln, bias=one1.ap), R=[one1.b], W=[lz.b])
            S.op("dve", lambda e: e.tensor_scalar(out=lf1.ap, in0=lz.ap[0:32, :], scalar1=-1.0, scalar2=None, op0=ALU.mult), R=[lz.b], W=[lf1.b])
            S.barrier()
            top[0] = keep
            sample_attention(32, 1, q1, k1, v1, lf1, ptT, kc_d, vc_d, lc_d, oo)
            S.dma("sp", lambda e: e.dma_start(out=oa_d, in_=oo.ap), R=[oo.b])

        if not MAIN:
            phase_attn()
            S.emit(sems, dsems, block)
            return nc
        phase_S0()
        S.barrier()
        top[0] = base_top
        phase_A()
        if nphase <= 1:
            top[0] = base_top
            t = alloc("dbg", [128, D])
            for i in range(NB):
                S.dma("sp", lambda e, i=i: e.dma_start(out=t.ap, in_=xres_d[i * 128:(i + 1) * 128, :]), W=[t.b])
                S.dma("sp", lambda e, i=i: e.dma_start(out=y_d[i * 128:(i + 1) * 128, :], in_=t.ap), R=[t.b])
        else:
            S.barrier()
            limit[0] = AW
            res = setup_resident()
            phase_ffn(0, True, False, res)
            S.barrier()
            phase_sgu(res)
            S.barrier()
            phase_ffn(1, False, True, res)
        S.emit(sems, dsems, block)
    return nc


def make_in_maps(inp, attn2):
    f = np.float32
    xp = np.asarray(inp["x_prompt"], f)
    xs = np.zeros((128, D), f)
    xs[:32] = np.asarray(inp["x_sample"], f)[:, 0, :]

    def fm(v):
        return np.ascontiguousarray(np.asarray(v, f).reshape(8, 128).T)

    def bc(v):
        v = np.asarray(v, f)
        return np.ascontiguousarray(np.broadcast_to(v[None], (128,) + v.shape))

    gains = np.stack([fm(inp["norm_mix"][0]), fm(inp["norm_ffn"][0]), fm(inp["norm_mix"][1]), fm(inp["norm_ffn"][1])], axis=1)
    cw = np.ascontiguousarray(np.asarray(inp["conv_w"], f)[0].T.reshape(4, 128, 31).transpose(1, 0, 2))

    def c4(v):
        return np.asarray(v, f)[0].reshape(4, 128).T
    cv = np.ascontiguousarray(np.stack([c4(inp["conv_b"]), c4(inp["conv_ln_g"]), c4(inp["conv_ln_b"])], axis=1))
    common = {
        "xs": xs,
        "gains": np.ascontiguousarray(gains),
        "gfin": bc(inp["norm_final"]),
        "w_in_even": np.asarray(inp["w_in_even"], f)[0],
        "b_forget": bc(np.asarray(inp["b_forget"], f)[0]),
        "conv_w": cw,
        "conv_vec": cv,
        "w_out_even": np.asarray(inp["w_out_even"], f)[0],
        "w_in_odd": np.asarray(inp["w_in_odd"], f)[0],
        "sgu_ln": bc(np.stack([np.asarray(inp["sgu_ln_g"], f)[0], np.asarray(inp["sgu_ln_b"], f)[0]])),
        "sgu_w": np.asarray(inp["sgu_w"], f)[0],
        "sgu_b": bc(np.asarray(inp["sgu_b"], f)[0]),
        "sgu_w0": bc(np.asarray(inp["sgu_w"], f)[0][:, 0, 0]),
        "sgu_b0": bc(np.asarray(inp["sgu_b"], f)[0][:, 0]),
        "w_out_odd": np.asarray(inp["w_out_odd"], f)[0],
        "w_gate": np.asarray(inp["w_gate"], f),
        "w_up": np.asarray(inp["w_up"], f),
        "w_down": np.asarray(inp["w_down"], f),
    }
    del common["xs"]
    xsa = np.asarray(inp["x_sample"], f)[:, 0, :]
    stc = np.asarray(inp["state_conv"], f)[0]
    cwf = np.asarray(inp["conv_w"], f)[0]
    cvf = np.stack([np.asarray(inp["conv_b"], f)[0], np.asarray(inp["conv_ln_g"], f)[0], np.asarray(inp["conv_ln_b"], f)[0]])
    common["conv_wb"] = np.ascontiguousarray(np.broadcast_to(cwf[None], (8, 31, 512)))
    common["conv_vb"] = np.ascontiguousarray(np.broadcast_to(cvf[None], (8, 3, 512)))
    maps = []
    for c in range(8):
        b, hf = c // 2, c % 2
        m = dict(common)
        m["xo"] = np.ascontiguousarray(xp[b, hf * NOWN:(hf + 1) * NOWN])
        m["xc"] = np.ascontiguousarray(xp[b, 0:NOWN]) if hf == 1 else np.zeros((NOWN, D), f)
        m["ctxb"] = np.full((128, 1), 0.0 if hf == 1 else NEG, f)
        g = c // 2
        xs = np.zeros((128, D), f)
        xs[:8] = xsa[8 * g:8 * g + 8]
        m["xs"] = xs
        m["state_c"] = np.ascontiguousarray(stc[8 * g:8 * g + 8].reshape(8, 30 * 512))
        m["attn2"] = np.ascontiguousarray(attn2[8 * g:8 * g + 8])
        maps.append(m)
    return maps


def make_attn_maps(inp):
    f = np.float32
    xs = np.zeros((128, D), f)
    xs[:32] = np.asarray(inp["x_sample"], f)[:, 0, :]
    g0 = np.ascontiguousarray(np.asarray(inp["norm_mix"], f)[0].reshape(8, 128).T)
    gains = np.ascontiguousarray(np.stack([g0, g0, g0, g0], axis=1))
    wie = np.asarray(inp["w_in_even"], f)[0]
    ck = np.asarray(inp["cache_k"], f)[0]
    cvv = np.asarray(inp["cache_v"], f)[0]
    cl = np.asarray(inp["cache_logf"], f)[0]
    ptT = np.ascontiguousarray(np.asarray(inp["page_table"]).astype(np.int32).T)
    bfv = np.asarray(inp["b_forget"], f)[0]
    maps = []
    for h in range(8):
        w = np.zeros((D, 196), f)
        w[:, 0:64] = wie[:, h * 64:(h + 1) * 64]
        w[:, 64:128] = wie[:, 512 + h * 64:512 + (h + 1) * 64]
        w[:, 128:192] = wie[:, 1024 + h * 64:1024 + (h + 1) * 64]
        w[:, 192] = wie[:, 1536 + h]
        maps.append({
            "xs": xs, "gains": gains, "w_samp": w,
            "bf_c": np.full((128, 1), bfv[h], f),
            "ptT": ptT,
            "kc0": np.ascontiguousarray(ck[:, :, h, :]).reshape(5120, 8192),
            "vc0": np.ascontiguousarray(cvv[:, :, h, :]).reshape(5120, 8192),
            "lc0": np.ascontiguousarray(cl[:, :, h]),
        })
    return maps


_NC_CACHE = {}


def _prog(kind):
    if kind not in _NC_CACHE:
        _NC_CACHE[kind] = build(kind)
    return _NC_CACHE[kind]


def run(inp):
    r1 = run_bass_kernel_spmd(_prog("attn"), make_attn_maps(inp), core_ids=list(range(8))).results
    attn2 = np.ascontiguousarray(np.stack([r1[h]["o_attn"] for h in range(8)], axis=1)).reshape(32, 512)
    return run_bass_kernel_spmd(_prog("main"), make_in_maps(inp, attn2), core_ids=list(range(8))).results


def assemble(res):
    f = np.float32
    y_p = np.zeros((4, 4096, D), f)
    k_p = np.zeros((1, 4, 4096, 8, 64), f)
    v_p = np.zeros((1, 4, 4096, 8, 64), f)
    lf_p = np.zeros((1, 4, 4096, 8), f)
    cs_p = np.zeros((1, 4, 30, 512), f)
    for c in range(8):
        b, hf = c // 2, c % 2
        sl = slice(hf * NOWN, (hf + 1) * NOWN)
        r = res[c]
        y_p[b, sl] = r["y"][:NOWN]
        k_p[0, b, sl] = r["k_out"][:NOWN].reshape(NOWN, 8, 64)
        v_p[0, b, sl] = r["v_out"][:NOWN].reshape(NOWN, 8, 64)
        lf_p[0, b, sl] = r["lf_out"][:NOWN]
        if hf == 1:
            cs_p[0, b] = r["cs_out"][2:32]
    y_s = np.zeros((32, 1, D), f)
    k_s = np.zeros((1, 32, 1, 8, 64), f)
    v_s = np.zeros((1, 32, 1, 8, 64), f)
    lf_s = np.zeros((1, 32, 1, 8), f)
    cs_s = np.zeros((1, 32, 30, 512), f)
    sv_s = np.zeros((1, 32, 1, D), f)
    for g in range(4):
        r = res[2 * g]
        sl = slice(8 * g, 8 * g + 8)
        y_s[sl, 0] = r["y"][NOWN:NOWN + 8]
        k_s[0, sl, 0] = r["k_out"][NOWN:NOWN + 8].reshape(8, 8, 64)
        v_s[0, sl, 0] = r["v_out"][NOWN:NOWN + 8].reshape(8, 8, 64)
        lf_s[0, sl, 0] = r["lf_out"][NOWN:NOWN + 8]
        cs_s[0, sl] = r["css_out"]
        sv_s[0, sl, 0] = r["sv_out"]
    return (y_p, y_s, k_p, v_p, lf_p, cs_p, k_s, v_s, lf_s, cs_s, sv_s)


def kernel(**inp):
    return assemble(run(inp))
```

```python
import contextlib
import numpy as np
import concourse.bass as bass
import concourse.mybir as mybir
from concourse.bass_utils import run_bass_kernel_spmd

F32 = mybir.dt.float32
BF16 = mybir.dt.bfloat16
I32 = mybir.dt.int32
AF = mybir.ActivationFunctionType
ALU = mybir.AluOpType

ENGS = ("pe", "act", "dve", "pool", "sp")
NDMA = 48

D = 1024
NOWN = 2048
NB = 16
DIN = 2568
DFF = 2816
QCH = [(0, 6), (6, 12), (12, 17), (17, 22)]
RMS_EPS = 1e-6
LN_EPS = 1e-5
NEG = -30000.0


class Buf:
    __slots__ = ("name", "w", "r")

    def __init__(self, name):
        self.name = name
        self.w = None
        self.r = []


class Op:
    __slots__ = ("eng", "fn", "deps", "dma", "signal", "count", "waits")

    def __init__(self, eng, fn, dma=None):
        self.eng = eng
        self.fn = fn
        self.deps = set()
        self.dma = dma
        self.signal = False
        self.count = 0
        self.waits = []


class Sched:
    def __init__(self, nc):
        self.nc = nc
        self.ops = {e: [] for e in ENGS}
        self.dma_uses = [0] * NDMA
        self.dma_next = 0
        self.pending = {e: set() for e in ENGS}

    def barrier(self):
        deps = set()
        for f in ENGS:
            if self.ops[f]:
                k = len(self.ops[f]) - 1
                while k >= 0 and self.ops[f][k].dma is not None:
                    k -= 1
                if k >= 0:
                    deps.add((f, k))
        for s in range(NDMA):
            if self.dma_uses[s] > 0:
                deps.add(("dma", s, self.dma_uses[s]))
        for e in ENGS:
            self.pending[e] |= deps

    def _track(self, key, op, R, W):
        for b in R:
            if b.w is not None:
                op.deps.add(b.w)
        for b in W:
            if b.w is not None:
                op.deps.add(b.w)
            for k in b.r:
                op.deps.add(k)
        for b in R:
            b.r.append(key)
        for b in W:
            b.w = key
            b.r = []

    def op(self, eng, fn, R=(), W=()):
        o = Op(eng, fn)
        key = (eng, len(self.ops[eng]))
        self._track(key, o, R, W)
        o.deps |= self.pending[eng]
        self.pending[eng] = set()
        o.deps.discard(key)
        self.ops[eng].append(o)
        return o

    def dma(self, eng, fn, R=(), W=()):
        slot = self.dma_next
        self.dma_next = (self.dma_next + 1) % NDMA
        self.dma_uses[slot] += 1
        use = self.dma_uses[slot]
        o = Op(eng, fn, dma=(slot, use))
        key = ("dma", slot, use)
        self._track(key, o, R, W)
        o.deps |= self.pending[eng]
        self.pending[eng] = set()
        o.deps.discard(key)
        if use > 1:
            o.deps.add(("dma", slot, use - 1))
        self.ops[eng].append(o)
        return o

    def emit(self, sems, dsems, block):
        for e in ENGS:
            for o in self.ops[e]:
                for d in o.deps:
                    if d[0] != "dma":
                        if d[0] == "pe" and e == "pe":
                            continue
                        self.ops[d[0]][d[1]].signal = True
        for e in ENGS:
            c = 0
            for o in self.ops[e]:
                if o.signal and o.dma is None:
                    c += 1
                o.count = c
        for e in ENGS:
            known = {f: -1 for f in ENGS}
            kd = {}
            for o in self.ops[e]:
                need = {}
                for d in o.deps:
                    if d[0] == "dma":
                        _, slot, use = d
                        if kd.get(slot, 0) < use:
                            kd[slot] = use
                            o.waits.append((dsems[slot], 16 * use))
                    else:
                        f, k = d
                        if f == "pe" and e == "pe":
                            continue
                        if k > known[f]:
                            need[f] = max(need.get(f, -1), k)
                for f, k in need.items():
                    known[f] = k
                    o.waits.append((sems[f], self.ops[f][k].count))
        final = [(dsems[s], 16 * self.dma_uses[s]) for s in range(NDMA) if self.dma_uses[s] > 0]

        def run(e, handle):
            for o in self.ops[e]:
                for (s, v) in o.waits:
                    handle.wait_ge(s, v)
                ins = o.fn(handle)
                if o.dma is not None:
                    ins.then_inc(dsems[o.dma[0]], 16)
                elif o.signal:
                    ins.then_inc(sems[e], 1)
            if e == "sp":
                for (s, v) in final:
                    handle.wait_ge(s, v)

        @block.tensor
        def _(eng):
            run("pe", eng)

        @block.scalar
        def _(eng):
            run("act", eng)

        @block.vector
        def _(eng):
            run("dve", eng)

        @block.gpsimd
        def _(eng):
            run("pool", eng)

        @block.sync
        def _(eng):
            run("sp", eng)


class Tl:
    __slots__ = ("ap", "b")

    def __init__(self, ap, b):
        self.ap = ap
        self.b = b


def _dsize(dt):
    return 2 if dt == BF16 else 4


def build(kind="main", nphase=4):
    nc = bass.Bass("TRN2", target_bir_lowering=False)

    def din(name, shape, dt=F32):
        return nc.dram_tensor(name, list(shape), dt, kind="ExternalInput").ap()

    def dout(name, shape, dt=F32):
        return nc.dram_tensor(name, list(shape), dt, kind="ExternalOutput").ap()

    MAIN = (kind == "main")
    gains_d = din("gains", [128, 4, 8])
    xs_d = din("xs", [128, D])
    if MAIN:
        xc_d = din("xc", [NOWN, D])
        xo_d = din("xo", [NOWN, D])
        ctxb_d = din("ctxb", [128, 1])
        gfin_d = din("gfin", [128, D])
        wie_d = din("w_in_even", [D, DIN])
        bf_d = din("b_forget", [128, 8])
        cw_d = din("conv_w", [128, 4, 31])
        cv_d = din("conv_vec", [128, 3, 4])
        woe_d = din("w_out_even", [D, D])
        wio_d = din("w_in_odd", [D, 2 * D])
        sln_d = din("sgu_ln", [128, 2, D])
        sw_d = din("sgu_w", [8, 128, 128])
        sb_d = din("sgu_b", [128, 8, 128])
        sw0_d = din("sgu_w0", [128, 8])
        sb0_d = din("sgu_b0", [128, 8])
        woo_d = din("w_out_odd", [D, D])
        wg_d = din("w_gate", [2, D, DFF])
        wu_d = din("w_up", [2, D, DFF])
        wd_d = din("w_down", [2, DFF, D])
        st_d = din("state_c", [8, 30 * 512])
        cwb_d = din("conv_wb", [8, 31, 512])
        cvb_d = din("conv_vb", [8, 3, 512])
        attn2_d = din("attn2", [8, 512])
        css_d = dout("css_out", [8, 30, 512])
        y_d = dout("y", [NOWN + 128, D])
        ko_d = dout("k_out", [NOWN + 128, 512])
        vo_d = dout("v_out", [NOWN + 128, 512])
        lf_d = dout("lf_out", [NOWN + 128, 8])
        cs_d = dout("cs_out", [32, 512])
        sv_d = dout("sv_out", [8, D])
        xres_d = nc.dram_tensor("xres", [NOWN + 128, D], F32, kind="Internal").ap()
    else:
        wsm_d = din("w_samp", [D, 196])
        bfc_d = din("bf_c", [128, 1])
        ptT_d = din("ptT", [128, 32], I32)
        kc_d = [din("kc0", [5120, 8192])]
        vc_d = [din("vc0", [5120, 8192])]
        lc_d = [din("lc0", [5120, 128])]
        oa_d = dout("o_attn", [32, 64])

    S = Sched(nc)
    with contextlib.ExitStack() as st:
        AW = 53100
        arena = st.enter_context(nc.sbuf_tensor("arena", [128, AW], F32))
        top = [0]
        limit = [AW]

        def alloc(name, shape, dt=F32, nb=None):
            n = int(np.prod(shape[1:]))
            words = (n * _dsize(dt) + 3) // 4
            off = top[0]
            top[0] += words
            assert top[0] <= limit[0], (name, top[0], limit[0])
            ap = arena[0:shape[0], off:off + words]
            if dt != F32:
                ap = ap.bitcast(dt)
            if n * _dsize(dt) != words * 4:
                ap = ap[:, 0:n]
            if len(shape) == 3:
                ap = ap.rearrange("p (a b) -> p a b", a=shape[1])
            elif len(shape) == 4:
                ap = ap.rearrange("p (a b c) -> p a b c", a=shape[1], b=shape[2])
            return Tl(ap, Buf(name))

        banks = []
        for i in range(8):
            t = st.enter_context(nc.psum_tensor(f"bank{i}", [128, 512], F32))
            banks.append(Tl(t[:], Buf(f"bank{i}")))
        sems = {e: st.enter_context(nc.semaphore("s_" + e)) for e in ENGS}
        dsems = [st.enter_context(nc.semaphore(f"d{i}")) for i in range(NDMA)]
        block = st.enter_context(nc.Block())

        rr = [0]

        def nbank(lo=0, hi=6):
            b = banks[lo + rr[0] % (hi - lo)]
            rr[0] += 1
            return b

        ident_f = alloc("ident_f", [128, 128])
        ident_b = alloc("ident_b", [128, 128], BF16)
        tri_f = alloc("tri_f", [128, 128])
        tri_b = alloc("tri_b", [128, 128], BF16)
        ones_f = alloc("ones_f", [128, 128])
        o512_f = alloc("o512_f", [128, 128])
        gains = alloc("gains", [128, 4, 8])
        ctxb = alloc("ctxb", [128, 1])
        bfb = alloc("bfb", [128, 8])
        epsr = alloc("epsr", [128, 1])
        epsl = alloc("epsl", [128, 1])
        one1 = alloc("one1", [128, 1])

        S.op("pool", lambda e: e.memset(ident_f.ap, 0.0), W=[ident_f.b])
        S.op("pool", lambda e: e.affine_select(out=ident_f.ap, in_=ident_f.ap, pattern=[[-1, 128]], compare_op=ALU.not_equal,
                                               fill=1.0, base=0, channel_multiplier=1), R=[ident_f.b], W=[ident_f.b])
        S.op("pool", lambda e: e.tensor_copy(out=ident_b.ap, in_=ident_f.ap), R=[ident_f.b], W=[ident_b.b])
        S.op("pool", lambda e: e.memset(tri_f.ap, 1.0), W=[tri_f.b])
        S.op("pool", lambda e: e.affine_select(out=tri_f.ap, in_=tri_f.ap, pattern=[[1, 128]], compare_op=ALU.is_ge,
                                               fill=0.0, base=0, channel_multiplier=-1), R=[tri_f.b], W=[tri_f.b])
        S.op("pool", lambda e: e.tensor_copy(out=tri_b.ap, in_=tri_f.ap), R=[tri_f.b], W=[tri_b.b])
        S.op("pool", lambda e: e.memset(ones_f.ap, 1.0), W=[ones_f.b])
        S.op("pool", lambda e: e.memset(o512_f.ap, 1.0 / 512), W=[o512_f.b])
        S.op("pool", lambda e: e.memset(epsr.ap, RMS_EPS), W=[epsr.b])
        S.op("pool", lambda e: e.memset(epsl.ap, LN_EPS), W=[epsl.b])
        S.op("pool", lambda e: e.memset(one1.ap, 1.0), W=[one1.b])
        S.dma("sp", lambda e: e.dma_start(out=gains.ap, in_=gains_d), W=[gains.b])
        if MAIN:
            S.dma("sp", lambda e: e.dma_start(out=ctxb.ap, in_=ctxb_d), W=[ctxb.b])
            S.dma("sp", lambda e: e.dma_start(out=bfb.ap, in_=bf_d), W=[bfb.b])

        _save = top[0]
        top[0] = AW - 900
        conv2s = alloc("conv2s", [128, 512])
        o_s = alloc("o_s", [128, 256])
        uf = alloc("uf", [128, 128])
        top[0] = _save
        limit[0] = AW - 900
        S.op("pool", lambda e: e.tensor_scalar(out=uf.ap, in0=tri_f.ap, scalar1=-1.0, scalar2=1.0, op0=ALU.mult, op1=ALU.add), R=[tri_f.b], W=[uf.b])
        base_top = top[0]

        ncnt = [0]

        def norm_T(xt, gi, hT_ap, hT_bufs, scr):
            hn, ss, rstd = scr[ncnt[0] % 2]
            ncnt[0] += 1
            S.op("act", lambda e: e.activation(out=hn.ap, in_=xt.ap, func=AF.Square, accum_out=ss.ap), R=[xt.b], W=[hn.b, ss.b])
            S.op("act", lambda e: e.activation(out=rstd.ap, in_=ss.ap, func=AF.Sqrt, scale=1.0 / D, bias=epsr.ap), R=[ss.b, epsr.b], W=[rstd.b])
            S.op("dve", lambda e: e.reciprocal(out=rstd.ap, in_=rstd.ap), R=[rstd.b], W=[rstd.b])
            S.op("act", lambda e: e.activation(out=hn.ap, in_=xt.ap, func=AF.Copy, scale=rstd.ap), R=[xt.b, rstd.b], W=[hn.b])
            bk = nbank()
            pv = bk.ap.bitcast(BF16).rearrange("p (a b) -> p a b", a=8)
            for c in range(8):
                S.op("pe", lambda e, c=c: e.transpose(out=pv[:, c, :], in_=hn.ap[:, c * 128:(c + 1) * 128], identity=ident_b.ap),
                     R=[hn.b, ident_b.b], W=[bk.b])
            S.op("dve", lambda e: e.tensor_tensor(out=hT_ap, in0=pv, in1=gains.ap[:, gi, :].unsqueeze(2).to_broadcast([128, 8, 128]), op=ALU.mult),
                 R=[gains.b], W=[bk.b] + list(hT_bufs))

        def norm_scratch():
            return [(alloc(f"hn{i}", [128, D], BF16), alloc(f"ss{i}", [128, 1]), alloc(f"rstd{i}", [128, 1])) for i in range(2)]

        def load_w(dst, src_ap, engine="pool"):
            S.dma(engine, lambda e: e.dma_start(out=dst.ap, in_=src_ap), W=[dst.b])

        def phase_S0():
            top[0] = base_top
            Wi = alloc("Wi", [128, 8, DIN], BF16)
            WoS = alloc("Wo", [128, 8, D], BF16)
            glu2 = alloc("glu2", [128, 512])
            keep_top = top[0]
            xs_t = alloc("xs_t", [128, D])
            hT = alloc("hT_s", [128, 8, 128], BF16)
            stg = alloc("stg", [128, 512])
            sig = alloc("sig", [128, 512])
            lz = alloc("lzs", [128, 8])
            scr = norm_scratch()
            wv = wie_d.rearrange("(c p) n -> p c n", p=128)
            for (a, b_) in [(0, 1284), (1284, 2568)]:
                S.dma("pool", lambda e, a=a, b_=b_: e.dma_start(out=Wi.ap[:, :, a:b_], in_=wv[:, :, a:b_]), W=[Wi.b])
            load_w(WoS, woe_d.rearrange("(c p) n -> p c n", p=128))
            S.dma("sp", lambda e: e.dma_start(out=xs_t.ap, in_=xs_d), W=[xs_t.b])
            norm_T(xs_t, 0, hT.ap, [hT.b], scr)

            def proj(Wt, c0, n):
                bk = nbank(0, 8)
                for c in range(8):
                    S.op("pe", lambda e, c=c: e.matmul(bk.ap[:, 0:n], lhsT=hT.ap[:, c, :], rhs=Wt.ap[:, c, c0:c0 + n], start=(c == 0), stop=(c == 7)), R=[hT.b, Wt.b], W=[bk.b])
                return bk

            def logsig(bank, src_ap, n, bias_t, out_t):
                S.op("dve", lambda e: e.tensor_tensor(out=lz.ap[:, 0:n], in0=src_ap, in1=bias_t.ap[:, 0:n], op=ALU.add), R=[bias_t.b], W=[lz.b, bank.b])
                S.op("act", lambda e: e.activation(out=lz.ap[:, 0:n], in_=lz.ap[:, 0:n], func=AF.Exp, scale=-1.0), R=[], W=[lz.b])
                S.op("act", lambda e: e.activation(out=lz.ap[:, 0:n], in_=lz.ap[:, 0:n], func=AF.Ln, bias=one1.ap), R=[one1.b], W=[lz.b])
                S.op("dve", lambda e: e.tensor_scalar(out=out_t.ap[:, 0:n] if out_t.ap.shape[0] == 128 else out_t.ap, in0=lz.ap[0:out_t.ap.shape[0], 0:n], scalar1=-1.0, scalar2=None, op0=ALU.mult),
                     R=[lz.b], W=[out_t.b])

            bk = proj(Wi, 512, 512)
            S.op("dve", lambda e, bk=bk: e.tensor_copy(out=stg.ap, in_=bk.ap), R=[], W=[bk.b, stg.b])
            S.dma("sp", lambda e: e.dma_start(out=ko_d[NOWN:NOWN + 128, :], in_=stg.ap), R=[stg.b])
            bk = proj(Wi, 1024, 512)
            S.op("dve", lambda e, bk=bk: e.tensor_copy(out=stg.ap, in_=bk.ap), R=[], W=[bk.b, stg.b])
            S.dma("sp", lambda e: e.dma_start(out=vo_d[NOWN:NOWN + 128, :], in_=stg.ap), R=[stg.b])
            bk = proj(Wi, 1536, 8)
            lfo = alloc("lfo_s", [128, 8])
            logsig(bk, bk.ap[:, 0:8], 8, bfb, lfo)
            S.dma("sp", lambda e: e.dma_start(out=lf_d[NOWN:NOWN + 128, :], in_=lfo.ap), R=[lfo.b])
            bkg = proj(Wi, 2056, 512)
            S.op("act", lambda e: e.activation(out=sig.ap, in_=bkg.ap, func=AF.Sigmoid), R=[], W=[bkg.b, sig.b])
            bkv = proj(Wi, 1544, 512)
            S.op("dve", lambda e: e.tensor_tensor(out=glu2.ap, in0=bkv.ap, in1=sig.ap, op=ALU.mult), R=[sig.b], W=[bkv.b, glu2.b])

            S.barrier()
            top[0] = keep_top
            st = alloc("st", [8, 30, 512])
            wb = alloc("wb", [8, 31, 512])
            cvb = alloc("cvb", [8, 3, 512])
            acc = alloc("acc_s", [8, 512])
            tmp = alloc("tmp_s", [8, 512])
            bst = alloc("bst_s", [8, 6])
            mv = alloc("mv_s", [8, 2])
            rs1 = alloc("rs1_s", [8, 1])
            S.dma("sp", lambda e: e.dma_start(out=st.ap, in_=st_d.rearrange("p (j c) -> p j c", j=30)), W=[st.b])
            S.dma("sp", lambda e: e.dma_start(out=wb.ap, in_=cwb_d), W=[wb.b])
            S.dma("sp", lambda e: e.dma_start(out=cvb.ap, in_=cvb_d), W=[cvb.b])
            S.dma("sp", lambda e: e.dma_start(out=css_d[:, 0:29, :], in_=st.ap[:, 1:30, :]), R=[st.b])
            S.dma("sp", lambda e: e.dma_start(out=css_d[:, 29, :], in_=glu2.ap[0:8, :]), R=[glu2.b])
            S.op("dve", lambda e: e.tensor_tensor(out=wb.ap[:, 0:30, :], in0=st.ap, in1=wb.ap[:, 0:30, :], op=ALU.mult), R=[st.b], W=[wb.b])
            S.op("dve", lambda e: e.tensor_reduce(out=acc.ap, in_=wb.ap[:, 0:30, :].rearrange("p j c -> p c j"), axis=mybir.AxisListType.X, op=ALU.add), R=[wb.b], W=[acc.b])
            S.op("dve", lambda e: e.tensor_tensor(out=tmp.ap, in0=glu2.ap[0:8, :], in1=wb.ap[:, 30, :], op=ALU.mult), R=[glu2.b, wb.b], W=[tmp.b])
            S.op("dve", lambda e: e.tensor_tensor(out=acc.ap, in0=acc.ap, in1=tmp.ap, op=ALU.add), R=[tmp.b], W=[acc.b])
            S.op("dve", lambda e: e.tensor_tensor(out=acc.ap, in0=acc.ap, in1=cvb.ap[:, 0, :], op=ALU.add), R=[cvb.b], W=[acc.b])
            S.op("dve", lambda e: e.bn_stats(out=bst.ap, in_=acc.ap), R=[acc.b], W=[bst.b])
            S.op("dve", lambda e: e.bn_aggr(out=mv.ap, in_=bst.ap), R=[bst.b], W=[mv.b])
            S.op("act", lambda e: e.activation(out=rs1.ap, in_=mv.ap[:, 1:2], func=AF.Sqrt, bias=epsl.ap[0:8, :]), R=[mv.b, epsl.b], W=[rs1.b])
            S.op("dve", lambda e: e.reciprocal(out=rs1.ap, in_=rs1.ap), R=[], W=[rs1.b])
            S.op("dve", lambda e: e.tensor_scalar(out=acc.ap, in0=acc.ap, scalar1=mv.ap[:, 0:1], scalar2=rs1.ap, op0=ALU.subtract, op1=ALU.mult), R=[mv.b, rs1.b], W=[acc.b])
            S.op("dve", lambda e: e.tensor_tensor(out=acc.ap, in0=acc.ap, in1=cvb.ap[:, 1, :], op=ALU.mult), R=[cvb.b], W=[acc.b])
            S.op("dve", lambda e: e.tensor_tensor(out=acc.ap, in0=acc.ap, in1=cvb.ap[:, 2, :], op=ALU.add), R=[cvb.b], W=[acc.b])
            S.op("pool", lambda e: e.memset(conv2s.ap, 0.0), W=[conv2s.b])
            S.op("act", lambda e: e.activation(out=conv2s.ap[0:8, :], in_=acc.ap, func=AF.Silu), R=[acc.b], W=[conv2s.b])


        def sample_attention(NS, NH, q4, k4, v4, lf4, ptT, kcs, vcs, lcs, o_out):
            W65 = NH * 65
            Kt = [alloc(f"Kt{i}", [128, 128, 64]) for i in range(2)]
            Vt = [alloc(f"Vt{i}", [128, 128, 64]) for i in range(2)]
            pvb = [alloc(f"pvb{i}", [128, 8192], BF16) for i in range(2)]
            Ft = [alloc(f"Ft{i}", [128, 128]) for i in range(2)]
            Pf = alloc("Pf", [128, 128])
            lg = alloc("lg", [128, 128])
            pt = alloc("pt_s", [128, 128])
            bj = alloc("bj", [128, 1])
            Rr = alloc("Rr", [128, NS * NH])
            qrep = alloc("qrep", [128, NS, W65])
            qd = alloc("qd", [NS, NS, W65])
            sel = alloc("sel", [128, NS, NS], BF16)
            self_ = alloc("self", [128, NS, NS])
            osum = alloc("osum", [NS, NH, 64])
            dn = alloc("dn", [NS, NS, NH])
            den = alloc("den", [NS, NH])
            sn = alloc("sn", [NS, NH])
            pn = alloc("pn", [NS, NH])
            tq = alloc("tq", [NS, NH * 64])
            idn = ident_f.ap[0:NS, 0:NS]
            S.op("dve", lambda e: e.tensor_tensor(out=qd.ap[:, :, 0:NH * 64], in0=q4.ap.unsqueeze(1).to_broadcast([NS, NS, NH * 64]), in1=idn.unsqueeze(2).to_broadcast([NS, NS, NH * 64]), op=ALU.mult),
                 R=[q4.b, ident_f.b], W=[qd.b])
            S.op("dve", lambda e: e.tensor_tensor(out=qd.ap[:, :, NH * 64:W65], in0=lf4.ap.unsqueeze(1).to_broadcast([NS, NS, NH]), in1=idn.unsqueeze(2).to_broadcast([NS, NS, NH]), op=ALU.mult),
                 R=[lf4.b, ident_f.b], W=[qd.b])
            qdf = qd.ap.rearrange("p a b -> p (a b)")
            qrf = qrep.ap.rearrange("p a b -> p (a b)")
            tot = NS * W65
            assert tot % 5 == 0 and tot // 5 <= 512
            stp = tot // 5
            for i0_ in range(0, tot, stp):
                def rep(i0_):
                    bk = nbank(4, 8)
                    S.op("pe", lambda e: e.matmul(bk.ap[:, 0:stp], lhsT=ones_f.ap[0:NS, :], rhs=qdf[:, i0_:i0_ + stp], start=True, stop=True), R=[ones_f.b, qd.b], W=[bk.b])
                    S.op("act", lambda e: e.activation(out=qrf[:, i0_:i0_ + stp], in_=bk.ap[:, 0:stp], func=AF.Copy), R=[], W=[bk.b, qrep.b])
                rep(i0_)
            S.op("pool", lambda e: e.memset(self_.ap, 0.0), W=[self_.b])
            S.op("pool", lambda e: e.affine_select(out=self_.ap, in_=self_.ap, pattern=[[1, NS], [-1, NS]], compare_op=ALU.not_equal, fill=1.0, base=0, channel_multiplier=0),
                 R=[], W=[self_.b])
            S.op("pool", lambda e: e.tensor_copy(out=sel.ap, in_=self_.ap), R=[self_.b], W=[sel.b])
            S.op("pool", lambda e: e.memset(Rr.ap, 0.0), W=[Rr.b])
            obank = [banks[i] for i in range(NH)]
            cnt = [0]
            for b in range(NS):
                for hh in range(NH):
                    def one(b, hh):
                        i = cnt[0] % 2
                        cnt[0] += 1
                        col = b * NH + hh
                        K_, V_, F_, pv_ = Kt[i], Vt[i], Ft[i], pvb[i]
                        K2 = K_.ap.rearrange("p s d -> p (s d)")
                        V2 = V_.ap.rearrange("p s d -> p (s d)")
                        S.dma("pool", lambda e: e.indirect_dma_start(out=K2, out_offset=None, in_=kcs[hh], in_offset=bass.IndirectOffsetOnAxis(ap=ptT.ap[:, b:b + 1], axis=0)),
                              R=[ptT.b], W=[K_.b])
                        S.dma("pool", lambda e: e.indirect_dma_start(out=V2, out_offset=None, in_=vcs[hh], in_offset=bass.IndirectOffsetOnAxis(ap=ptT.ap[:, b:b + 1], axis=0)),
                              R=[ptT.b], W=[V_.b])
                        S.dma("pool", lambda e: e.indirect_dma_start(out=F_.ap, out_offset=None, in_=lcs[hh], in_offset=bass.IndirectOffsetOnAxis(ap=ptT.ap[:, b:b + 1], axis=0)),
                              R=[ptT.b], W=[F_.b])
                        S.op("pool", lambda e: e.tensor_tensor(out=K_.ap, in0=K_.ap, in1=qrep.ap[:, b, hh * 64:(hh + 1) * 64].unsqueeze(1).to_broadcast([128, 128, 64]), op=ALU.mult),
                             R=[qrep.b], W=[K_.b])
                        S.op("dve", lambda e: e.tensor_reduce(out=lg.ap, in_=K_.ap, axis=mybir.AxisListType.X, op=ALU.add), R=[K_.b], W=[lg.b])
                        S.op("dve", lambda e: e.tensor_tensor_scan(out=Pf.ap, data0=ones_f.ap, data1=F_.ap, initial=0.0, op0=ALU.mult, op1=ALU.add), R=[ones_f.b, F_.b], W=[Pf.b])
                        gb = nbank(4, 8)
                        S.op("pe", lambda e: e.matmul(gb.ap[:, 0:1], lhsT=uf.ap, rhs=Pf.ap[:, 127:128], start=True, stop=True), R=[uf.b, Pf.b], W=[gb.b])
                        S.op("dve", lambda e: e.scalar_tensor_tensor(out=bj.ap, in0=gb.ap[:, 0:1], scalar=Pf.ap[:, 127:128], in1=qrep.ap[:, b, NH * 64 + hh:NH * 64 + hh + 1], op0=ALU.add, op1=ALU.add),
                             R=[Pf.b, qrep.b], W=[gb.b, bj.b])
                        S.op("dve", lambda e: e.tensor_tensor(out=lg.ap, in0=lg.ap, in1=Pf.ap, op=ALU.subtract), R=[Pf.b], W=[lg.b])
                        S.op("act", lambda e: e.activation(out=pt.ap, in_=lg.ap, func=AF.Exp, bias=bj.ap, scale=1.0, accum_out=Rr.ap[:, col:col + 1]), R=[lg.b, bj.b], W=[pt.b, Rr.b])
                        S.op("dve", lambda e: e.tensor_tensor(out=pv_.ap.rearrange("p (s d) -> p s d", s=128), in0=V_.ap, in1=pt.ap.unsqueeze(2).to_broadcast([128, 128, 64]), op=ALU.mult),
                             R=[V_.b, pt.b], W=[pv_.b])
                        ob = obank[hh]
                        for ck in range(16):
                            S.op("pe", lambda e, ck=ck: e.matmul(ob.ap[0:NS, :], lhsT=sel.ap[:, b, :], rhs=pv_.ap[:, ck * 512:(ck + 1) * 512], start=(b == 0 and ck == 0), stop=False, skip_group_check=True),
                                 R=[sel.b, pv_.b], W=[ob.b])
                    one(b, hh)
            for hh in range(NH):
                S.op("dve", lambda e, hh=hh: e.tensor_reduce(out=osum.ap[:, hh, :], in_=obank[hh].ap[0:NS, :].rearrange("p (s d) -> p d s", s=8), axis=mybir.AxisListType.X, op=ALU.add),
                     R=[], W=[obank[hh].b, osum.b])
            db = nbank(4, 8)
            S.op("pe", lambda e: e.matmul(db.ap[0:NS, 0:NS * NH], lhsT=ones_f.ap[:, 0:NS], rhs=Rr.ap, start=True, stop=True), R=[ones_f.b, Rr.b], W=[db.b])
            S.op("dve", lambda e: e.tensor_tensor(out=dn.ap, in0=db.ap[0:NS, 0:NS * NH].rearrange("p (a b) -> p a b", a=NS), in1=idn.unsqueeze(2).to_broadcast([NS, NS, NH]), op=ALU.mult),
                 R=[ident_f.b], W=[db.b, dn.b])
            S.op("dve", lambda e: e.tensor_reduce(out=den.ap, in_=dn.ap.rearrange("p a b -> p b a"), axis=mybir.AxisListType.X, op=ALU.add), R=[dn.b], W=[den.b])
            S.op("dve", lambda e: e.tensor_tensor(out=tq.ap, in0=q4.ap, in1=k4.ap, op=ALU.mult), R=[q4.b, k4.b], W=[tq.b])
            S.op("dve", lambda e: e.tensor_reduce(out=sn.ap, in_=tq.ap.rearrange("p (h d) -> p h d", h=NH), axis=mybir.AxisListType.X, op=ALU.add), R=[tq.b], W=[sn.b])
            S.op("act", lambda e: e.activation(out=pn.ap, in_=sn.ap, func=AF.Exp), R=[sn.b], W=[pn.b])
            S.op("dve", lambda e: e.tensor_tensor(out=den.ap, in0=den.ap, in1=pn.ap, op=ALU.add), R=[pn.b], W=[den.b])
            S.op("dve", lambda e: e.reciprocal(out=den.ap, in_=den.ap), R=[], W=[den.b])
            S.op("dve", lambda e: e.tensor_tensor(out=tq.ap.rearrange("p (h d) -> p h d", h=NH), in0=v4.ap.rearrange("p (h d) -> p h d", h=NH), in1=pn.ap.unsqueeze(2).to_broadcast([NS, NH, 64]), op=ALU.mult),
                 R=[v4.b, pn.b], W=[tq.b])
            S.op("dve", lambda e: e.tensor_tensor(out=osum.ap, in0=osum.ap, in1=tq.ap.rearrange("p (h d) -> p h d", h=NH), op=ALU.add), R=[tq.b], W=[osum.b])
            S.op("dve", lambda e: e.tensor_tensor(out=o_out.ap.rearrange("p (h d) -> p h d", h=NH), in0=osum.ap, in1=den.ap.unsqueeze(2).to_broadcast([NS, NH, 64]), op=ALU.mult),
                 R=[den.b], W=[osum.b, o_out.b])

        def phase_A():
            Wi = alloc("Wi", [128, 8, DIN], BF16)
            Wo = alloc("Wo", [128, 8, D], BF16)
            kT = alloc("kT", [128, 4, 4096], BF16)
            V = alloc("V", [128, 32, 8, 65], BF16)
            CN = alloc("CN", [128, 32, 8])
            BI = alloc("BI", [128, 32, 8])
            carry = alloc("carry", [128, 8])
            cref = alloc("cref", [128, 8])
            cw = alloc("cw", [128, 4, 31])
            cv = alloc("cv", [128, 3, 4])
            hTs = alloc("hTs", [128, 8, 512], BF16)
            hb2 = Buf("hTs2")
            qT = alloc("qT", [128, 4, 512], BF16)
            glu = alloc("glu", [128, 4, 542])
            acc = alloc("acc", [128, 4, 512])
            ysq = alloc("ysq", [128, 512])
            msb = alloc("msb", [128, 512])
            rsd = alloc("rsd", [128, 512])
            convT = alloc("convT", [128, 4, 512], BF16)
            Pt = [alloc(f"P{i}", [128, 512], BF16) for i in range(3)]
            xin = [alloc(f"xin{i}", [128, D]) for i in range(2)]
            xr = [alloc(f"xr{i}", [128, D]) for i in range(2)]
            kst = alloc("kst", [128, 512])
            vst = alloc("vst", [128, 512])
            lz = alloc("lz", [128, 8])
            lfo = alloc("lfo", [128, 8])
            rd = alloc("rd", [128, 4, 1])
            cst = alloc("cst", [32, 512])
            scr = norm_scratch()
            attn_tok = Tl(hTs.ap.rearrange("p a b -> p (a b)")[:, 0:2048].rearrange("p (a b) -> p a b", a=4), hTs.b)
            attnT = Tl(hTs.ap.rearrange("p a b -> p (a b)")[:, 2048:4096].rearrange("p (a b) -> p a b", a=4), hb2)

            S.dma("sp", lambda e: e.dma_start(out=cw.ap, in_=cw_d), W=[cw.b])
            S.dma("sp", lambda e: e.dma_start(out=cv.ap, in_=cv_d), W=[cv.b])
            S.op("pool", lambda e: e.memset(V.ap, 1.0), W=[V.b])
            S.op("pool", lambda e: e.memset(carry.ap, 0.0), W=[carry.b])
            S.op("pool", lambda e: e.memset(glu.ap, 0.0), W=[glu.b])

            xcnt = [0]

            def proj_fm(col0, evac):
                bk = nbank()
                for c in range(8):
                    S.op("pe", lambda e, c=c: e.matmul(bk.ap, lhsT=Wi.ap[:, c, col0:col0 + 128], rhs=hTs.ap[:, c, :], start=(c == 0), stop=(c == 7)),
                         R=[Wi.b, hTs.b, hb2], W=[bk.b])
                evac(bk)

            def superblock(sbi, own):
                gsb = sbi if not own else 4 + sbi
                src = xo_d if own else xc_d
                def ldx(b):
                    xt = xin[(xcnt[0] + b) % 2]
                    r0 = (sbi * 4 + b) * 128
                    S.dma("act", lambda e: e.dma_start(out=xt.ap, in_=src[r0:r0 + 128, :]), W=[xt.b])
                    return xt
                xts = {0: ldx(0), 1: ldx(1)}
                for b in range(4):
                    norm_T(xts[b], 0, hTs.ap[:, :, b * 128:(b + 1) * 128], [hTs.b, hb2], scr)
                    if b + 2 < 4:
                        xts[b + 2] = ldx(b + 2)
                xcnt[0] += 4
                for p in range(4):
                    def ev(bk, p=p):
                        S.op("act", lambda e: e.activation(out=kT.ap[:, p, gsb * 512:(gsb + 1) * 512], in_=bk.ap, func=AF.Copy), R=[], W=[bk.b, kT.b])
                    proj_fm(512 + p * 128, ev)
                if own:
                    for p in range(4):
                        def ev(bk, p=p):
                            S.op("act", lambda e: e.activation(out=qT.ap[:, p, :], in_=bk.ap, func=AF.Copy, scale=0.125), R=[], W=[bk.b, qT.b])
                        proj_fm(p * 128, ev)
                if own or sbi == 3:
                    S.op("pool", lambda e: e.tensor_copy(out=glu.ap[:, :, 0:30], in_=glu.ap[:, :, 512:542]), R=[glu.b], W=[glu.b])
                    for ch in range(4):
                        def ev_g(bk, ch=ch):
                            S.op("act", lambda e: e.activation(out=glu.ap[:, ch, 30:542], in_=bk.ap, func=AF.Sigmoid), R=[], W=[bk.b, glu.b])
                        proj_fm(1544 + 512 + ch * 128, ev_g)

                        def ev_v(bk, ch=ch):
                            S.op("dve", lambda e: e.tensor_tensor(out=glu.ap[:, ch, 30:542], in0=bk.ap, in1=glu.ap[:, ch, 30:542], op=ALU.mult), R=[], W=[bk.b, glu.b])
                        proj_fm(1544 + ch * 128, ev_v)
                for b in range(4):
                    jb = gsb * 4 + b
                    r0 = (sbi * 4 + b) * 128
                    bk = nbank()
                    for c in range(8):
                        S.op("pe", lambda e, c=c, bk=bk, b=b: e.matmul(bk.ap, lhsT=hTs.ap[:, c, b * 128:(b + 1) * 128], rhs=Wi.ap[:, c, 1024:1536], start=(c == 0), stop=(c == 7)),
                             R=[Wi.b, hTs.b, hb2], W=[bk.b])
                    S.op("act", lambda e, bk=bk, jb=jb: e.activation(out=V.ap[:, jb, :, 0:64], in_=bk.ap.rearrange("p (h d) -> p h d", h=8), func=AF.Copy), R=[], W=[bk.b, V.b])
                    if own:
                        S.op("dve", lambda e, bk=bk: e.tensor_copy(out=vst.ap, in_=bk.ap), R=[], W=[bk.b, vst.b])
                        S.dma("sp", lambda e, r0=r0: e.dma_start(out=vo_d[r0:r0 + 128, :], in_=vst.ap), R=[vst.b])
                        bk2 = nbank()
                        for c in range(8):
                            S.op("pe", lambda e, c=c, bk2=bk2, b=b: e.matmul(bk2.ap, lhsT=hTs.ap[:, c, b * 128:(b + 1) * 128], rhs=Wi.ap[:, c, 512:1024], start=(c == 0), stop=(c == 7)),
                                 R=[Wi.b, hTs.b, hb2], W=[bk2.b])
                        S.op("dve", lambda e, bk2=bk2: e.tensor_copy(out=kst.ap, in_=bk2.ap), R=[], W=[bk2.b, kst.b])
                        S.dma("sp", lambda e, r0=r0: e.dma_start(out=ko_d[r0:r0 + 128, :], in_=kst.ap), R=[kst.b])
                    bk3 = nbank()
                    for c in range(8):
                        S.op("pe", lambda e, c=c, bk3=bk3, b=b: e.matmul(bk3.ap[:, 0:8], lhsT=hTs.ap[:, c, b * 128:(b + 1) * 128], rhs=Wi.ap[:, c, 1536:1544], start=(c == 0), stop=(c == 7)),
                             R=[Wi.b, hTs.b, hb2], W=[bk3.b])
                    S.op("dve", lambda e, bk3=bk3: e.tensor_tensor(out=lz.ap, in0=bk3.ap[:, 0:8], in1=bfb.ap, op=ALU.add), R=[bfb.b], W=[bk3.b, lz.b])
                    S.op("act", lambda e: e.activation(out=lz.ap, in_=lz.ap, func=AF.Exp, scale=-1.0), R=[], W=[lz.b])
                    S.op("act", lambda e: e.activation(out=lz.ap, in_=lz.ap, func=AF.Ln, bias=one1.ap), R=[one1.b], W=[lz.b])
                    if own:
                        S.op("dve", lambda e: e.tensor_scalar(out=lfo.ap, in0=lz.ap, scalar1=-1.0, scalar2=None, op0=ALU.mult), R=[lz.b], W=[lfo.b])
                        S.dma("sp", lambda e, r0=r0: e.dma_start(out=lf_d[r0:r0 + 128, :], in_=lfo.ap), R=[lfo.b])
                    if own and b == 0:
                        S.op("dve", lambda e: e.tensor_copy(out=cref.ap, in_=carry.ap), R=[carry.b], W=[cref.b])
                    bk4 = nbank()
                    S.op("pe", lambda e, bk4=bk4: e.matmul(bk4.ap[:, 0:8], lhsT=tri_f.ap, rhs=lz.ap, start=True, stop=True), R=[tri_f.b, lz.b], W=[bk4.b])
                    S.op("pe", lambda e, bk4=bk4: e.matmul(bk4.ap[:, 8:16], lhsT=ones_f.ap, rhs=lz.ap, start=False, stop=True, skip_group_check=True), R=[ones_f.b, lz.b], W=[bk4.b])
                    S.op("dve", lambda e, bk4=bk4, jb=jb: e.tensor_tensor(out=CN.ap[:, jb, :], in0=bk4.ap[:, 0:8], in1=carry.ap, op=ALU.add), R=[carry.b], W=[bk4.b, CN.b])
                    S.op("dve", lambda e, bk4=bk4: e.tensor_tensor(out=carry.ap, in0=bk4.ap[:, 8:16], in1=carry.ap, op=ALU.add), R=[], W=[bk4.b, carry.b])
                if not own:
                    return
                nj = gsb * 4 + 4
                S.op("dve", lambda e: e.tensor_tensor(out=BI.ap[:, 0:nj, :], in0=CN.ap[:, 0:nj, :], in1=cref.ap.unsqueeze(1).to_broadcast([128, nj, 8]), op=ALU.subtract),
                     R=[CN.b, cref.b], W=[BI.b])
                S.op("dve", lambda e: e.tensor_scalar(out=BI.ap[:, 0:16, :], in0=BI.ap[:, 0:16, :], scalar1=ctxb.ap, scalar2=None, op0=ALU.add), R=[ctxb.b], W=[BI.b])

                def conv_ops():
                    for ch in range(4):
                        yield lambda ch=ch: S.op("dve", lambda e: e.tensor_scalar(out=acc.ap[:, ch, :], in0=glu.ap[:, ch, 0:512], scalar1=cw.ap[:, ch, 0:1], scalar2=cv.ap[:, 0, ch:ch + 1], op0=ALU.mult, op1=ALU.add),
                                                 R=[glu.b, cw.b, cv.b], W=[acc.b])
                        for j in range(1, 31):
                            yield lambda ch=ch, j=j: S.op("dve", lambda e: e.scalar_tensor_tensor(out=acc.ap[:, ch, :], in0=glu.ap[:, ch, j:j + 512], scalar=cw.ap[:, ch, j:j + 1], in1=acc.ap[:, ch, :], op0=ALU.mult, op1=ALU.add),
                                                          R=[glu.b, cw.b], W=[acc.b])
                cgen = conv_ops()

                def conv_some(n):
                    for _ in range(n):
                        f = next(cgen, None)
                        if f is None:
                            return
                        f()

                f0 = gsb * 4
                pcnt = [0]
                def head(h):
                    p, r0 = h // 2, (h % 2) * 64
                    ob = banks[6 + h % 2]
                    ov = ob.ap[:, 0:260].rearrange("p (a b) -> p a b", a=4)
                    jobs = []
                    for j in range(f0 + 4):
                        jobs.append((j, max(0, j - f0)))

                    def do_S(j, jj):
                        sb_ = nbank(0, 3)
                        pt = Pt[pcnt[0] % 3]
                        pcnt[0] += 1
                        c0 = jj * 128
                        S.op("pe", lambda e: e.matmul(sb_.ap[:, c0:512], lhsT=kT.ap[r0:r0 + 64, p, j * 128:(j + 1) * 128], rhs=qT.ap[r0:r0 + 64, p, c0:512], start=True, stop=True),
                             R=[kT.b, qT.b], W=[sb_.b])
                        S.op("act", lambda e: e.activation(out=pt.ap[:, c0:512], in_=sb_.ap[:, c0:512], func=AF.Exp, bias=BI.ap[:, j, h:h + 1], scale=1.0), R=[BI.b], W=[sb_.b, pt.b])
                        if j >= f0:
                            S.op("pool", lambda e: e.tensor_tensor(out=pt.ap[:, c0:c0 + 128], in0=pt.ap[:, c0:c0 + 128], in1=tri_b.ap, op=ALU.mult), R=[tri_b.b], W=[pt.b])
                        return pt

                    def do_PV(j, jj, pt, first):
                        def one(qb):
                            st_ = first and qb == 0
                            S.op("pe", lambda e: e.matmul(ov[:, qb, :], lhsT=pt.ap[:, qb * 128:(qb + 1) * 128], rhs=V.ap[:, j, h, :], start=st_, stop=False, skip_group_check=True),
                                 R=[pt.b, V.b], W=[ob.b])
                        for qb in range(jj, 4):
                            one(qb)

                    pend = []
                    for (j, jj) in jobs:
                        pt = do_S(j, jj)
                        pend.append((j, jj, pt))
                        if len(pend) > 2:
                            a = pend.pop(0)
                            do_PV(a[0], a[1], a[2], a[0] == 0)
                    for a in pend:
                        do_PV(a[0], a[1], a[2], a[0] == 0)
                    S.op("dve", lambda e: e.reciprocal(out=rd.ap, in_=ov[:, :, 64:65]), R=[], W=[ob.b, rd.b])
                    S.op("dve", lambda e: e.tensor_tensor(out=attn_tok.ap[:, :, h * 64:(h + 1) * 64], in0=ov[:, :, 0:64], in1=rd.ap.to_broadcast([128, 4, 64]), op=ALU.mult),
                         R=[rd.b], W=[ob.b, attn_tok.b])

                def conv_ln():
                    bm, be = nbank(), nbank()
                    for ch in range(4):
                        S.op("pe", lambda e, ch=ch: e.matmul(bm.ap, lhsT=o512_f.ap, rhs=acc.ap[:, ch, :], start=(ch == 0), stop=(ch == 3)), R=[o512_f.b, acc.b], W=[bm.b])
                    for ch in range(4):
                        S.op("act", lambda e, ch=ch: e.activation(out=ysq.ap, in_=acc.ap[:, ch, :], func=AF.Square), R=[acc.b], W=[ysq.b])
                        S.op("pe", lambda e, ch=ch: e.matmul(be.ap, lhsT=o512_f.ap, rhs=ysq.ap, start=(ch == 0), stop=(ch == 3)), R=[o512_f.b, ysq.b], W=[be.b])
                    S.op("act", lambda e: e.activation(out=msb.ap, in_=bm.ap, func=AF.Copy), R=[], W=[bm.b, msb.b])
                    S.op("dve", lambda e: e.tensor_tensor(out=rsd.ap, in0=msb.ap, in1=msb.ap, op=ALU.mult), R=[msb.b], W=[rsd.b])
                    S.op("dve", lambda e: e.tensor_tensor(out=rsd.ap, in0=be.ap, in1=rsd.ap, op=ALU.subtract), R=[], W=[be.b, rsd.b])
                    S.op("act", lambda e: e.activation(out=rsd.ap, in_=rsd.ap, func=AF.Sqrt, bias=epsl.ap), R=[epsl.b], W=[rsd.b])
                    S.op("dve", lambda e: e.reciprocal(out=rsd.ap, in_=rsd.ap), R=[], W=[rsd.b])
                    for ch in range(4):
                        S.op("dve", lambda e, ch=ch: e.tensor_tensor(out=acc.ap[:, ch, :], in0=acc.ap[:, ch, :], in1=msb.ap, op=ALU.subtract), R=[msb.b], W=[acc.b])
                        S.op("dve", lambda e, ch=ch: e.tensor_tensor(out=acc.ap[:, ch, :], in0=acc.ap[:, ch, :], in1=rsd.ap, op=ALU.mult), R=[rsd.b], W=[acc.b])
                        S.op("act", lambda e, ch=ch: e.activation(out=convT.ap[:, ch, :], in_=acc.ap[:, ch, :], func=AF.Silu, scale=cv.ap[:, 1, ch:ch + 1], bias=cv.ap[:, 2, ch:ch + 1]),
                             R=[acc.b, cv.b], W=[convT.b])

                for h in range(8):
                    head(h)
                    conv_some(18)
                    if h == 6:
                        conv_some(1000)
                        conv_ln()

                if sbi == 3:
                    bkc = nbank()
                    for ch in range(4):
                        S.op("pe", lambda e, ch=ch: e.transpose(out=bkc.ap[0:32, ch * 128:(ch + 1) * 128], in_=glu.ap[:, ch, 510:542], identity=ident_f.ap),
                             R=[glu.b, ident_f.b], W=[bkc.b])
                    S.op("dve", lambda e: e.tensor_copy(out=cst.ap, in_=bkc.ap[0:32, :]), R=[], W=[bkc.b, cst.b])
                    S.dma("sp", lambda e: e.dma_start(out=cs_d, in_=cst.ap), R=[cst.b])

                def tr_attn(qb):
                    bk = nbank()
                    pv = bk.ap.bitcast(BF16).rearrange("p (a b) -> p a b", a=8)
                    for cc in range(4):
                        S.op("pe", lambda e, cc=cc: e.transpose(out=pv[:, cc, :], in_=attn_tok.ap[:, qb, cc * 128:(cc + 1) * 128], identity=ident_b.ap),
                             R=[attn_tok.b, ident_b.b], W=[bk.b])
                    S.op("act", lambda e: e.activation(out=attnT.ap[:, :, qb * 128:(qb + 1) * 128], in_=pv[:, 0:4, :], func=AF.Copy), R=[], W=[bk.b, attnT.b])
                for qb in range(4):
                    tr_attn(qb)
                for qb in range(4):
                    r0 = (sbi * 4 + qb) * 128
                    xt = xr[qb % 2]
                    S.dma("act", lambda e, xt=xt, r0=r0: e.dma_start(out=xt.ap, in_=xo_d[r0:r0 + 128, :]), W=[xt.b])
                    for hf in range(2):
                        bk = nbank()
                        for c in range(8):
                            lhs = attnT.ap[:, c, qb * 128:(qb + 1) * 128] if c < 4 else convT.ap[:, c - 4, qb * 128:(qb + 1) * 128]
                            S.op("pe", lambda e, c=c, lhs=lhs, bk=bk, hf=hf: e.matmul(bk.ap, lhsT=lhs, rhs=Wo.ap[:, c, hf * 512:(hf + 1) * 512], start=(c == 0), stop=(c == 7)),
                                 R=[attnT.b, convT.b, Wo.b], W=[bk.b])
                        S.op("dve", lambda e, bk=bk, xt=xt, hf=hf: e.tensor_tensor(out=xt.ap[:, hf * 512:(hf + 1) * 512], in0=bk.ap, in1=xt.ap[:, hf * 512:(hf + 1) * 512], op=ALU.add),
                             R=[], W=[bk.b, xt.b])
                    S.dma("sp", lambda e, xt=xt, r0=r0: e.dma_start(out=xres_d[r0:r0 + 128, :], in_=xt.ap), R=[xt.b])

            for sbi in range(4):
                superblock(sbi, False)
            for sbi in range(4):
                superblock(sbi, True)

            xa = alloc("xa", [8, 2, 256])
            mix = alloc("mix", [128, D], BF16)
            S.dma("sp", lambda e: e.dma_start(out=xa.ap, in_=attn2_d.rearrange("b (par d) -> b par d", par=2)), W=[xa.b])
            S.op("pool", lambda e: e.memset(mix.ap, 0.0), W=[mix.b])
            S.op("act", lambda e: e.activation(out=mix.ap[0:8, 0:512], in_=xa.ap.rearrange("p a b -> p (a b)"), func=AF.Copy), R=[xa.b], W=[mix.b])
            S.op("act", lambda e: e.activation(out=mix.ap[0:8, 512:1024], in_=conv2s.ap[0:8, :], func=AF.Copy), R=[conv2s.b], W=[mix.b])
            bk = nbank()
            pv = bk.ap.bitcast(BF16).rearrange("p (a b) -> p a b", a=8)
            for c in range(8):
                S.op("pe", lambda e, c=c: e.transpose(out=pv[:, c, :], in_=mix.ap[:, c * 128:(c + 1) * 128], identity=ident_b.ap), R=[mix.b, ident_b.b], W=[bk.b])
            S.op("act", lambda e: e.activation(out=hTs.ap[:, :, 0:128], in_=pv, func=AF.Copy), R=[], W=[bk.b, hTs.b, hb2])
            xt = xr[0]
            S.dma("sp", lambda e: e.dma_start(out=xt.ap, in_=xs_d), W=[xt.b])
            for hf in range(2):
                def s1h(hf):
                    bk2 = nbank()
                    for c in range(8):
                        S.op("pe", lambda e, c=c: e.matmul(bk2.ap, lhsT=hTs.ap[:, c, 0:128], rhs=Wo.ap[:, c, hf * 512:(hf + 1) * 512], start=(c == 0), stop=(c == 7)),
                             R=[hTs.b, hb2, Wo.b], W=[bk2.b])
                    S.op("dve", lambda e: e.tensor_tensor(out=xt.ap[:, hf * 512:(hf + 1) * 512], in0=bk2.ap, in1=xt.ap[:, hf * 512:(hf + 1) * 512], op=ALU.add), R=[], W=[bk2.b, xt.b])
                s1h(hf)
            S.dma("sp", lambda e: e.dma_start(out=xres_d[NOWN:NOWN + 128, :], in_=xt.ap), R=[xt.b])

        SBS = [(0, 4), (4, 4), (8, 4), (12, 4), (16, 1)]

        def setup_resident():
            top[0] = base_top
            xs_ = alloc("xres_sb", [128, 17, D])
            xb = [Buf(f"x{i}") for i in range(17)]
            hTa = alloc("hT_all", [128, 8, 17 * 128], BF16)
            hb = [Buf(f"hTa{i}") for i in range(5)]
            return xs_, xb, hTa, hb

        def phase_ffn(l, from_dram, final, res):
            xs_, xb, hTa, hb = res
            mark = top[0]
            gi = 1 + 2 * l
            Wg = [alloc(f"Wg{i}", [128, 8, 768], BF16) for i in range(2)]
            Wu = [alloc(f"Wu{i}", [128, 8, 768], BF16) for i in range(2)]
            Wd = [alloc(f"Wd{i}", [128, 6, D], BF16) for i in range(2)]
            actT = [alloc(f"actT{i}", [128, 6, 512], BF16) for i in range(2)]
            sg = [alloc(f"sg{i}", [128, 512]) for i in range(2)]
            scr = norm_scratch()
            if final:
                gfin = alloc("gfin", [128, D])
                yst = [alloc("yst0", [128, D])] * 2
                S.dma("sp", lambda e: e.dma_start(out=gfin.ap, in_=gfin_d), W=[gfin.b])
            wgv = wg_d[l].rearrange("(c p) n -> p c n", p=128)
            wuv = wu_d[l].rearrange("(c p) n -> p c n", p=128)
            cnt = [0]

            def load_pass(q):
                c0, c1 = QCH[q]
                n = c1 - c0
                i = q % 2
                S.dma("pool", lambda e: e.dma_start(out=Wg[i].ap[:, :, 0:n * 128], in_=wgv[:, :, c0 * 128:c1 * 128]), W=[Wg[i].b])
                S.dma("pool", lambda e: e.dma_start(out=Wu[i].ap[:, :, 0:n * 128], in_=wuv[:, :, c0 * 128:c1 * 128]), W=[Wu[i].b])
                S.dma("pool", lambda e: e.dma_start(out=Wd[i].ap[:, 0:n, :], in_=wd_d[l][c0 * 128:c1 * 128, :].rearrange("(c p) n -> p c n", p=128)), W=[Wd[i].b])

            def norms(sbi):
                b0, nb_ = SBS[sbi]
                for b in range(b0, b0 + nb_):
                    def one(b):
                        xt = Tl(xs_.ap[:, b, :], xb[b])
                        if from_dram:
                            S.dma("sp", lambda e: e.dma_start(out=xt.ap, in_=xres_d[b * 128:(b + 1) * 128, :]), W=[xt.b])
                        norm_T(xt, gi, hTa.ap[:, :, b * 128:(b + 1) * 128], [hb[sbi]], scr)
                    one(b)

            def gate_up(q, sbi):
                b0, nb_ = SBS[sbi]
                t0, nt = b0 * 128, nb_ * 128
                n = QCH[q][1] - QCH[q][0]
                i = q % 2
                a = actT[cnt[0] % 2]

                def chunk(ci):
                    bg, bu = nbank(0, 8), nbank(0, 8)
                    for c in range(8):
                        S.op("pe", lambda e, c=c: e.matmul(bg.ap[:, 0:nt], lhsT=Wg[i].ap[:, c, ci * 128:(ci + 1) * 128], rhs=hTa.ap[:, c, t0:t0 + nt], start=(c == 0), stop=(c == 7)),
                             R=[Wg[i].b, hb[sbi]], W=[bg.b])
                    for c in range(8):
                        S.op("pe", lambda e, c=c: e.matmul(bu.ap[:, 0:nt], lhsT=Wu[i].ap[:, c, ci * 128:(ci + 1) * 128], rhs=hTa.ap[:, c, t0:t0 + nt], start=(c == 0), stop=(c == 7)),
                             R=[Wu[i].b, hb[sbi]], W=[bu.b])
                    s_ = sg[ci % 2]
                    S.op("act", lambda e: e.activation(out=s_.ap[:, 0:nt], in_=bg.ap[:, 0:nt], func=AF.Silu), R=[], W=[bg.b, s_.b])
                    S.op("dve", lambda e: e.tensor_tensor(out=a.ap[:, ci, 0:nt], in0=bu.ap[:, 0:nt], in1=s_.ap[:, 0:nt], op=ALU.mult), R=[s_.b], W=[bu.b, a.b])
                for ci in range(n):
                    chunk(ci)
                cnt[0] += 1
                return a

            def down(q, sbi, a):
                b0, nb_ = SBS[sbi]
                n = QCH[q][1] - QCH[q][0]
                i = q % 2

                def blk(bl):
                    b = b0 + bl
                    for hf in range(2):
                        def half(hf):
                            bk = nbank(0, 8)
                            for ci in range(n):
                                S.op("pe", lambda e, ci=ci: e.matmul(bk.ap, lhsT=a.ap[:, ci, bl * 128:(bl + 1) * 128], rhs=Wd[i].ap[:, ci, hf * 512:(hf + 1) * 512], start=(ci == 0), stop=(ci == n - 1)),
                                     R=[a.b, Wd[i].b], W=[bk.b])
                            S.op("dve", lambda e: e.tensor_tensor(out=xs_.ap[:, b, hf * 512:(hf + 1) * 512], in0=bk.ap, in1=xs_.ap[:, b, hf * 512:(hf + 1) * 512], op=ALU.add),
                                 R=[], W=[bk.b, xb[b]])
                        half(hf)
                    if final and q == 3:
                        sq, ss, rstd = scr[b % 2]
                        yt = yst[b % 2]
                        xap = xs_.ap[:, b, :]
                        S.op("act", lambda e: e.activation(out=sq.ap, in_=xap, func=AF.Square, accum_out=ss.ap), R=[xb[b]], W=[sq.b, ss.b])
                        S.op("act", lambda e: e.activation(out=rstd.ap, in_=ss.ap, func=AF.Sqrt, scale=1.0 / D, bias=epsr.ap), R=[ss.b, epsr.b], W=[rstd.b])
                        S.op("dve", lambda e: e.reciprocal(out=rstd.ap, in_=rstd.ap), R=[rstd.b], W=[rstd.b])
                        S.op("act", lambda e: e.activation(out=yt.ap, in_=xap, func=AF.Copy, scale=rstd.ap), R=[xb[b], rstd.b], W=[yt.b])
                        S.op("dve", lambda e: e.tensor_tensor(out=yt.ap, in0=yt.ap, in1=gfin.ap, op=ALU.mult), R=[gfin.b], W=[yt.b])
                        S.dma("sp", lambda e: e.dma_start(out=y_d[b * 128:(b + 1) * 128, :], in_=yt.ap), R=[yt.b])
                for bl in range(nb_):
                    blk(bl)

            load_pass(0)
            for q in range(4):
                if q + 1 < 4:
                    load_pass(q + 1)
                prev = None
                if q == 0:
                    norms(0)
                for sbi in range(5):
                    a = gate_up(q, sbi)
                    if q == 0 and sbi + 1 < 5:
                        norms(sbi + 1)
                    if prev is not None:
                        down(q, prev[0], prev[1])
                    prev = (sbi, a)
                down(q, prev[0], prev[1])
            top[0] = mark

        def phase_sgu(res):
            xs_, xb, hTa, hb = res
            mark = top[0]
            Wi1 = alloc("Wi1", [128, 8, 2 * D], BF16)
            Wo1 = alloc("Wo1", [128, 8, D], BF16)
            swt = alloc("swt", [128, 8, 128])
            WcT = alloc("WcT", [128, 8, 128], BF16)
            WcTs = alloc("WcTs", [128, 8, 128], BF16)
            BS = alloc("BS", [128, 8, 128])
            sw0 = alloc("sw0", [128, 8])
            sb0 = alloc("sb0", [128, 8])
            sln = alloc("sln", [128, 2, D])
            hTs = Tl(hTa.ap[:, :, 0:512], Buf("hTs1"))
            uT = Tl(hTa.ap[:, :, 512:1024], Buf("uT"))
            vts = [alloc(f"vt{i}", [128, D]) for i in range(2)]
            vnfs = [alloc(f"vnf{i}", [128, D]) for i in range(2)]
            vns = [alloc(f"vn{i}", [128, D], BF16) for i in range(2)]
            gated = alloc("gated", [128, 8, 128], BF16)
            tmp = alloc("tmpm", [128, 4, 128])
            bsts = [alloc(f"bst{i}", [128, 2, 6]) for i in range(2)]
            mvs = [alloc(f"mv{i}", [128, 2]) for i in range(2)]
            rs1s = [alloc(f"rs1{i}", [128, 1]) for i in range(2)]
            bcnt = [0]
            scr = norm_scratch()
            wv = wio_d.rearrange("(c p) n -> p c n", p=128)
            for a_ in range(0, 2048, 1024):
                S.dma("pool", lambda e, a_=a_: e.dma_start(out=Wi1.ap[:, :, a_:a_ + 1024], in_=wv[:, :, a_:a_ + 1024]), W=[Wi1.b])
            load_w(Wo1, woo_d.rearrange("(c p) n -> p c n", p=128))
            S.dma("sp", lambda e: e.dma_start(out=swt.ap, in_=sw_d.rearrange("g t s -> t g s")), W=[swt.b])
            S.dma("sp", lambda e: e.dma_start(out=BS.ap, in_=sb_d), W=[BS.b])
            S.dma("sp", lambda e: e.dma_start(out=sw0.ap, in_=sw0_d), W=[sw0.b])
            S.dma("sp", lambda e: e.dma_start(out=sb0.ap, in_=sb0_d), W=[sb0.b])
            S.dma("sp", lambda e: e.dma_start(out=sln.ap, in_=sln_d), W=[sln.b])
            S.op("pool", lambda e: e.affine_select(out=swt.ap, in_=swt.ap, pattern=[[0, 8], [-1, 128]], compare_op=ALU.is_ge, fill=0.0, base=0, channel_multiplier=1),
                 R=[], W=[swt.b])
            for g in range(8):
                def one(g):
                    bk = nbank(0, 8)
                    S.op("pe", lambda e: e.transpose(out=bk.ap[:, 0:128], in_=swt.ap[:, g, :], identity=ident_f.ap), R=[swt.b, ident_f.b], W=[bk.b])
                    S.op("act", lambda e: e.activation(out=WcT.ap[:, g, :], in_=bk.ap[:, 0:128], func=AF.Copy), R=[], W=[bk.b, WcT.b])
                    S.op("dve", lambda e: e.tensor_scalar(out=WcTs.ap[:, g, :], in0=ident_f.ap, scalar1=sw0.ap[:, g:g + 1], scalar2=None, op0=ALU.mult), R=[ident_f.b, sw0.b], W=[WcTs.b])
                one(g)

            def superblock(sbi):
                b0, nb_ = SBS[sbi]
                nt = nb_ * 128
                samp = (sbi == 4)
                W_ = WcTs if samp else WcT
                for bl in range(nb_):
                    def nb1(bl):
                        b = b0 + bl
                        norm_T(Tl(xs_.ap[:, b, :], xb[b]), 2, hTs.ap[:, :, bl * 128:(bl + 1) * 128], [hTs.b], scr)
                    nb1(bl)
                for ch in range(8):
                    def uch(ch):
                        bk = nbank(0, 8)
                        for c in range(8):
                            S.op("pe", lambda e, c=c: e.matmul(bk.ap[:, 0:nt], lhsT=Wi1.ap[:, c, ch * 128:(ch + 1) * 128], rhs=hTs.ap[:, c, 0:nt], start=(c == 0), stop=(c == 7)),
                                 R=[Wi1.b, hTs.b], W=[bk.b])
                        S.op("act", lambda e: e.activation(out=uT.ap[:, ch, 0:nt], in_=bk.ap[:, 0:nt], func=AF.Gelu), R=[], W=[bk.b, uT.b])
                    uch(ch)

                def stage1(bl):
                    i = bcnt[0] % 2
                    bcnt[0] += 1
                    vt, vnf, vn, bst, mv, rs1 = vts[i], vnfs[i], vns[i], bsts[i], mvs[i], rs1s[i]
                    for hf in range(2):
                        def vh(hf):
                            bk = nbank(0, 8)
                            for c in range(8):
                                S.op("pe", lambda e, c=c: e.matmul(bk.ap, lhsT=hTs.ap[:, c, bl * 128:(bl + 1) * 128], rhs=Wi1.ap[:, c, D + hf * 512:D + (hf + 1) * 512], start=(c == 0), stop=(c == 7)),
                                     R=[Wi1.b, hTs.b], W=[bk.b])
                            S.op("act", lambda e: e.activation(out=vt.ap[:, hf * 512:(hf + 1) * 512], in_=bk.ap, func=AF.Gelu), R=[], W=[bk.b, vt.b])
                            S.op("dve", lambda e: e.bn_stats(out=bst.ap[:, hf, :], in_=vt.ap[:, hf * 512:(hf + 1) * 512]), R=[vt.b], W=[bst.b])
                        vh(hf)
                    S.op("dve", lambda e: e.bn_aggr(out=mv.ap, in_=bst.ap), R=[bst.b], W=[mv.b])
                    S.op("act", lambda e: e.activation(out=rs1.ap, in_=mv.ap[:, 1:2], func=AF.Sqrt, bias=epsl.ap), R=[mv.b, epsl.b], W=[rs1.b])
                    S.op("dve", lambda e: e.reciprocal(out=rs1.ap, in_=rs1.ap), R=[], W=[rs1.b])
                    S.op("dve", lambda e: e.tensor_scalar(out=vnf.ap, in0=vt.ap, scalar1=mv.ap[:, 0:1], scalar2=rs1.ap, op0=ALU.subtract, op1=ALU.mult), R=[vt.b, mv.b, rs1.b], W=[vnf.b])
                    S.op("dve", lambda e: e.tensor_tensor(out=vnf.ap, in0=vnf.ap, in1=sln.ap[:, 0, :], op=ALU.mult), R=[sln.b], W=[vnf.b])
                    S.op("dve", lambda e: e.tensor_tensor(out=vnf.ap, in0=vnf.ap, in1=sln.ap[:, 1, :], op=ALU.add), R=[sln.b], W=[vnf.b])
                    S.op("act", lambda e: e.activation(out=vn.ap, in_=vnf.ap, func=AF.Copy), R=[vnf.b], W=[vn.b])
                    if samp:
                        S.dma("sp", lambda e: e.dma_start(out=sv_d, in_=vnf.ap[0:8, :]), R=[vnf.b])
                    return vn

                def stage2(bl, vn):
                    b = b0 + bl
                    for hh in range(2):
                        def mix(hh):
                            bk = nbank(0, 8)
                            bv = bk.ap.rearrange("p (a b) -> p a b", a=4)
                            for g4 in range(4):
                                g = hh * 4 + g4
                                S.op("pe", lambda e, g=g, g4=g4: e.matmul(bv[:, g4, :], lhsT=vn.ap[:, g * 128:(g + 1) * 128], rhs=W_.ap[:, g, :], start=(g4 == 0), stop=False, skip_group_check=True),
                                     R=[vn.b, W_.b], W=[bk.b])
                            if samp:
                                bias_ap = sb0.ap[:, hh * 4:hh * 4 + 4].unsqueeze(2).to_broadcast([128, 4, 128])
                                bias_b = sb0.b
                            else:
                                bias_ap = BS.ap[:, hh * 4:hh * 4 + 4, :]
                                bias_b = BS.b
                            S.op("dve", lambda e: e.tensor_tensor(out=tmp.ap, in0=bv, in1=bias_ap, op=ALU.add), R=[bias_b], W=[bk.b, tmp.b])
                            S.op("dve", lambda e: e.tensor_tensor(out=gated.ap[:, hh * 4:hh * 4 + 4, :], in0=tmp.ap, in1=uT.ap[:, hh * 4:hh * 4 + 4, bl * 128:(bl + 1) * 128], op=ALU.mult),
                                 R=[tmp.b, uT.b], W=[gated.b])
                        mix(hh)
                    for hf in range(2):
                        def oh(hf):
                            bk = nbank(0, 8)
                            for c in range(8):
                                S.op("pe", lambda e, c=c: e.matmul(bk.ap, lhsT=gated.ap[:, c, :], rhs=Wo1.ap[:, c, hf * 512:(hf + 1) * 512], start=(c == 0), stop=(c == 7)),
                                     R=[gated.b, Wo1.b], W=[bk.b])
                            S.op("dve", lambda e: e.tensor_tensor(out=xs_.ap[:, b, hf * 512:(hf + 1) * 512], in0=bk.ap, in1=xs_.ap[:, b, hf * 512:(hf + 1) * 512], op=ALU.add),
                                 R=[], W=[bk.b, xb[b]])
                        oh(hf)

                vn_next = stage1(0)
                for bl in range(nb_):
                    vn_cur = vn_next
                    if bl + 1 < nb_:
                        vn_next = stage1(bl + 1)
                    stage2(bl, vn_cur)

            for sbi in range(5):
                superblock(sbi)
            top[0] = mark

        def phase_attn():
            top[0] = base_top
            q1 = alloc("q1", [32, 64])
            k1 = alloc("k1", [32, 64])
            v1 = alloc("v1", [32, 64])
            lf1 = alloc("lf1", [32, 1])
            ptT = alloc("ptT", [128, 32], I32)
            oo = alloc("oo", [32, 64])
            keep = top[0]
            Wc = alloc("Wc", [128, 8, 196], BF16)
            xs_t = alloc("xs_t", [128, D])
            hT = alloc("hT_s", [128, 8, 128], BF16)
            lz = alloc("lzs", [128, 1])
            bfc = alloc("bfc", [128, 1])
            scr = norm_scratch()
            S.dma("pool", lambda e: e.dma_start(out=Wc.ap, in_=wsm_d.rearrange("(c p) n -> p c n", p=128)), W=[Wc.b])
            S.dma("sp", lambda e: e.dma_start(out=xs_t.ap, in_=xs_d), W=[xs_t.b])
            S.dma("sp", lambda e: e.dma_start(out=bfc.ap, in_=bfc_d), W=[bfc.b])
            S.dma("sp", lambda e: e.dma_start(out=ptT.ap, in_=ptT_d), W=[ptT.b])
            norm_T(xs_t, 0, hT.ap, [hT.b], scr)
            bk = nbank(0, 8)
            for c in range(8):
                S.op("pe", lambda e, c=c: e.matmul(bk.ap[:, 0:196], lhsT=hT.ap[:, c, :], rhs=Wc.ap[:, c, :], start=(c == 0), stop=(c == 7)), R=[hT.b, Wc.b], W=[bk.b])
            S.op("act", lambda e: e.activation(out=q1.ap, in_=bk.ap[0:32, 0:64], func=AF.Copy, scale=0.125), R=[], W=[bk.b, q1.b])
            S.op("act", lambda e: e.activation(out=k1.ap, in_=bk.ap[0:32, 64:128], func=AF.Copy), R=[], W=[bk.b, k1.b])
            S.op("act", lambda e: e.activation(out=v1.ap, in_=bk.ap[0:32, 128:192], func=AF.Copy), R=[], W=[bk.b, v1.b])
            S.op("dve", lambda e: e.tensor_tensor(out=lz.ap, in0=bk.ap[:, 192:193], in1=bfc.ap, op=ALU.add), R=[bfc.b], W=[lz.b, bk.b])
            S.op("act", lambda e: e.activation(out=lz.ap, in_=lz.ap, func=AF.Exp, scale=-1.0), R=[], W=[lz.b])
            S.op("act", lambda e: e.activation(out=lz.ap, in_=lz.ap, func=AF.Ln, bias=one1.ap), R=[one1.b], W=[lz.b])
            S.op("dve", lambda e: e.tensor_scalar(out=lf1.ap, in0=lz.ap[0:32, :], scalar1=-1.0, scalar2=None, op0=ALU.mult), R=[lz.b], W=[lf1.b])
            S.barrier()
            top[0] = keep
            sample_attention(32, 1, q1, k1, v1, lf1, ptT, kc_d, vc_d, lc_d, oo)
            S.dma("sp", lambda e: e.dma_start(out=oa_d, in_=oo.ap), R=[oo.b])

        if not MAIN:
            phase_attn()
            S.emit(sems, dsems, block)
            return nc
        phase_S0()
        S.barrier()
        top[0] = base_top
        phase_A()
        if nphase <= 1:
            top[0] = base_top
            t = alloc("dbg", [128, D])
            for i in range(NB):
                S.dma("sp", lambda e, i=i: e.dma_start(out=t.ap, in_=xres_d[i * 128:(i + 1) * 128, :]), W=[t.b])
                S.dma("sp", lambda e, i=i: e.dma_start(out=y_d[i * 128:(i + 1) * 128, :], in_=t.ap), R=[t.b])
        else:
            S.barrier()
            limit[0] = AW
            res = setup_resident()
            phase_ffn(0, True, False, res)
            S.barrier()
            phase_sgu(res)
            S.barrier()
            phase_ffn(1, False, True, res)
        S.emit(sems, dsems, block)
    return nc


def make_in_maps(inp, attn2):
    f = np.float32
    xp = np.asarray(inp["x_prompt"], f)
    xs = np.zeros((128, D), f)
    xs[:32] = np.asarray(inp["x_sample"], f)[:, 0, :]

    def fm(v):
        return np.ascontiguousarray(np.asarray(v, f).reshape(8, 128).T)

    def bc(v):
        v = np.asarray(v, f)
        return np.ascontiguousarray(np.broadcast_to(v[None], (128,) + v.shape))

    gains = np.stack([fm(inp["norm_mix"][0]), fm(inp["norm_ffn"][0]), fm(inp["norm_mix"][1]), fm(inp["norm_ffn"][1])], axis=1)
    cw = np.ascontiguousarray(np.asarray(inp["conv_w"], f)[0].T.reshape(4, 128, 31).transpose(1, 0, 2))

    def c4(v):
        return np.asarray(v, f)[0].reshape(4, 128).T
    cv = np.ascontiguousarray(np.stack([c4(inp["conv_b"]), c4(inp["conv_ln_g"]), c4(inp["conv_ln_b"])], axis=1))
    common = {
        "xs": xs,
        "gains": np.ascontiguousarray(gains),
        "gfin": bc(inp["norm_final"]),
        "w_in_even": np.asarray(inp["w_in_even"], f)[0],
        "b_forget": bc(np.asarray(inp["b_forget"], f)[0]),
        "conv_w": cw,
        "conv_vec": cv,
        "w_out_even": np.asarray(inp["w_out_even"], f)[0],
        "w_in_odd": np.asarray(inp["w_in_odd"], f)[0],
        "sgu_ln": bc(np.stack([np.asarray(inp["sgu_ln_g"], f)[0], np.asarray(inp["sgu_ln_b"], f)[0]])),
        "sgu_w": np.asarray(inp["sgu_w"], f)[0],
        "sgu_b": bc(np.asarray(inp["sgu_b"], f)[0]),
        "sgu_w0": bc(np.asarray(inp["sgu_w"], f)[0][:, 0, 0]),
        "sgu_b0": bc(np.asarray(inp["sgu_b"], f)[0][:, 0]),
        "w_out_odd": np.asarray(inp["w_out_odd"], f)[0],
        "w_gate": np.asarray(inp["w_gate"], f),
        "w_up": np.asarray(inp["w_up"], f),
        "w_down": np.asarray(inp["w_down"], f),
    }
    del common["xs"]
    xsa = np.asarray(inp["x_sample"], f)[:, 0, :]
    stc = np.asarray(inp["state_conv"], f)[0]
    cwf = np.asarray(inp["conv_w"], f)[0]
    cvf = np.stack([np.asarray(inp["conv_b"], f)[0], np.asarray(inp["conv_ln_g"], f)[0], np.asarray(inp["conv_ln_b"], f)[0]])
    common["conv_wb"] = np.ascontiguousarray(np.broadcast_to(cwf[None], (8, 31, 512)))
    common["conv_vb"] = np.ascontiguousarray(np.broadcast_to(cvf[None], (8, 3, 512)))
    maps = []
    for c in range(8):
        b, hf = c // 2, c % 2
        m = dict(common)
        m["xo"] = np.ascontiguousarray(xp[b, hf * NOWN:(hf + 1) * NOWN])
        m["xc"] = np.ascontiguousarray(xp[b, 0:NOWN]) if hf == 1 else np.zeros((NOWN, D), f)
        m["ctxb"] = np.full((128, 1), 0.0 if hf == 1 else NEG, f)
        g = c // 2
        xs = np.zeros((128, D), f)
        xs[:8] = xsa[8 * g:8 * g + 8]
        m["xs"] = xs
        m["state_c"] = np.ascontiguousarray(stc[8 * g:8 * g + 8].reshape(8, 30 * 512))
        m["attn2"] = np.ascontiguousarray(attn2[8 * g:8 * g + 8])
        maps.append(m)
    return maps


def make_attn_maps(inp):
    f = np.float32
    xs = np.zeros((128, D), f)
    xs[:32] = np.asarray(inp["x_sample"], f)[:, 0, :]
    g0 = np.ascontiguousarray(np.asarray(inp["norm_mix"], f)[0].reshape(8, 128).T)
    gains = np.ascontiguousarray(np.stack([g0, g0, g0, g0], axis=1))
    wie = np.asarray(inp["w_in_even"], f)[0]
    ck = np.asarray(inp["cache_k"], f)[0]
    cvv = np.asarray(inp["cache_v"], f)[0]
    cl = np.asarray(inp["cache_logf"], f)[0]
    ptT = np.ascontiguousarray(np.asarray(inp["page_table"]).astype(np.int32).T)
    bfv = np.asarray(inp["b_forget"], f)[0]
    maps = []
    for h in range(8):
        w = np.zeros((D, 196), f)
        w[:, 0:64] = wie[:, h * 64:(h + 1) * 64]
        w[:, 64:128] = wie[:, 512 + h * 64:512 + (h + 1) * 64]
        w[:, 128:192] = wie[:, 1024 + h * 64:1024 + (h + 1) * 64]
        w[:, 192] = wie[:, 1536 + h]
        maps.append({
            "xs": xs, "gains": gains, "w_samp": w,
            "bf_c": np.full((128, 1), bfv[h], f),
            "ptT": ptT,
            "kc0": np.ascontiguousarray(ck[:, :, h, :]).reshape(5120, 8192),
            "vc0": np.ascontiguousarray(cvv[:, :, h, :]).reshape(5120, 8192),
            "lc0": np.ascontiguousarray(cl[:, :, h]),
        })
    return maps


_NC_CACHE = {}


def _prog(kind):
    if kind not in _NC_CACHE:
        _NC_CACHE[kind] = build(kind)
    return _NC_CACHE[kind]


def run(inp):
    r1 = run_bass_kernel_spmd(_prog("attn"), make_attn_maps(inp), core_ids=list(range(8))).results
    attn2 = np.ascontiguousarray(np.stack([r1[h]["o_attn"] for h in range(8)], axis=1)).reshape(32, 512)
    return run_bass_kernel_spmd(_prog("main"), make_in_maps(inp, attn2), core_ids=list(range(8))).results


def assemble(res):
    f = np.float32
    y_p = np.zeros((4, 4096, D), f)
    k_p = np.zeros((1, 4, 4096, 8, 64), f)
    v_p = np.zeros((1, 4, 4096, 8, 64), f)
    lf_p = np.zeros((1, 4, 4096, 8), f)
    cs_p = np.zeros((1, 4, 30, 512), f)
    for c in range(8):
        b, hf = c // 2, c % 2
        sl = slice(hf * NOWN, (hf + 1) * NOWN)
        r = res[c]
        y_p[b, sl] = r["y"][:NOWN]
        k_p[0, b, sl] = r["k_out"][:NOWN].reshape(NOWN, 8, 64)
        v_p[0, b, sl] = r["v_out"][:NOWN].reshape(NOWN, 8, 64)
        lf_p[0, b, sl] = r["lf_out"][:NOWN]
        if hf == 1:
            cs_p[0, b] = r["cs_out"][2:32]
    y_s = np.zeros((32, 1, D), f)
    k_s = np.zeros((1, 32, 1, 8, 64), f)
    v_s = np.zeros((1, 32, 1, 8, 64), f)
    lf_s = np.zeros((1, 32, 1, 8), f)
    cs_s = np.zeros((1, 32, 30, 512), f)
    sv_s = np.zeros((1, 32, 1, D), f)
    for g in range(4):
        r = res[2 * g]
        sl = slice(8 * g, 8 * g + 8)
        y_s[sl, 0] = r["y"][NOWN:NOWN + 8]
        k_s[0, sl, 0] = r["k_out"][NOWN:NOWN + 8].reshape(8, 8, 64)
        v_s[0, sl, 0] = r["v_out"][NOWN:NOWN + 8].reshape(8, 8, 64)
        lf_s[0, sl, 0] = r["lf_out"][NOWN:NOWN + 8]
        cs_s[0, sl] = r["css_out"]
        sv_s[0, sl, 0] = r["sv_out"]
    return (y_p, y_s, k_p, v_p, lf_p, cs_p, k_s, v_s, lf_s, cs_s, sv_s)


def kernel(**inp):
    return assemble(run(inp))
```

```python
import contextlib
import numpy as np
import concourse.bass as bass
import concourse.mybir as mybir
from concourse.bass_utils import run_bass_kernel_spmd

F32 = mybir.dt.float32
BF16 = mybir.dt.bfloat16
I32 = mybir.dt.int32
AF = mybir.ActivationFunctionType
ALU = mybir.AluOpType

ENGS = ("pe", "act", "dve", "pool", "sp")
NDMA = 48

D = 1024
NOWN = 2048
NB = 16
DIN = 2568
DFF = 2816
QCH = [(0, 6), (6, 12), (12, 17), (17, 22)]
RMS_EPS = 1e-6
LN_EPS = 1e-5
NEG = -30000.0


class Buf:
    __slots__ = ("name", "w", "r")

    def __init__(self, name):
        self.name = name
        self.w = None
        self.r = []


class Op:
    __slots__ = ("eng", "fn", "deps", "dma", "signal", "count", "waits")

    def __init__(self, eng, fn, dma=None):
        self.eng = eng
        self.fn = fn
        self.deps = set()
        self.dma = dma
        self.signal = False
        self.count = 0
        self.waits = []


class Sched:
    def __init__(self, nc):
        self.nc = nc
        self.ops = {e: [] for e in ENGS}
        self.dma_uses = [0] * NDMA
        self.dma_next = 0
        self.pending = {e: set() for e in ENGS}

    def barrier(self):
        deps = set()
        for f in ENGS:
            if self.ops[f]:
                k = len(self.ops[f]) - 1
                while k >= 0 and self.ops[f][k].dma is not None:
                    k -= 1
                if k >= 0:
                    deps.add((f, k))
        for s in range(NDMA):
            if self.dma_uses[s] > 0:
                deps.add(("dma", s, self.dma_uses[s]))
        for e in ENGS:
            self.pending[e] |= deps

    def _track(self, key, op, R, W):
        for b in R:
            if b.w is not None:
                op.deps.add(b.w)
        for b in W:
            if b.w is not None:
                op.deps.add(b.w)
            for k in b.r:
                op.deps.add(k)
        for b in R:
            b.r.append(key)
        for b in W:
            b.w = key
            b.r = []

    def op(self, eng, fn, R=(), W=()):
        o = Op(eng, fn)
        key = (eng, len(self.ops[eng]))
        self._track(key, o, R, W)
        o.deps |= self.pending[eng]
        self.pending[eng] = set()
        o.deps.discard(key)
        self.ops[eng].append(o)
        return o

    def dma(self, eng, fn, R=(), W=()):
        slot = self.dma_next
        self.dma_next = (self.dma_next + 1) % NDMA
        self.dma_uses[slot] += 1
        use = self.dma_uses[slot]
        o = Op(eng, fn, dma=(slot, use))
        key = ("dma", slot, use)
        self._track(key, o, R, W)
        o.deps |= self.pending[eng]
        self.pending[eng] = set()
        o.deps.discard(key)
        if use > 1:
            o.deps.add(("dma", slot, use - 1))
        self.ops[eng].append(o)
        return o

    def emit(self, sems, dsems, block):
        for e in ENGS:
            for o in self.ops[e]:
                for d in o.deps:
                    if d[0] != "dma":
                        if d[0] == "pe" and e == "pe":
                            continue
                        self.ops[d[0]][d[1]].signal = True
        for e in ENGS:
            c = 0
            for o in self.ops[e]:
                if o.signal and o.dma is None:
                    c += 1
                o.count = c
        for e in ENGS:
            known = {f: -1 for f in ENGS}
            kd = {}
            for o in self.ops[e]:
                need = {}
                for d in o.deps:
                    if d[0] == "dma":
                        _, slot, use = d
                        if kd.get(slot, 0) < use:
                            kd[slot] = use
                            o.waits.append((dsems[slot], 16 * use))
                    else:
                        f, k = d
                        if f == "pe" and e == "pe":
                            continue
                        if k > known[f]:
                            need[f] = max(need.get(f, -1), k)
                for f, k in need.items():
                    known[f] = k
                    o.waits.append((sems[f], self.ops[f][k].count))
        final = [(dsems[s], 16 * self.dma_uses[s]) for s in range(NDMA) if self.dma_uses[s] > 0]

        def run(e, handle):
            for o in self.ops[e]:
                for (s, v) in o.waits:
                    handle.wait_ge(s, v)
                ins = o.fn(handle)
                if o.dma is not None:
                    ins.then_inc(dsems[o.dma[0]], 16)
                elif o.signal:
                    ins.then_inc(sems[e], 1)
            if e == "sp":
                for (s, v) in final:
                    handle.wait_ge(s, v)

        @block.tensor
        def _(eng):
            run("pe", eng)

        @block.scalar
        def _(eng):
            run("act", eng)

        @block.vector
        def _(eng):
            run("dve", eng)

        @block.gpsimd
        def _(eng):
            run("pool", eng)

        @block.sync
        def _(eng):
            run("sp", eng)


class Tl:
    __slots__ = ("ap", "b")

    def __init__(self, ap, b):
        self.ap = ap
        self.b = b


def _dsize(dt):
    return 2 if dt == BF16 else 4


def build(kind="main", nphase=4):
    nc = bass.Bass("TRN2", target_bir_lowering=False)

    def din(name, shape, dt=F32):
        return nc.dram_tensor(name, list(shape), dt, kind="ExternalInput").ap()

    def dout(name, shape, dt=F32):
        return nc.dram_tensor(name, list(shape), dt, kind="ExternalOutput").ap()

    MAIN = (kind == "main")
    gains_d = din("gains", [128, 4, 8])
    xs_d = din("xs", [128, D])
    if MAIN:
        xc_d = din("xc", [NOWN, D])
        xo_d = din("xo", [NOWN, D])
        ctxb_d = din("ctxb", [128, 1])
        gfin_d = din("gfin", [128, D])
        wie_d = din("w_in_even", [D, DIN])
        bf_d = din("b_forget", [128, 8])
        cw_d = din("conv_w", [128, 4, 31])
        cv_d = din("conv_vec", [128, 3, 4])
        woe_d = din("w_out_even", [D, D])
        wio_d = din("w_in_odd", [D, 2 * D])
        sln_d = din("sgu_ln", [128, 2, D])
        sw_d = din("sgu_w", [8, 128, 128])
        sb_d = din("sgu_b", [128, 8, 128])
        sw0_d = din("sgu_w0", [128, 8])
        sb0_d = din("sgu_b0", [128, 8])
        woo_d = din("w_out_odd", [D, D])
        wg_d = din("w_gate", [2, D, DFF])
        wu_d = din("w_up", [2, D, DFF])
        wd_d = din("w_down", [2, DFF, D])
        st_d = din("state_c", [8, 30 * 512])
        cwb_d = din("conv_wb", [8, 31, 512])
        cvb_d = din("conv_vb", [8, 3, 512])
        attn2_d = din("attn2", [8, 512])
        css_d = dout("css_out", [8, 30, 512])
        y_d = dout("y", [NOWN + 128, D])
        ko_d = dout("k_out", [NOWN + 128, 512])
        vo_d = dout("v_out", [NOWN + 128, 512])
        lf_d = dout("lf_out", [NOWN + 128, 8])
        cs_d = dout("cs_out", [32, 512])
        sv_d = dout("sv_out", [8, D])
        xres_d = nc.dram_tensor("xres", [NOWN + 128, D], F32, kind="Internal").ap()
    else:
        wsm_d = din("w_samp", [D, 196])
        bfc_d = din("bf_c", [128, 1])
        ptT_d = din("ptT", [128, 32], I32)
        kc_d = [din("kc0", [5120, 8192])]
        vc_d = [din("vc0", [5120, 8192])]
        lc_d = [din("lc0", [5120, 128])]
        oa_d = dout("o_attn", [32, 64])

    S = Sched(nc)
    with contextlib.ExitStack() as st:
        AW = 53100
        arena = st.enter_context(nc.sbuf_tensor("arena", [128, AW], F32))
        top = [0]
        limit = [AW]

        def alloc(name, shape, dt=F32, nb=None):
            n = int(np.prod(shape[1:]))
            words = (n * _dsize(dt) + 3) // 4
            off = top[0]
            top[0] += words
            assert top[0] <= limit[0], (name, top[0], limit[0])
            ap = arena[0:shape[0], off:off + words]
            if dt != F32:
                ap = ap.bitcast(dt)
            if n * _dsize(dt) != words * 4:
                ap = ap[:, 0:n]
            if len(shape) == 3:
                ap = ap.rearrange("p (a b) -> p a b", a=shape[1])
            elif len(shape) == 4:
                ap = ap.rearrange("p (a b c) -> p a b c", a=shape[1], b=shape[2])
            return Tl(ap, Buf(name))

        banks = []
        for i in range(8):
            t = st.enter_context(nc.psum_tensor(f"bank{i}", [128, 512], F32))
            banks.append(Tl(t[:], Buf(f"bank{i}")))
        sems = {e: st.enter_context(nc.semaphore("s_" + e)) for e in ENGS}
        dsems = [st.enter_context(nc.semaphore(f"d{i}")) for i in range(NDMA)]
        block = st.enter_context(nc.Block())

        rr = [0]

        def nbank(lo=0, hi=6):
            b = banks[lo + rr[0] % (hi - lo)]
            rr[0] += 1
            return b

        ident_f = alloc("ident_f", [128, 128])
        ident_b = alloc("ident_b", [128, 128], BF16)
        tri_f = alloc("tri_f", [128, 128])
        tri_b = alloc("tri_b", [128, 128], BF16)
        ones_f = alloc("ones_f", [128, 128])
        o512_f = alloc("o512_f", [128, 128])
        gains = alloc("gains", [128, 4, 8])
        ctxb = alloc("ctxb", [128, 1])
        bfb = alloc("bfb", [128, 8])
        epsr = alloc("epsr", [128, 1])
        epsl = alloc("epsl", [128, 1])
        one1 = alloc("one1", [128, 1])

        S.op("pool", lambda e: e.memset(ident_f.ap, 0.0), W=[ident_f.b])
        S.op("pool", lambda e: e.affine_select(out=ident_f.ap, in_=ident_f.ap, pattern=[[-1, 128]], compare_op=ALU.not_equal,
                                               fill=1.0, base=0, channel_multiplier=1), R=[ident_f.b], W=[ident_f.b])
        S.op("pool", lambda e: e.tensor_copy(out=ident_b.ap, in_=ident_f.ap), R=[ident_f.b], W=[ident_b.b])
        S.op("pool", lambda e: e.memset(tri_f.ap, 1.0), W=[tri_f.b])
        S.op("pool", lambda e: e.affine_select(out=tri_f.ap, in_=tri_f.ap, pattern=[[1, 128]], compare_op=ALU.is_ge,
                                               fill=0.0, base=0, channel_multiplier=-1), R=[tri_f.b], W=[tri_f.b])
        S.op("pool", lambda e: e.tensor_copy(out=tri_b.ap, in_=tri_f.ap), R=[tri_f.b], W=[tri_b.b])
        S.op("pool", lambda e: e.memset(ones_f.ap, 1.0), W=[ones_f.b])
        S.op("pool", lambda e: e.memset(o512_f.ap, 1.0 / 512), W=[o512_f.b])
        S.op("pool", lambda e: e.memset(epsr.ap, RMS_EPS), W=[epsr.b])
        S.op("pool", lambda e: e.memset(epsl.ap, LN_EPS), W=[epsl.b])
        S.op("pool", lambda e: e.memset(one1.ap, 1.0), W=[one1.b])
        S.dma("sp", lambda e: e.dma_start(out=gains.ap, in_=gains_d), W=[gains.b])
        if MAIN:
            S.dma("sp", lambda e: e.dma_start(out=ctxb.ap, in_=ctxb_d), W=[ctxb.b])
            S.dma("sp", lambda e: e.dma_start(out=bfb.ap, in_=bf_d), W=[bfb.b])

        _save = top[0]
        top[0] = AW - 900
        conv2s = alloc("conv2s", [128, 512])
        o_s = alloc("o_s", [128, 256])
        uf = alloc("uf", [128, 128])
        top[0] = _save
        limit[0] = AW - 900
        S.op("pool", lambda e: e.tensor_scalar(out=uf.ap, in0=tri_f.ap, scalar1=-1.0, scalar2=1.0, op0=ALU.mult, op1=ALU.add), R=[tri_f.b], W=[uf.b])
        base_top = top[0]

        ncnt = [0]

        def norm_T(xt, gi, hT_ap, hT_bufs, scr):
            hn, ss, rstd = scr[ncnt[0] % 2]
            ncnt[0] += 1
            S.op("act", lambda e: e.activation(out=hn.ap, in_=xt.ap, func=AF.Square, accum_out=ss.ap), R=[xt.b], W=[hn.b, ss.b])
            S.op("act", lambda e: e.activation(out=rstd.ap, in_=ss.ap, func=AF.Sqrt, scale=1.0 / D, bias=epsr.ap), R=[ss.b, epsr.b], W=[rstd.b])
            S.op("dve", lambda e: e.reciprocal(out=rstd.ap, in_=rstd.ap), R=[rstd.b], W=[rstd.b])
            S.op("act", lambda e: e.activation(out=hn.ap, in_=xt.ap, func=AF.Copy, scale=rstd.ap), R=[xt.b, rstd.b], W=[hn.b])
            bk = nbank()
            pv = bk.ap.bitcast(BF16).rearrange("p (a b) -> p a b", a=8)
            for c in range(8):
                S.op("pe", lambda e, c=c: e.transpose(out=pv[:, c, :], in_=hn.ap[:, c * 128:(c + 1) * 128], identity=ident_b.ap),
                     R=[hn.b, ident_b.b], W=[bk.b])
            S.op("dve", lambda e: e.tensor_tensor(out=hT_ap, in0=pv, in1=gains.ap[:, gi, :].unsqueeze(2).to_broadcast([128, 8, 128]), op=ALU.mult),
                 R=[gains.b], W=[bk.b] + list(hT_bufs))

        def norm_scratch():
            return [(alloc(f"hn{i}", [128, D], BF16), alloc(f"ss{i}", [128, 1]), alloc(f"rstd{i}", [128, 1])) for i in range(2)]

        def load_w(dst, src_ap, engine="pool"):
            S.dma(engine, lambda e: e.dma_start(out=dst.ap, in_=src_ap), W=[dst.b])

        def phase_S0():
            top[0] = base_top
            Wi = alloc("Wi", [128, 8, DIN], BF16)
            WoS = alloc("Wo", [128, 8, D], BF16)
            glu2 = alloc("glu2", [128, 512])
            keep_top = top[0]
            xs_t = alloc("xs_t", [128, D])
            hT = alloc("hT_s", [128, 8, 128], BF16)
            stg = alloc("stg", [128, 512])
            sig = alloc("sig", [128, 512])
            lz = alloc("lzs", [128, 8])
            scr = norm_scratch()
            wv = wie_d.rearrange("(c p) n -> p c n", p=128)
            for (a, b_) in [(0, 1284), (1284, 2568)]:
                S.dma("pool", lambda e, a=a, b_=b_: e.dma_start(out=Wi.ap[:, :, a:b_], in_=wv[:, :, a:b_]), W=[Wi.b])
            load_w(WoS, woe_d.rearrange("(c p) n -> p c n", p=128))
            S.dma("sp", lambda e: e.dma_start(out=xs_t.ap, in_=xs_d), W=[xs_t.b])
            norm_T(xs_t, 0, hT.ap, [hT.b], scr)

            def proj(Wt, c0, n):
                bk = nbank(0, 8)
                for c in range(8):
                    S.op("pe", lambda e, c=c: e.matmul(bk.ap[:, 0:n], lhsT=hT.ap[:, c, :], rhs=Wt.ap[:, c, c0:c0 + n], start=(c == 0), stop=(c == 7)), R=[hT.b, Wt.b], W=[bk.b])
                return bk

            def logsig(bank, src_ap, n, bias_t, out_t):
                S.op("dve", lambda e: e.tensor_tensor(out=lz.ap[:, 0:n], in0=src_ap, in1=bias_t.ap[:, 0:n], op=ALU.add), R=[bias_t.b], W=[lz.b, bank.b])
                S.op("act", lambda e: e.activation(out=lz.ap[:, 0:n], in_=lz.ap[:, 0:n], func=AF.Exp, scale=-1.0), R=[], W=[lz.b])
                S.op("act", lambda e: e.activation(out=lz.ap[:, 0:n], in_=lz.ap[:, 0:n], func=AF.Ln, bias=one1.ap), R=[one1.b], W=[lz.b])
                S.op("dve", lambda e: e.tensor_scalar(out=out_t.ap[:, 0:n] if out_t.ap.shape[0] == 128 else out_t.ap, in0=lz.ap[0:out_t.ap.shape[0], 0:n], scalar1=-1.0, scalar2=None, op0=ALU.mult),
                     R=[lz.b], W=[out_t.b])

            bk = proj(Wi, 512, 512)
            S.op("dve", lambda e, bk=bk: e.tensor_copy(out=stg.ap, in_=bk.ap), R=[], W=[bk.b, stg.b])
            S.dma("sp", lambda e: e.dma_start(out=ko_d[NOWN:NOWN + 128, :], in_=stg.ap), R=[stg.b])
            bk = proj(Wi, 1024, 512)
            S.op("dve", lambda e, bk=bk: e.tensor_copy(out=stg.ap, in_=bk.ap), R=[], W=[bk.b, stg.b])
            S.dma("sp", lambda e: e.dma_start(out=vo_d[NOWN:NOWN + 128, :], in_=stg.ap), R=[stg.b])
            bk = proj(Wi, 1536, 8)
            lfo = alloc("lfo_s", [128, 8])
            logsig(bk, bk.ap[:, 0:8], 8, bfb, lfo)
            S.dma("sp", lambda e: e.dma_start(out=lf_d[NOWN:NOWN + 128, :], in_=lfo.ap), R=[lfo.b])
            bkg = proj(Wi, 2056, 512)
            S.op("act", lambda e: e.activation(out=sig.ap, in_=bkg.ap, func=AF.Sigmoid), R=[], W=[bkg.b, sig.b])
            bkv = proj(Wi, 1544, 512)
            S.op("dve", lambda e: e.tensor_tensor(out=glu2.ap, in0=bkv.ap, in1=sig.ap, op=ALU.mult), R=[sig.b], W=[bkv.b, glu2.b])

            S.barrier()
            top[0] = keep_top
            st = alloc("st", [8, 30, 512])
            wb = alloc("wb", [8, 31, 512])
            cvb = alloc("cvb", [8, 3, 512])
            acc = alloc("acc_s", [8, 512])
            tmp = alloc("tmp_s", [8, 512])
            bst = alloc("bst_s", [8, 6])
            mv = alloc("mv_s", [8, 2])
            rs1 = alloc("rs1_s", [8, 1])
            S.dma("sp", lambda e: e.dma_start(out=st.ap, in_=st_d.rearrange("p (j c) -> p j c", j=30)), W=[st.b])
            S.dma("sp", lambda e: e.dma_start(out=wb.ap, in_=cwb_d), W=[wb.b])
            S.dma("sp", lambda e: e.dma_start(out=cvb.ap, in_=cvb_d), W=[cvb.b])
            S.dma("sp", lambda e: e.dma_start(out=css_d[:, 0:29, :], in_=st.ap[:, 1:30, :]), R=[st.b])
            S.dma("sp", lambda e: e.dma_start(out=css_d[:, 29, :], in_=glu2.ap[0:8, :]), R=[glu2.b])
            S.op("dve", lambda e: e.tensor_tensor(out=wb.ap[:, 0:30, :], in0=st.ap, in1=wb.ap[:, 0:30, :], op=ALU.mult), R=[st.b], W=[wb.b])
            S.op("dve", lambda e: e.tensor_reduce(out=acc.ap, in_=wb.ap[:, 0:30, :].rearrange("p j c -> p c j"), axis=mybir.AxisListType.X, op=ALU.add), R=[wb.b], W=[acc.b])
            S.op("dve", lambda e: e.tensor_tensor(out=tmp.ap, in0=glu2.ap[0:8, :], in1=wb.ap[:, 30, :], op=ALU.mult), R=[glu2.b, wb.b], W=[tmp.b])
            S.op("dve", lambda e: e.tensor_tensor(out=acc.ap, in0=acc.ap, in1=tmp.ap, op=ALU.add), R=[tmp.b], W=[acc.b])
            S.op("dve", lambda e: e.tensor_tensor(out=acc.ap, in0=acc.ap, in1=cvb.ap[:, 0, :], op=ALU.add), R=[cvb.b], W=[acc.b])
            S.op("dve", lambda e: e.bn_stats(out=bst.ap, in_=acc.ap), R=[acc.b], W=[bst.b])
            S.op("dve", lambda e: e.bn_aggr(out=mv.ap, in_=bst.ap), R=[bst.b], W=[mv.b])
            S.op("act", lambda e: e.activation(out=rs1.ap, in_=mv.ap[:, 1:2], func=AF.Sqrt, bias=epsl.ap[0:8, :]), R=[mv.b, epsl.b], W=[rs1.b])
            S.op("dve", lambda e: e.reciprocal(out=rs1.ap, in_=rs1.ap), R=[], W=[rs1.b])
            S.op("dve", lambda e: e.tensor_scalar(out=acc.ap, in0=acc.ap, scalar1=mv.ap[:, 0:1], scalar2=rs1.ap, op0=ALU.subtract, op1=ALU.mult), R=[mv.b, rs1.b], W=[acc.b])
            S.op("dve", lambda e: e.tensor_tensor(out=acc.ap, in0=acc.ap, in1=cvb.ap[:, 1, :], op=ALU.mult), R=[cvb.b], W=[acc.b])
            S.op("dve", lambda e: e.tensor_tensor(out=acc.ap, in0=acc.ap, in1=cvb.ap[:, 2, :], op=ALU.add), R=[cvb.b], W=[acc.b])
            S.op("pool", lambda e: e.memset(conv2s.ap, 0.0), W=[conv2s.b])
            S.op("act", lambda e: e.activation(out=conv2s.ap[0:8, :], in_=acc.ap, func=AF.Silu), R=[acc.b], W=[conv2s.b])


        def sample_attention(NS, NH, q4, k4, v4, lf4, ptT, kcs, vcs, lcs, o_out):
            W65 = NH * 65
            Kt = [alloc(f"Kt{i}", [128, 128, 64]) for i in range(2)]
            Vt = [alloc(f"Vt{i}", [128, 128, 64]) for i in range(2)]
            pvb = [alloc(f"pvb{i}", [128, 8192], BF16) for i in range(2)]
            Ft = [alloc(f"Ft{i}", [128, 128]) for i in range(2)]
            Pf = alloc("Pf", [128, 128])
            lg = alloc("lg", [128, 128])
            pt = alloc("pt_s", [128, 128])
            bj = alloc("bj", [128, 1])
            Rr = alloc("Rr", [128, NS * NH])
            qrep = alloc("qrep", [128, NS, W65])
            qd = alloc("qd", [NS, NS, W65])
            sel = alloc("sel", [128, NS, NS], BF16)
            self_ = alloc("self", [128, NS, NS])
            osum = alloc("osum", [NS, NH, 64])
            dn = alloc("dn", [NS, NS, NH])
            den = alloc("den", [NS, NH])
            sn = alloc("sn", [NS, NH])
            pn = alloc("pn", [NS, NH])
            tq = alloc("tq", [NS, NH * 64])
            idn = ident_f.ap[0:NS, 0:NS]
            S.op("dve", lambda e: e.tensor_tensor(out=qd.ap[:, :, 0:NH * 64], in0=q4.ap.unsqueeze(1).to_broadcast([NS, NS, NH * 64]), in1=idn.unsqueeze(2).to_broadcast([NS, NS, NH * 64]), op=ALU.mult),
                 R=[q4.b, ident_f.b], W=[qd.b])
            S.op("dve", lambda e: e.tensor_tensor(out=qd.ap[:, :, NH * 64:W65], in0=lf4.ap.unsqueeze(1).to_broadcast([NS, NS, NH]), in1=idn.unsqueeze(2).to_broadcast([NS, NS, NH]), op=ALU.mult),
                 R=[lf4.b, ident_f.b], W=[qd.b])
            qdf = qd.ap.rearrange("p a b -> p (a b)")
            qrf = qrep.ap.rearrange("p a b -> p (a b)")
            tot = NS * W65
            assert tot % 5 == 0 and tot // 5 <= 512
            stp = tot // 5
            for i0_ in range(0, tot, stp):
                def rep(i0_):
                    bk = nbank(4, 8)
                    S.op("pe", lambda e: e.matmul(bk.ap[:, 0:stp], lhsT=ones_f.ap[0:NS, :], rhs=qdf[:, i0_:i0_ + stp], start=True, stop=True), R=[ones_f.b, qd.b], W=[bk.b])
                    S.op("act", lambda e: e.activation(out=qrf[:, i0_:i0_ + stp], in_=bk.ap[:, 0:stp], func=AF.Copy), R=[], W=[bk.b, qrep.b])
                rep(i0_)
            S.op("pool", lambda e: e.memset(self_.ap, 0.0), W=[self_.b])
            S.op("pool", lambda e: e.affine_select(out=self_.ap, in_=self_.ap, pattern=[[1, NS], [-1, NS]], compare_op=ALU.not_equal, fill=1.0, base=0, channel_multiplier=0),
                 R=[], W=[self_.b])
            S.op("pool", lambda e: e.tensor_copy(out=sel.ap, in_=self_.ap), R=[self_.b], W=[sel.b])
            S.op("pool", lambda e: e.memset(Rr.ap, 0.0), W=[Rr.b])
            obank = [banks[i] for i in range(NH)]
            units = [(b, hh) for b in range(NS) for hh in range(NH)]

            def gather(u):
                b, hh = units[u]
                i = u % 2
                K_, V_, F_ = Kt[i], Vt[i], Ft[i]
                K2 = K_.ap.rearrange("p s d -> p (s d)")
                V2 = V_.ap.rearrange("p s d -> p (s d)")
                S.dma("pool", lambda e: e.indirect_dma_start(out=K2, out_offset=None, in_=kcs[hh], in_offset=bass.IndirectOffsetOnAxis(ap=ptT.ap[:, b:b + 1], axis=0)),
                      R=[ptT.b], W=[K_.b])
                S.dma("pool", lambda e: e.indirect_dma_start(out=F_.ap, out_offset=None, in_=lcs[hh], in_offset=bass.IndirectOffsetOnAxis(ap=ptT.ap[:, b:b + 1], axis=0)),
                      R=[ptT.b], W=[F_.b])
                S.dma("pool", lambda e: e.indirect_dma_start(out=V2, out_offset=None, in_=vcs[hh], in_offset=bass.IndirectOffsetOnAxis(ap=ptT.ap[:, b:b + 1], axis=0)),
                      R=[ptT.b], W=[V_.b])

            def compute(u):
                b, hh = units[u]
                i = u % 2
                col = b * NH + hh
                K_, V_, F_, pv_ = Kt[i], Vt[i], Ft[i], pvb[i]
                S.op("dve", lambda e: e.tensor_tensor(out=K_.ap, in0=K_.ap, in1=qrep.ap[:, b, hh * 64:(hh + 1) * 64].unsqueeze(1).to_broadcast([128, 128, 64]), op=ALU.mult),
                     R=[qrep.b], W=[K_.b])
                S.op("dve", lambda e: e.tensor_reduce(out=lg.ap, in_=K_.ap, axis=mybir.AxisListType.X, op=ALU.add), R=[K_.b], W=[lg.b])
                S.op("dve", lambda e: e.tensor_tensor_scan(out=Pf.ap, data0=ones_f.ap, data1=F_.ap, initial=0.0, op0=ALU.mult, op1=ALU.add), R=[ones_f.b, F_.b], W=[Pf.b])
                gb = nbank(4, 8)
                S.op("pe", lambda e: e.matmul(gb.ap[:, 0:1], lhsT=uf.ap, rhs=Pf.ap[:, 127:128], start=True, stop=True), R=[uf.b, Pf.b], W=[gb.b])
                S.op("dve", lambda e: e.scalar_tensor_tensor(out=bj.ap, in0=gb.ap[:, 0:1], scalar=Pf.ap[:, 127:128], in1=qrep.ap[:, b, NH * 64 + hh:NH * 64 + hh + 1], op0=ALU.add, op1=ALU.add),
                     R=[Pf.b, qrep.b], W=[gb.b, bj.b])
                S.op("dve", lambda e: e.tensor_tensor(out=lg.ap, in0=lg.ap, in1=Pf.ap, op=ALU.subtract), R=[Pf.b], W=[lg.b])
                S.op("act", lambda e: e.activation(out=pt.ap, in_=lg.ap, func=AF.Exp, bias=bj.ap, scale=1.0, accum_out=Rr.ap[:, col:col + 1]), R=[lg.b, bj.b], W=[pt.b, Rr.b])
                S.op("dve", lambda e: e.tensor_tensor(out=pv_.ap.rearrange("p (s d) -> p s d", s=128), in0=V_.ap, in1=pt.ap.unsqueeze(2).to_broadcast([128, 128, 64]), op=ALU.mult),
                     R=[V_.b, pt.b], W=[pv_.b])
                ob = obank[hh]
                for ck in range(16):
                    S.op("pe", lambda e, ck=ck: e.matmul(ob.ap[0:NS, :], lhsT=sel.ap[:, b, :], rhs=pv_.ap[:, ck * 512:(ck + 1) * 512], start=(b == 0 and ck == 0), stop=False, skip_group_check=True),
                         R=[sel.b, pv_.b], W=[ob.b])

            gather(0)
            for u in range(len(units)):
                if u + 1 < len(units):
                    gather(u + 1)
                compute(u)
            for hh in range(NH):
                S.op("dve", lambda e, hh=hh: e.tensor_reduce(out=osum.ap[:, hh, :], in_=obank[hh].ap[0:NS, :].rearrange("p (s d) -> p d s", s=8), axis=mybir.AxisListType.X, op=ALU.add),
                     R=[], W=[obank[hh].b, osum.b])
            db = nbank(4, 8)
            S.op("pe", lambda e: e.matmul(db.ap[0:NS, 0:NS * NH], lhsT=ones_f.ap[:, 0:NS], rhs=Rr.ap, start=True, stop=True), R=[ones_f.b, Rr.b], W=[db.b])
            S.op("dve", lambda e: e.tensor_tensor(out=dn.ap, in0=db.ap[0:NS, 0:NS * NH].rearrange("p (a b) -> p a b", a=NS), in1=idn.unsqueeze(2).to_broadcast([NS, NS, NH]), op=ALU.mult),
                 R=[ident_f.b], W=[db.b, dn.b])
            S.op("dve", lambda e: e.tensor_reduce(out=den.ap, in_=dn.ap.rearrange("p a b -> p b a"), axis=mybir.AxisListType.X, op=ALU.add), R=[dn.b], W=[den.b])
            S.op("dve", lambda e: e.tensor_tensor(out=tq.ap, in0=q4.ap, in1=k4.ap, op=ALU.mult), R=[q4.b, k4.b], W=[tq.b])
            S.op("dve", lambda e: e.tensor_reduce(out=sn.ap, in_=tq.ap.rearrange("p (h d) -> p h d", h=NH), axis=mybir.AxisListType.X, op=ALU.add), R=[tq.b], W=[sn.b])
            S.op("act", lambda e: e.activation(out=pn.ap, in_=sn.ap, func=AF.Exp), R=[sn.b], W=[pn.b])
            S.op("dve", lambda e: e.tensor_tensor(out=den.ap, in0=den.ap, in1=pn.ap, op=ALU.add), R=[pn.b], W=[den.b])
            S.op("dve", lambda e: e.reciprocal(out=den.ap, in_=den.ap), R=[], W=[den.b])
            S.op("dve", lambda e: e.tensor_tensor(out=tq.ap.rearrange("p (h d) -> p h d", h=NH), in0=v4.ap.rearrange("p (h d) -> p h d", h=NH), in1=pn.ap.unsqueeze(2).to_broadcast([NS, NH, 64]), op=ALU.mult),
                 R=[v4.b, pn.b], W=[tq.b])
            S.op("dve", lambda e: e.tensor_tensor(out=osum.ap, in0=osum.ap, in1=tq.ap.rearrange("p (h d) -> p h d", h=NH), op=ALU.add), R=[tq.b], W=[osum.b])
            S.op("dve", lambda e: e.tensor_tensor(out=o_out.ap.rearrange("p (h d) -> p h d", h=NH), in0=osum.ap, in1=den.ap.unsqueeze(2).to_broadcast([NS, NH, 64]), op=ALU.mult),
                 R=[den.b], W=[osum.b, o_out.b])

        def phase_A():
            Wi = alloc("Wi", [128, 8, DIN], BF16)
            Wo = alloc("Wo", [128, 8, D], BF16)
            kT = alloc("kT", [128, 4, 4096], BF16)
            V = alloc("V", [128, 32, 8, 65], BF16)
            CN = alloc("CN", [128, 32, 8])
            BI = alloc("BI", [128, 32, 8])
            carry = alloc("carry", [128, 8])
            cref = alloc("cref", [128, 8])
            cw = alloc("cw", [128, 4, 31])
            cv = alloc("cv", [128, 3, 4])
            hTs = alloc("hTs", [128, 8, 512], BF16)
            hb2 = Buf("hTs2")
            qT = alloc("qT", [128, 4, 512], BF16)
            glu = alloc("glu", [128, 4, 542])
            acc = alloc("acc", [128, 4, 512])
            ysq = alloc("ysq", [128, 512])
            msb = alloc("msb", [128, 512])
            rsd = alloc("rsd", [128, 512])
            convT = alloc("convT", [128, 4, 512], BF16)
            Pt = [alloc(f"P{i}", [128, 512], BF16) for i in range(3)]
            xin = [alloc(f"xin{i}", [128, D]) for i in range(2)]
            xr = [alloc(f"xr{i}", [128, D]) for i in range(2)]
            kst = alloc("kst", [128, 512])
            vst = alloc("vst", [128, 512])
            lz = alloc("lz", [128, 8])
            lfo = alloc("lfo", [128, 8])
            rd = alloc("rd", [128, 4, 1])
            cst = alloc("cst", [32, 512])
            scr = norm_scratch()
            attn_tok = Tl(hTs.ap.rearrange("p a b -> p (a b)")[:, 0:2048].rearrange("p (a b) -> p a b", a=4), hTs.b)
            attnT = Tl(hTs.ap.rearrange("p a b -> p (a b)")[:, 2048:4096].rearrange("p (a b) -> p a b", a=4), hb2)

            S.dma("sp", lambda e: e.dma_start(out=cw.ap, in_=cw_d), W=[cw.b])
            S.dma("sp", lambda e: e.dma_start(out=cv.ap, in_=cv_d), W=[cv.b])
            S.op("pool", lambda e: e.memset(V.ap, 1.0), W=[V.b])
            S.op("pool", lambda e: e.memset(carry.ap, 0.0), W=[carry.b])
            S.op("pool", lambda e: e.memset(glu.ap, 0.0), W=[glu.b])

            xcnt = [0]

            def proj_fm(col0, evac):
                bk = nbank()
                for c in range(8):
                    S.op("pe", lambda e, c=c: e.matmul(bk.ap, lhsT=Wi.ap[:, c, col0:col0 + 128], rhs=hTs.ap[:, c, :], start=(c == 0), stop=(c == 7)),
                         R=[Wi.b, hTs.b, hb2], W=[bk.b])
                evac(bk)

            def superblock(sbi, own):
                gsb = sbi if not own else 4 + sbi
                src = xo_d if own else xc_d
                def ldx(b):
                    xt = xin[(xcnt[0] + b) % 2]
                    r0 = (sbi * 4 + b) * 128
                    S.dma("act", lambda e: e.dma_start(out=xt.ap, in_=src[r0:r0 + 128, :]), W=[xt.b])
                    return xt
                xts = {0: ldx(0), 1: ldx(1)}
                for b in range(4):
                    norm_T(xts[b], 0, hTs.ap[:, :, b * 128:(b + 1) * 128], [hTs.b, hb2], scr)
                    if b + 2 < 4:
                        xts[b + 2] = ldx(b + 2)
                xcnt[0] += 4
                for p in range(4):
                    def ev(bk, p=p):
                        S.op("act", lambda e: e.activation(out=kT.ap[:, p, gsb * 512:(gsb + 1) * 512], in_=bk.ap, func=AF.Copy), R=[], W=[bk.b, kT.b])
                    proj_fm(512 + p * 128, ev)
                if own:
                    for p in range(4):
                        def ev(bk, p=p):
                            S.op("act", lambda e: e.activation(out=qT.ap[:, p, :], in_=bk.ap, func=AF.Copy, scale=0.125), R=[], W=[bk.b, qT.b])
                        proj_fm(p * 128, ev)
                if own or sbi == 3:
                    S.op("pool", lambda e: e.tensor_copy(out=glu.ap[:, :, 0:30], in_=glu.ap[:, :, 512:542]), R=[glu.b], W=[glu.b])
                    for ch in range(4):
                        def ev_g(bk, ch=ch):
                            S.op("act", lambda e: e.activation(out=glu.ap[:, ch, 30:542], in_=bk.ap, func=AF.Sigmoid), R=[], W=[bk.b, glu.b])
                        proj_fm(1544 + 512 + ch * 128, ev_g)

                        def ev_v(bk, ch=ch):
                            S.op("dve", lambda e: e.tensor_tensor(out=glu.ap[:, ch, 30:542], in0=bk.ap, in1=glu.ap[:, ch, 30:542], op=ALU.mult), R=[], W=[bk.b, glu.b])
                        proj_fm(1544 + ch * 128, ev_v)
                for b in range(4):
                    jb = gsb * 4 + b
                    r0 = (sbi * 4 + b) * 128
                    bk = nbank()
                    for c in range(8):
                        S.op("pe", lambda e, c=c, bk=bk, b=b: e.matmul(bk.ap, lhsT=hTs.ap[:, c, b * 128:(b + 1) * 128], rhs=Wi.ap[:, c, 1024:1536], start=(c == 0), stop=(c == 7)),
                             R=[Wi.b, hTs.b, hb2], W=[bk.b])
                    S.op("act", lambda e, bk=bk, jb=jb: e.activation(out=V.ap[:, jb, :, 0:64], in_=bk.ap.rearrange("p (h d) -> p h d", h=8), func=AF.Copy), R=[], W=[bk.b, V.b])
                    if own:
                        S.op("dve", lambda e, bk=bk: e.tensor_copy(out=vst.ap, in_=bk.ap), R=[], W=[bk.b, vst.b])
                        S.dma("sp", lambda e, r0=r0: e.dma_start(out=vo_d[r0:r0 + 128, :], in_=vst.ap), R=[vst.b])
                        bk2 = nbank()
                        for c in range(8):
                            S.op("pe", lambda e, c=c, bk2=bk2, b=b: e.matmul(bk2.ap, lhsT=hTs.ap[:, c, b * 128:(b + 1) * 128], rhs=Wi.ap[:, c, 512:1024], start=(c == 0), stop=(c == 7)),
                                 R=[Wi.b, hTs.b, hb2], W=[bk2.b])
                        S.op("dve", lambda e, bk2=bk2: e.tensor_copy(out=kst.ap, in_=bk2.ap), R=[], W=[bk2.b, kst.b])
                        S.dma("sp", lambda e, r0=r0: e.dma_start(out=ko_d[r0:r0 + 128, :], in_=kst.ap), R=[kst.b])
                    bk3 = nbank()
                    for c in range(8):
                        S.op("pe", lambda e, c=c, bk3=bk3, b=b: e.matmul(bk3.ap[:, 0:8], lhsT=hTs.ap[:, c, b * 128:(b + 1) * 128], rhs=Wi.ap[:, c, 1536:1544], start=(c == 0), stop=(c == 7)),
                             R=[Wi.b, hTs.b, hb2], W=[bk3.b])
                    S.op("dve", lambda e, bk3=bk3: e.tensor_tensor(out=lz.ap, in0=bk3.ap[:, 0:8], in1=bfb.ap, op=ALU.add), R=[bfb.b], W=[bk3.b, lz.b])
                    S.op("act", lambda e: e.activation(out=lz.ap, in_=lz.ap, func=AF.Exp, scale=-1.0), R=[], W=[lz.b])
                    S.op("act", lambda e: e.activation(out=lz.ap, in_=lz.ap, func=AF.Ln, bias=one1.ap), R=[one1.b], W=[lz.b])
                    if own:
                        S.op("dve", lambda e: e.tensor_scalar(out=lfo.ap, in0=lz.ap, scalar1=-1.0, scalar2=None, op0=ALU.mult), R=[lz.b], W=[lfo.b])
                        S.dma("sp", lambda e, r0=r0: e.dma_start(out=lf_d[r0:r0 + 128, :], in_=lfo.ap), R=[lfo.b])
                    if own and b == 0:
                        S.op("dve", lambda e: e.tensor_copy(out=cref.ap, in_=carry.ap), R=[carry.b], W=[cref.b])
                    bk4 = nbank()
                    S.op("pe", lambda e, bk4=bk4: e.matmul(bk4.ap[:, 0:8], lhsT=tri_f.ap, rhs=lz.ap, start=True, stop=True), R=[tri_f.b, lz.b], W=[bk4.b])
                    S.op("pe", lambda e, bk4=bk4: e.matmul(bk4.ap[:, 8:16], lhsT=ones_f.ap, rhs=lz.ap, start=False, stop=True, skip_group_check=True), R=[ones_f.b, lz.b], W=[bk4.b])
                    S.op("dve", lambda e, bk4=bk4, jb=jb: e.tensor_tensor(out=CN.ap[:, jb, :], in0=bk4.ap[:, 0:8], in1=carry.ap, op=ALU.add), R=[carry.b], W=[bk4.b, CN.b])
                    S.op("dve", lambda e, bk4=bk4: e.tensor_tensor(out=carry.ap, in0=bk4.ap[:, 8:16], in1=carry.ap, op=ALU.add), R=[], W=[bk4.b, carry.b])
                if not own:
                    return
                nj = gsb * 4 + 4
                S.op("dve", lambda e: e.tensor_tensor(out=BI.ap[:, 0:nj, :], in0=CN.ap[:, 0:nj, :], in1=cref.ap.unsqueeze(1).to_broadcast([128, nj, 8]), op=ALU.subtract),
                     R=[CN.b, cref.b], W=[BI.b])
                S.op("dve", lambda e: e.tensor_scalar(out=BI.ap[:, 0:16, :], in0=BI.ap[:, 0:16, :], scalar1=ctxb.ap, scalar2=None, op0=ALU.add), R=[ctxb.b], W=[BI.b])

                def conv_ops():
                    for ch in range(4):
                        yield lambda ch=ch: S.op("dve", lambda e: e.tensor_scalar(out=acc.ap[:, ch, :], in0=glu.ap[:, ch, 0:512], scalar1=cw.ap[:, ch, 0:1], scalar2=cv.ap[:, 0, ch:ch + 1], op0=ALU.mult, op1=ALU.add),
                                                 R=[glu.b, cw.b, cv.b], W=[acc.b])
                        for j in range(1, 31):
                            yield lambda ch=ch, j=j: S.op("dve", lambda e: e.scalar_tensor_tensor(out=acc.ap[:, ch, :], in0=glu.ap[:, ch, j:j + 512], scalar=cw.ap[:, ch, j:j + 1], in1=acc.ap[:, ch, :], op0=ALU.mult, op1=ALU.add),
                                                          R=[glu.b, cw.b], W=[acc.b])
                cgen = conv_ops()

                def conv_some(n):
                    for _ in range(n):
                        f = next(cgen, None)
                        if f is None:
                            return
                        f()

                f0 = gsb * 4
                pcnt = [0]
                def head(h):
                    p, r0 = h // 2, (h % 2) * 64
                    ob = banks[6 + h % 2]
                    ov = ob.ap[:, 0:260].rearrange("p (a b) -> p a b", a=4)
                    jobs = []
                    for j in range(f0 + 4):
                        jobs.append((j, max(0, j - f0)))

                    def do_S(j, jj):
                        sb_ = nbank(0, 3)
                        pt = Pt[pcnt[0] % 3]
                        pcnt[0] += 1
                        c0 = jj * 128
                        S.op("pe", lambda e: e.matmul(sb_.ap[:, c0:512], lhsT=kT.ap[r0:r0 + 64, p, j * 128:(j + 1) * 128], rhs=qT.ap[r0:r0 + 64, p, c0:512], start=True, stop=True),
                             R=[kT.b, qT.b], W=[sb_.b])
                        S.op("act", lambda e: e.activation(out=pt.ap[:, c0:512], in_=sb_.ap[:, c0:512], func=AF.Exp, bias=BI.ap[:, j, h:h + 1], scale=1.0), R=[BI.b], W=[sb_.b, pt.b])
                        if j >= f0:
                            S.op("pool", lambda e: e.tensor_tensor(out=pt.ap[:, c0:c0 + 128], in0=pt.ap[:, c0:c0 + 128], in1=tri_b.ap, op=ALU.mult), R=[tri_b.b], W=[pt.b])
                        return pt

                    def do_PV(j, jj, pt, first):
                        def one(qb):
                            st_ = first and qb == 0
                            S.op("pe", lambda e: e.matmul(ov[:, qb, :], lhsT=pt.ap[:, qb * 128:(qb + 1) * 128], rhs=V.ap[:, j, h, :], start=st_, stop=False, skip_group_check=True),
                                 R=[pt.b, V.b], W=[ob.b])
                        for qb in range(jj, 4):
                            one(qb)

                    pend = []
                    for (j, jj) in jobs:
                        pt = do_S(j, jj)
                        pend.append((j, jj, pt))
                        if len(pend) > 2:
                            a = pend.pop(0)
                            do_PV(a[0], a[1], a[2], a[0] == 0)
                    for a in pend:
                        do_PV(a[0], a[1], a[2], a[0] == 0)
                    S.op("dve", lambda e: e.reciprocal(out=rd.ap, in_=ov[:, :, 64:65]), R=[], W=[ob.b, rd.b])
                    S.op("dve", lambda e: e.tensor_tensor(out=attn_tok.ap[:, :, h * 64:(h + 1) * 64], in0=ov[:, :, 0:64], in1=rd.ap.to_broadcast([128, 4, 64]), op=ALU.mult),
                         R=[rd.b], W=[ob.b, attn_tok.b])

                def conv_ln():
                    bm, be = nbank(), nbank()
                    for ch in range(4):
                        S.op("pe", lambda e, ch=ch: e.matmul(bm.ap, lhsT=o512_f.ap, rhs=acc.ap[:, ch, :], start=(ch == 0), stop=(ch == 3)), R=[o512_f.b, acc.b], W=[bm.b])
                    for ch in range(4):
                        S.op("act", lambda e, ch=ch: e.activation(out=ysq.ap, in_=acc.ap[:, ch, :], func=AF.Square), R=[acc.b], W=[ysq.b])
                        S.op("pe", lambda e, ch=ch: e.matmul(be.ap, lhsT=o512_f.ap, rhs=ysq.ap, start=(ch == 0), stop=(ch == 3)), R=[o512_f.b, ysq.b], W=[be.b])
                    S.op("act", lambda e: e.activation(out=msb.ap, in_=bm.ap, func=AF.Copy), R=[], W=[bm.b, msb.b])
                    S.op("dve", lambda e: e.tensor_tensor(out=rsd.ap, in0=msb.ap, in1=msb.ap, op=ALU.mult), R=[msb.b], W=[rsd.b])
                    S.op("dve", lambda e: e.tensor_tensor(out=rsd.ap, in0=be.ap, in1=rsd.ap, op=ALU.subtract), R=[], W=[be.b, rsd.b])
                    S.op("act", lambda e: e.activation(out=rsd.ap, in_=rsd.ap, func=AF.Sqrt, bias=epsl.ap), R=[epsl.b], W=[rsd.b])
                    S.op("dve", lambda e: e.reciprocal(out=rsd.ap, in_=rsd.ap), R=[], W=[rsd.b])
                    for ch in range(4):
                        S.op("dve", lambda e, ch=ch: e.tensor_tensor(out=acc.ap[:, ch, :], in0=acc.ap[:, ch, :], in1=msb.ap, op=ALU.subtract), R=[msb.b], W=[acc.b])
                        S.op("dve", lambda e, ch=ch: e.tensor_tensor(out=acc.ap[:, ch, :], in0=acc.ap[:, ch, :], in1=rsd.ap, op=ALU.mult), R=[rsd.b], W=[acc.b])
                        S.op("act", lambda e, ch=ch: e.activation(out=convT.ap[:, ch, :], in_=acc.ap[:, ch, :], func=AF.Silu, scale=cv.ap[:, 1, ch:ch + 1], bias=cv.ap[:, 2, ch:ch + 1]),
                             R=[acc.b, cv.b], W=[convT.b])

                for h in range(8):
                    head(h)
                    conv_some(18)
                    if h == 6:
                        conv_some(1000)
                        conv_ln()

                if sbi == 3:
                    bkc = nbank()
                    for ch in range(4):
                        S.op("pe", lambda e, ch=ch: e.transpose(out=bkc.ap[0:32, ch * 128:(ch + 1) * 128], in_=glu.ap[:, ch, 510:542], identity=ident_f.ap),
                             R=[glu.b, ident_f.b], W=[bkc.b])
                    S.op("dve", lambda e: e.tensor_copy(out=cst.ap, in_=bkc.ap[0:32, :]), R=[], W=[bkc.b, cst.b])
                    S.dma("sp", lambda e: e.dma_start(out=cs_d, in_=cst.ap), R=[cst.b])

                def tr_attn(qb):
                    bk = nbank()
                    pv = bk.ap.bitcast(BF16).rearrange("p (a b) -> p a b", a=8)
                    for cc in range(4):
                        S.op("pe", lambda e, cc=cc: e.transpose(out=pv[:, cc, :], in_=attn_tok.ap[:, qb, cc * 128:(cc + 1) * 128], identity=ident_b.ap),
                             R=[attn_tok.b, ident_b.b], W=[bk.b])
                    S.op("act", lambda e: e.activation(out=attnT.ap[:, :, qb * 128:(qb + 1) * 128], in_=pv[:, 0:4, :], func=AF.Copy), R=[], W=[bk.b, attnT.b])
                for qb in range(4):
                    tr_attn(qb)
                for qb in range(4):
                    r0 = (sbi * 4 + qb) * 128
                    xt = xr[qb % 2]
                    S.dma("act", lambda e, xt=xt, r0=r0: e.dma_start(out=xt.ap, in_=xo_d[r0:r0 + 128, :]), W=[xt.b])
                    for hf in range(2):
                        bk = nbank()
                        for c in range(8):
                            lhs = attnT.ap[:, c, qb * 128:(qb + 1) * 128] if c < 4 else convT.ap[:, c - 4, qb * 128:(qb + 1) * 128]
                            S.op("pe", lambda e, c=c, lhs=lhs, bk=bk, hf=hf: e.matmul(bk.ap, lhsT=lhs, rhs=Wo.ap[:, c, hf * 512:(hf + 1) * 512], start=(c == 0), stop=(c == 7)),
                                 R=[attnT.b, convT.b, Wo.b], W=[bk.b])
                        S.op("dve", lambda e, bk=bk, xt=xt, hf=hf: e.tensor_tensor(out=xt.ap[:, hf * 512:(hf + 1) * 512], in0=bk.ap, in1=xt.ap[:, hf * 512:(hf + 1) * 512], op=ALU.add),
                             R=[], W=[bk.b, xt.b])
                    S.dma("sp", lambda e, xt=xt, r0=r0: e.dma_start(out=xres_d[r0:r0 + 128, :], in_=xt.ap), R=[xt.b])

            for sbi in range(4):
                superblock(sbi, False)
            for sbi in range(4):
                superblock(sbi, True)

            xa = alloc("xa", [8, 2, 256])
            mix = alloc("mix", [128, D], BF16)
            S.dma("sp", lambda e: e.dma_start(out=xa.ap, in_=attn2_d.rearrange("b (par d) -> b par d", par=2)), W=[xa.b])
            S.op("pool", lambda e: e.memset(mix.ap, 0.0), W=[mix.b])
            S.op("act", lambda e: e.activation(out=mix.ap[0:8, 0:512], in_=xa.ap.rearrange("p a b -> p (a b)"), func=AF.Copy), R=[xa.b], W=[mix.b])
            S.op("act", lambda e: e.activation(out=mix.ap[0:8, 512:1024], in_=conv2s.ap[0:8, :], func=AF.Copy), R=[conv2s.b], W=[mix.b])
            bk = nbank()
            pv = bk.ap.bitcast(BF16).rearrange("p (a b) -> p a b", a=8)
            for c in range(8):
                S.op("pe", lambda e, c=c: e.transpose(out=pv[:, c, :], in_=mix.ap[:, c * 128:(c + 1) * 128], identity=ident_b.ap), R=[mix.b, ident_b.b], W=[bk.b])
            S.op("act", lambda e: e.activation(out=hTs.ap[:, :, 0:128], in_=pv, func=AF.Copy), R=[], W=[bk.b, hTs.b, hb2])
            xt = xr[0]
            S.dma("sp", lambda e: e.dma_start(out=xt.ap, in_=xs_d), W=[xt.b])
            for hf in range(2):
                def s1h(hf):
                    bk2 = nbank()
                    for c in range(8):
                        S.op("pe", lambda e, c=c: e.matmul(bk2.ap, lhsT=hTs.ap[:, c, 0:128], rhs=Wo.ap[:, c, hf * 512:(hf + 1) * 512], start=(c == 0), stop=(c == 7)),
                             R=[hTs.b, hb2, Wo.b], W=[bk2.b])
                    S.op("dve", lambda e: e.tensor_tensor(out=xt.ap[:, hf * 512:(hf + 1) * 512], in0=bk2.ap, in1=xt.ap[:, hf * 512:(hf + 1) * 512], op=ALU.add), R=[], W=[bk2.b, xt.b])
                s1h(hf)
            S.dma("sp", lambda e: e.dma_start(out=xres_d[NOWN:NOWN + 128, :], in_=xt.ap), R=[xt.b])

        SBS = [(0, 4), (4, 4), (8, 4), (12, 4), (16, 1)]

        def setup_resident():
            top[0] = base_top
            xs_ = alloc("xres_sb", [128, 17, D])
            xb = [Buf(f"x{i}") for i in range(17)]
            hTa = alloc("hT_all", [128, 8, 17 * 128], BF16)
            hb = [Buf(f"hTa{i}") for i in range(5)]
            return xs_, xb, hTa, hb

        def phase_ffn(l, from_dram, final, res):
            xs_, xb, hTa, hb = res
            mark = top[0]
            gi = 1 + 2 * l
            Wg = [alloc(f"Wg{i}", [128, 8, 768], BF16) for i in range(2)]
            Wu = [alloc(f"Wu{i}", [128, 8, 768], BF16) for i in range(2)]
            Wd = [alloc(f"Wd{i}", [128, 6, D], BF16) for i in range(2)]
            actT = [alloc(f"actT{i}", [128, 6, 512], BF16) for i in range(2)]
            sg = [alloc(f"sg{i}", [128, 512]) for i in range(2)]
            scr = norm_scratch()
            if final:
                gfin = alloc("gfin", [128, D])
                yst = [alloc("yst0", [128, D])] * 2
                S.dma("sp", lambda e: e.dma_start(out=gfin.ap, in_=gfin_d), W=[gfin.b])
            wgv = wg_d[l].rearrange("(c p) n -> p c n", p=128)
            wuv = wu_d[l].rearrange("(c p) n -> p c n", p=128)
            cnt = [0]

            def load_pass(q):
                c0, c1 = QCH[q]
                n = c1 - c0
                i = q % 2
                S.dma("pool", lambda e: e.dma_start(out=Wg[i].ap[:, :, 0:n * 128], in_=wgv[:, :, c0 * 128:c1 * 128]), W=[Wg[i].b])
                S.dma("pool", lambda e: e.dma_start(out=Wu[i].ap[:, :, 0:n * 128], in_=wuv[:, :, c0 * 128:c1 * 128]), W=[Wu[i].b])
                S.dma("pool", lambda e: e.dma_start(out=Wd[i].ap[:, 0:n, :], in_=wd_d[l][c0 * 128:c1 * 128, :].rearrange("(c p) n -> p c n", p=128)), W=[Wd[i].b])

            def norms(sbi):
                b0, nb_ = SBS[sbi]
                for b in range(b0, b0 + nb_):
                    def one(b):
                        xt = Tl(xs_.ap[:, b, :], xb[b])
                        if from_dram:
                            S.dma("sp", lambda e: e.dma_start(out=xt.ap, in_=xres_d[b * 128:(b + 1) * 128, :]), W=[xt.b])
                        norm_T(xt, gi, hTa.ap[:, :, b * 128:(b + 1) * 128], [hb[sbi]], scr)
                    one(b)

            def gate_up(q, sbi):
                b0, nb_ = SBS[sbi]
                t0, nt = b0 * 128, nb_ * 128
                n = QCH[q][1] - QCH[q][0]
                i = q % 2
                a = actT[cnt[0] % 2]

                def chunk(ci):
                    bg, bu = nbank(0, 8), nbank(0, 8)
                    for c in range(8):
                        S.op("pe", lambda e, c=c: e.matmul(bg.ap[:, 0:nt], lhsT=Wg[i].ap[:, c, ci * 128:(ci + 1) * 128], rhs=hTa.ap[:, c, t0:t0 + nt], start=(c == 0), stop=(c == 7)),
                             R=[Wg[i].b, hb[sbi]], W=[bg.b])
                    for c in range(8):
                        S.op("pe", lambda e, c=c: e.matmul(bu.ap[:, 0:nt], lhsT=Wu[i].ap[:, c, ci * 128:(ci + 1) * 128], rhs=hTa.ap[:, c, t0:t0 + nt], start=(c == 0), stop=(c == 7)),
                             R=[Wu[i].b, hb[sbi]], W=[bu.b])
                    s_ = sg[ci % 2]
                    S.op("act", lambda e: e.activation(out=s_.ap[:, 0:nt], in_=bg.ap[:, 0:nt], func=AF.Silu), R=[], W=[bg.b, s_.b])
                    S.op("dve", lambda e: e.tensor_tensor(out=a.ap[:, ci, 0:nt], in0=bu.ap[:, 0:nt], in1=s_.ap[:, 0:nt], op=ALU.mult), R=[s_.b], W=[bu.b, a.b])
                for ci in range(n):
                    chunk(ci)
                cnt[0] += 1
                return a

            def down(q, sbi, a):
                b0, nb_ = SBS[sbi]
                n = QCH[q][1] - QCH[q][0]
                i = q % 2

                def blk(bl):
                    b = b0 + bl
                    for hf in range(2):
                        def half(hf):
                            bk = nbank(0, 8)
                            for ci in range(n):
                                S.op("pe", lambda e, ci=ci: e.matmul(bk.ap, lhsT=a.ap[:, ci, bl * 128:(bl + 1) * 128], rhs=Wd[i].ap[:, ci, hf * 512:(hf + 1) * 512], start=(ci == 0), stop=(ci == n - 1)),
                                     R=[a.b, Wd[i].b], W=[bk.b])
                            S.op("dve", lambda e: e.tensor_tensor(out=xs_.ap[:, b, hf * 512:(hf + 1) * 512], in0=bk.ap, in1=xs_.ap[:, b, hf * 512:(hf + 1) * 512], op=ALU.add),
                                 R=[], W=[bk.b, xb[b]])
                        half(hf)
                    if final and q == 3:
                        sq, ss, rstd = scr[b % 2]
                        yt = yst[b % 2]
                        xap = xs_.ap[:, b, :]
                        S.op("act", lambda e: e.activation(out=sq.ap, in_=xap, func=AF.Square, accum_out=ss.ap), R=[xb[b]], W=[sq.b, ss.b])
                        S.op("act", lambda e: e.activation(out=rstd.ap, in_=ss.ap, func=AF.Sqrt, scale=1.0 / D, bias=epsr.ap), R=[ss.b, epsr.b], W=[rstd.b])
                        S.op("dve", lambda e: e.reciprocal(out=rstd.ap, in_=rstd.ap), R=[rstd.b], W=[rstd.b])
                        S.op("act", lambda e: e.activation(out=yt.ap, in_=xap, func=AF.Copy, scale=rstd.ap), R=[xb[b], rstd.b], W=[yt.b])
                        S.op("dve", lambda e: e.tensor_tensor(out=yt.ap, in0=yt.ap, in1=gfin.ap, op=ALU.mult), R=[gfin.b], W=[yt.b])
                        S.dma("sp", lambda e: e.dma_start(out=y_d[b * 128:(b + 1) * 128, :], in_=yt.ap), R=[yt.b])
                for bl in range(nb_):
                    blk(bl)

            load_pass(0)
            for q in range(4):
                if q + 1 < 4:
                    load_pass(q + 1)
                prev = None
                if q == 0:
                    norms(0)
                for sbi in range(5):
                    a = gate_up(q, sbi)
                    if q == 0 and sbi + 1 < 5:
                        norms(sbi + 1)
                    if prev is not None:
                        down(q, prev[0], prev[1])
                    prev = (sbi, a)
                down(q, prev[0], prev[1])
            top[0] = mark

        def phase_sgu(res):
            xs_, xb, hTa, hb = res
            mark = top[0]
            Wi1 = alloc("Wi1", [128, 8, 2 * D], BF16)
            Wo1 = alloc("Wo1", [128, 8, D], BF16)
            swt = alloc("swt", [128, 8, 128])
            WcT = alloc("WcT", [128, 8, 128], BF16)
            WcTs = alloc("WcTs", [128, 8, 128], BF16)
            BS = alloc("BS", [128, 8, 128])
            sw0 = alloc("sw0", [128, 8])
            sb0 = alloc("sb0", [128, 8])
            sln = alloc("sln", [128, 2, D])
            hTs = Tl(hTa.ap[:, :, 0:512], Buf("hTs1"))
            uT = Tl(hTa.ap[:, :, 512:1024], Buf("uT"))
            vts = [alloc(f"vt{i}", [128, D]) for i in range(2)]
            vnfs = [alloc(f"vnf{i}", [128, D]) for i in range(2)]
            vns = [alloc(f"vn{i}", [128, D], BF16) for i in range(2)]
            gated = alloc("gated", [128, 8, 128], BF16)
            tmp = alloc("tmpm", [128, 4, 128])
            bsts = [alloc(f"bst{i}", [128, 2, 6]) for i in range(2)]
            mvs = [alloc(f"mv{i}", [128, 2]) for i in range(2)]
            rs1s = [alloc(f"rs1{i}", [128, 1]) for i in range(2)]
            bcnt = [0]
            scr = norm_scratch()
            wv = wio_d.rearrange("(c p) n -> p c n", p=128)
            for a_ in range(0, 2048, 1024):
                S.dma("pool", lambda e, a_=a_: e.dma_start(out=Wi1.ap[:, :, a_:a_ + 1024], in_=wv[:, :, a_:a_ + 1024]), W=[Wi1.b])
            load_w(Wo1, woo_d.rearrange("(c p) n -> p c n", p=128))
            S.dma("sp", lambda e: e.dma_start(out=swt.ap, in_=sw_d.rearrange("g t s -> t g s")), W=[swt.b])
            S.dma("sp", lambda e: e.dma_start(out=BS.ap, in_=sb_d), W=[BS.b])
            S.dma("sp", lambda e: e.dma_start(out=sw0.ap, in_=sw0_d), W=[sw0.b])
            S.dma("sp", lambda e: e.dma_start(out=sb0.ap, in_=sb0_d), W=[sb0.b])
            S.dma("sp", lambda e: e.dma_start(out=sln.ap, in_=sln_d), W=[sln.b])
            S.op("pool", lambda e: e.affine_select(out=swt.ap, in_=swt.ap, pattern=[[0, 8], [-1, 128]], compare_op=ALU.is_ge, fill=0.0, base=0, channel_multiplier=1),
                 R=[], W=[swt.b])
            for g in range(8):
                def one(g):
                    bk = nbank(0, 8)
                    S.op("pe", lambda e: e.transpose(out=bk.ap[:, 0:128], in_=swt.ap[:, g, :], identity=ident_f.ap), R=[swt.b, ident_f.b], W=[bk.b])
                    S.op("act", lambda e: e.activation(out=WcT.ap[:, g, :], in_=bk.ap[:, 0:128], func=AF.Copy), R=[], W=[bk.b, WcT.b])
                    S.op("dve", lambda e: e.tensor_scalar(out=WcTs.ap[:, g, :], in0=ident_f.ap, scalar1=sw0.ap[:, g:g + 1], scalar2=None, op0=ALU.mult), R=[ident_f.b, sw0.b], W=[WcTs.b])
                one(g)

            def superblock(sbi):
                b0, nb_ = SBS[sbi]
                nt = nb_ * 128
                samp = (sbi == 4)
                W_ = WcTs if samp else WcT
                for bl in range(nb_):
                    def nb1(bl):
                        b = b0 + bl
                        norm_T(Tl(xs_.ap[:, b, :], xb[b]), 2, hTs.ap[:, :, bl * 128:(bl + 1) * 128], [hTs.b], scr)
                    nb1(bl)
                for ch in range(8):
                    def uch(ch):
                        bk = nbank(0, 8)
                        for c in range(8):
                            S.op("pe", lambda e, c=c: e.matmul(bk.ap[:, 0:nt], lhsT=Wi1.ap[:, c, ch * 128:(ch + 1) * 128], rhs=hTs.ap[:, c, 0:nt], start=(c == 0), stop=(c == 7)),
                                 R=[Wi1.b, hTs.b], W=[bk.b])
                        S.op("act", lambda e: e.activation(out=uT.ap[:, ch, 0:nt], in_=bk.ap[:, 0:nt], func=AF.Gelu), R=[], W=[bk.b, uT.b])
                    uch(ch)

                def stage1(bl):
                    i = bcnt[0] % 2
                    bcnt[0] += 1
                    vt, vnf, vn, bst, mv, rs1 = vts[i], vnfs[i], vns[i], bsts[i], mvs[i], rs1s[i]
                    for hf in range(2):
                        def vh(hf):
                            bk = nbank(0, 8)
                            for c in range(8):
                                S.op("pe", lambda e, c=c: e.matmul(bk.ap, lhsT=hTs.ap[:, c, bl * 128:(bl + 1) * 128], rhs=Wi1.ap[:, c, D + hf * 512:D + (hf + 1) * 512], start=(c == 0), stop=(c == 7)),
                                     R=[Wi1.b, hTs.b], W=[bk.b])
                            S.op("act", lambda e: e.activation(out=vt.ap[:, hf * 512:(hf + 1) * 512], in_=bk.ap, func=AF.Gelu), R=[], W=[bk.b, vt.b])
                            S.op("dve", lambda e: e.bn_stats(out=bst.ap[:, hf, :], in_=vt.ap[:, hf * 512:(hf + 1) * 512]), R=[vt.b], W=[bst.b])
                        vh(hf)
                    S.op("dve", lambda e: e.bn_aggr(out=mv.ap, in_=bst.ap), R=[bst.b], W=[mv.b])
                    S.op("act", lambda e: e.activation(out=rs1.ap, in_=mv.ap[:, 1:2], func=AF.Sqrt, bias=epsl.ap), R=[mv.b, epsl.b], W=[rs1.b])
                    S.op("dve", lambda e: e.reciprocal(out=rs1.ap, in_=rs1.ap), R=[], W=[rs1.b])
                    S.op("dve", lambda e: e.tensor_scalar(out=vnf.ap, in0=vt.ap, scalar1=mv.ap[:, 0:1], scalar2=rs1.ap, op0=ALU.subtract, op1=ALU.mult), R=[vt.b, mv.b, rs1.b], W=[vnf.b])
                    S.op("dve", lambda e: e.tensor_tensor(out=vnf.ap, in0=vnf.ap, in1=sln.ap[:, 0, :], op=ALU.mult), R=[sln.b], W=[vnf.b])
                    S.op("dve", lambda e: e.tensor_tensor(out=vnf.ap, in0=vnf.ap, in1=sln.ap[:, 1, :], op=ALU.add), R=[sln.b], W=[vnf.b])
                    S.op("act", lambda e: e.activation(out=vn.ap, in_=vnf.ap, func=AF.Copy), R=[vnf.b], W=[vn.b])
                    if samp:
                        S.dma("sp", lambda e: e.dma_start(out=sv_d, in_=vnf.ap[0:8, :]), R=[vnf.b])
                    return vn

                def stage2(bl, vn):
                    b = b0 + bl
                    for hh in range(2):
                        def mix(hh):
                            bk = nbank(0, 8)
                            bv = bk.ap.rearrange("p (a b) -> p a b", a=4)
                            for g4 in range(4):
                                g = hh * 4 + g4
                                S.op("pe", lambda e, g=g, g4=g4: e.matmul(bv[:, g4, :], lhsT=vn.ap[:, g * 128:(g + 1) * 128], rhs=W_.ap[:, g, :], start=(g4 == 0), stop=False, skip_group_check=True),
                                     R=[vn.b, W_.b], W=[bk.b])
                            if samp:
                                bias_ap = sb0.ap[:, hh * 4:hh * 4 + 4].unsqueeze(2).to_broadcast([128, 4, 128])
                                bias_b = sb0.b
                            else:
                                bias_ap = BS.ap[:, hh * 4:hh * 4 + 4, :]
                                bias_b = BS.b
                            S.op("dve", lambda e: e.tensor_tensor(out=tmp.ap, in0=bv, in1=bias_ap, op=ALU.add), R=[bias_b], W=[bk.b, tmp.b])
                            S.op("dve", lambda e: e.tensor_tensor(out=gated.ap[:, hh * 4:hh * 4 + 4, :], in0=tmp.ap, in1=uT.ap[:, hh * 4:hh * 4 + 4, bl * 128:(bl + 1) * 128], op=ALU.mult),
                                 R=[tmp.b, uT.b], W=[gated.b])
                        mix(hh)
                    for hf in range(2):
                        def oh(hf):
                            bk = nbank(0, 8)
                            for c in range(8):
                                S.op("pe", lambda e, c=c: e.matmul(bk.ap, lhsT=gated.ap[:, c, :], rhs=Wo1.ap[:, c, hf * 512:(hf + 1) * 512], start=(c == 0), stop=(c == 7)),
                                     R=[gated.b, Wo1.b], W=[bk.b])
                            S.op("dve", lambda e: e.tensor_tensor(out=xs_.ap[:, b, hf * 512:(hf + 1) * 512], in0=bk.ap, in1=xs_.ap[:, b, hf * 512:(hf + 1) * 512], op=ALU.add),
                                 R=[], W=[bk.b, xb[b]])
                        oh(hf)

                vn_next = stage1(0)
                for bl in range(nb_):
                    vn_cur = vn_next
                    if bl + 1 < nb_:
                        vn_next = stage1(bl + 1)
                    stage2(bl, vn_cur)

            for sbi in range(5):
                superblock(sbi)
            top[0] = mark

        def phase_attn():
            top[0] = base_top
            q1 = alloc("q1", [32, 64])
            k1 = alloc("k1", [32, 64])
            v1 = alloc("v1", [32, 64])
            lf1 = alloc("lf1", [32, 1])
            ptT = alloc("ptT", [128, 32], I32)
            oo = alloc("oo", [32, 64])
            keep = top[0]
            Wc = alloc("Wc", [128, 8, 196], BF16)
            xs_t = alloc("xs_t", [128, D])
            hT = alloc("hT_s", [128, 8, 128], BF16)
            lz = alloc("lzs", [128, 1])
            bfc = alloc("bfc", [128, 1])
            scr = norm_scratch()
            S.dma("pool", lambda e: e.dma_start(out=Wc.ap, in_=wsm_d.rearrange("(c p) n -> p c n", p=128)), W=[Wc.b])
            S.dma("sp", lambda e: e.dma_start(out=xs_t.ap, in_=xs_d), W=[xs_t.b])
            S.dma("sp", lambda e: e.dma_start(out=bfc.ap, in_=bfc_d), W=[bfc.b])
            S.dma("sp", lambda e: e.dma_start(out=ptT.ap, in_=ptT_d), W=[ptT.b])
            norm_T(xs_t, 0, hT.ap, [hT.b], scr)
            bk = nbank(0, 8)
            for c in range(8):
                S.op("pe", lambda e, c=c: e.matmul(bk.ap[:, 0:196], lhsT=hT.ap[:, c, :], rhs=Wc.ap[:, c, :], start=(c == 0), stop=(c == 7)), R=[hT.b, Wc.b], W=[bk.b])
            S.op("act", lambda e: e.activation(out=q1.ap, in_=bk.ap[0:32, 0:64], func=AF.Copy, scale=0.125), R=[], W=[bk.b, q1.b])
            S.op("act", lambda e: e.activation(out=k1.ap, in_=bk.ap[0:32, 64:128], func=AF.Copy), R=[], W=[bk.b, k1.b])
            S.op("act", lambda e: e.activation(out=v1.ap, in_=bk.ap[0:32, 128:192], func=AF.Copy), R=[], W=[bk.b, v1.b])
            S.op("dve", lambda e: e.tensor_tensor(out=lz.ap, in0=bk.ap[:, 192:193], in1=bfc.ap, op=ALU.add), R=[bfc.b], W=[lz.b, bk.b])
            S.op("act", lambda e: e.activation(out=lz.ap, in_=lz.ap, func=AF.Exp, scale=-1.0), R=[], W=[lz.b])
            S.op("act", lambda e: e.activation(out=lz.ap, in_=lz.ap, func=AF.Ln, bias=one1.ap), R=[one1.b], W=[lz.b])
            S.op("dve", lambda e: e.tensor_scalar(out=lf1.ap, in0=lz.ap[0:32, :], scalar1=-1.0, scalar2=None, op0=ALU.mult), R=[lz.b], W=[lf1.b])
            S.barrier()
            top[0] = keep
            sample_attention(32, 1, q1, k1, v1, lf1, ptT, kc_d, vc_d, lc_d, oo)
            S.dma("sp", lambda e: e.dma_start(out=oa_d, in_=oo.ap), R=[oo.b])

        if not MAIN:
            phase_attn()
            S.emit(sems, dsems, block)
            return nc
        phase_S0()
        S.barrier()
        top[0] = base_top
        phase_A()
        if nphase <= 1:
            top[0] = base_top
            t = alloc("dbg", [128, D])
            for i in range(NB):
                S.dma("sp", lambda e, i=i: e.dma_start(out=t.ap, in_=xres_d[i * 128:(i + 1) * 128, :]), W=[t.b])
                S.dma("sp", lambda e, i=i: e.dma_start(out=y_d[i * 128:(i + 1) * 128, :], in_=t.ap), R=[t.b])
        else:
            S.barrier()
            limit[0] = AW
            res = setup_resident()
            phase_ffn(0, True, False, res)
            S.barrier()
            phase_sgu(res)
            S.barrier()
            phase_ffn(1, False, True, res)
        S.emit(sems, dsems, block)
    return nc


def make_in_maps(inp, attn2):
    f = np.float32
    xp = np.asarray(inp["x_prompt"], f)
    xs = np.zeros((128, D), f)
    xs[:32] = np.asarray(inp["x_sample"], f)[:, 0, :]

    def fm(v):
        return np.ascontiguousarray(np.asarray(v, f).reshape(8, 128).T)

    def bc(v):
        v = np.asarray(v, f)
        return np.ascontiguousarray(np.broadcast_to(v[None], (128,) + v.shape))

    gains = np.stack([fm(inp["norm_mix"][0]), fm(inp["norm_ffn"][0]), fm(inp["norm_mix"][1]), fm(inp["norm_ffn"][1])], axis=1)
    cw = np.ascontiguousarray(np.asarray(inp["conv_w"], f)[0].T.reshape(4, 128, 31).transpose(1, 0, 2))

    def c4(v):
        return np.asarray(v, f)[0].reshape(4, 128).T
    cv = np.ascontiguousarray(np.stack([c4(inp["conv_b"]), c4(inp["conv_ln_g"]), c4(inp["conv_ln_b"])], axis=1))
    common = {
        "xs": xs,
        "gains": np.ascontiguousarray(gains),
        "gfin": bc(inp["norm_final"]),
        "w_in_even": np.asarray(inp["w_in_even"], f)[0],
        "b_forget": bc(np.asarray(inp["b_forget"], f)[0]),
        "conv_w": cw,
        "conv_vec": cv,
        "w_out_even": np.asarray(inp["w_out_even"], f)[0],
        "w_in_odd": np.asarray(inp["w_in_odd"], f)[0],
        "sgu_ln": bc(np.stack([np.asarray(inp["sgu_ln_g"], f)[0], np.asarray(inp["sgu_ln_b"], f)[0]])),
        "sgu_w": np.asarray(inp["sgu_w"], f)[0],
        "sgu_b": bc(np.asarray(inp["sgu_b"], f)[0]),
        "sgu_w0": bc(np.asarray(inp["sgu_w"], f)[0][:, 0, 0]),
        "sgu_b0": bc(np.asarray(inp["sgu_b"], f)[0][:, 0]),
        "w_out_odd": np.asarray(inp["w_out_odd"], f)[0],
        "w_gate": np.asarray(inp["w_gate"], f),
        "w_up": np.asarray(inp["w_up"], f),
        "w_down": np.asarray(inp["w_down"], f),
    }
    del common["xs"]
    xsa = np.asarray(inp["x_sample"], f)[:, 0, :]
    stc = np.asarray(inp["state_conv"], f)[0]
    cwf = np.asarray(inp["conv_w"], f)[0]
    cvf = np.stack([np.asarray(inp["conv_b"], f)[0], np.asarray(inp["conv_ln_g"], f)[0], np.asarray(inp["conv_ln_b"], f)[0]])
    common["conv_wb"] = np.ascontiguousarray(np.broadcast_to(cwf[None], (8, 31, 512)))
    common["conv_vb"] = np.ascontiguousarray(np.broadcast_to(cvf[None], (8, 3, 512)))
    maps = []
    for c in range(8):
        b, hf = c // 2, c % 2
        m = dict(common)
        m["xo"] = np.ascontiguousarray(xp[b, hf * NOWN:(hf + 1) * NOWN])
        m["xc"] = np.ascontiguousarray(xp[b, 0:NOWN]) if hf == 1 else np.zeros((NOWN, D), f)
        m["ctxb"] = np.full((128, 1), 0.0 if hf == 1 else NEG, f)
        g = c // 2
        xs = np.zeros((128, D), f)
        xs[:8] = xsa[8 * g:8 * g + 8]
        m["xs"] = xs
        m["state_c"] = np.ascontiguousarray(stc[8 * g:8 * g + 8].reshape(8, 30 * 512))
        m["attn2"] = np.ascontiguousarray(attn2[8 * g:8 * g + 8])
        maps.append(m)
    return maps


def make_attn_maps(inp):
    f = np.float32
    xs = np.zeros((128, D), f)
    xs[:32] = np.asarray(inp["x_sample"], f)[:, 0, :]
    g0 = np.ascontiguousarray(np.asarray(inp["norm_mix"], f)[0].reshape(8, 128).T)
    gains = np.ascontiguousarray(np.stack([g0, g0, g0, g0], axis=1))
    wie = np.asarray(inp["w_in_even"], f)[0]
    ck = np.asarray(inp["cache_k"], f)[0]
    cvv = np.asarray(inp["cache_v"], f)[0]
    cl = np.asarray(inp["cache_logf"], f)[0]
    ptT = np.ascontiguousarray(np.asarray(inp["page_table"]).astype(np.int32).T)
    bfv = np.asarray(inp["b_forget"], f)[0]
    maps = []
    for h in range(8):
        w = np.zeros((D, 196), f)
        w[:, 0:64] = wie[:, h * 64:(h + 1) * 64]
        w[:, 64:128] = wie[:, 512 + h * 64:512 + (h + 1) * 64]
        w[:, 128:192] = wie[:, 1024 + h * 64:1024 + (h + 1) * 64]
        w[:, 192] = wie[:, 1536 + h]
        maps.append({
            "xs": xs, "gains": gains, "w_samp": w,
            "bf_c": np.full((128, 1), bfv[h], f),
            "ptT": ptT,
            "kc0": np.ascontiguousarray(ck[:, :, h, :]).reshape(5120, 8192),
            "vc0": np.ascontiguousarray(cvv[:, :, h, :]).reshape(5120, 8192),
            "lc0": np.ascontiguousarray(cl[:, :, h]),
        })
    return maps


_NC_CACHE = {}


def _prog(kind):
    if kind not in _NC_CACHE:
        _NC_CACHE[kind] = build(kind)
    return _NC_CACHE[kind]


def run(inp):
    r1 = run_bass_kernel_spmd(_prog("attn"), make_attn_maps(inp), core_ids=list(range(8))).results
    attn2 = np.ascontiguousarray(np.stack([r1[h]["o_attn"] for h in range(8)], axis=1)).reshape(32, 512)
    return run_bass_kernel_spmd(_prog("main"), make_in_maps(inp, attn2), core_ids=list(range(8))).results


def assemble(res):
    f = np.float32
    y_p = np.zeros((4, 4096, D), f)
    k_p = np.zeros((1, 4, 4096, 8, 64), f)
    v_p = np.zeros((1, 4, 4096, 8, 64), f)
    lf_p = np.zeros((1, 4, 4096, 8), f)
    cs_p = np.zeros((1, 4, 30, 512), f)
    for c in range(8):
        b, hf = c // 2, c % 2
        sl = slice(hf * NOWN, (hf + 1) * NOWN)
        r = res[c]
        y_p[b, sl] = r["y"][:NOWN]
        k_p[0, b, sl] = r["k_out"][:NOWN].reshape(NOWN, 8, 64)
        v_p[0, b, sl] = r["v_out"][:NOWN].reshape(NOWN, 8, 64)
        lf_p[0, b, sl] = r["lf_out"][:NOWN]
        if hf == 1:
            cs_p[0, b] = r["cs_out"][2:32]
    y_s = np.zeros((32, 1, D), f)
    k_s = np.zeros((1, 32, 1, 8, 64), f)
    v_s = np.zeros((1, 32, 1, 8, 64), f)
    lf_s = np.zeros((1, 32, 1, 8), f)
    cs_s = np.zeros((1, 32, 30, 512), f)
    sv_s = np.zeros((1, 32, 1, D), f)
    for g in range(4):
        r = res[2 * g]
        sl = slice(8 * g, 8 * g + 8)
        y_s[sl, 0] = r["y"][NOWN:NOWN + 8]
        k_s[0, sl, 0] = r["k_out"][NOWN:NOWN + 8].reshape(8, 8, 64)
        v_s[0, sl, 0] = r["v_out"][NOWN:NOWN + 8].reshape(8, 8, 64)
        lf_s[0, sl, 0] = r["lf_out"][NOWN:NOWN + 8]
        cs_s[0, sl] = r["css_out"]
        sv_s[0, sl, 0] = r["sv_out"]
    return (y_p, y_s, k_p, v_p, lf_p, cs_p, k_s, v_s, lf_s, cs_s, sv_s)


def kernel(**inp):
    return assemble(run(inp))
```
